# Optimizing a Trainium2 kernel written in Bass

```python
import jax, jax.numpy as jnp
from jax import lax
import numpy as np

D_MODEL = 1024
BATCH = 2
SEQ = 8192
DEPTH = 2

PLE_DIM = 256
N_MIXERS = 2
N_FOX_LAYERS = (DEPTH + 1) // 2
N_DIL_LAYERS = DEPTH // 2

FOX_HEADS = 16
FOX_HEAD_DIM = D_MODEL // FOX_HEADS
FOX_WIDTH = FOX_HEADS * FOX_HEAD_DIM
FOX_IN = 4 * FOX_WIDTH + FOX_HEADS
QUERY_BLOCK = 128
FORGET_BIAS_CENTER = 2.0

DIL_PATTERN = ((128, 1), (512, 4), (2048, 16))
DIL_GROUPS = len(DIL_PATTERN)
DIL_HEADS_PER_GROUP = 8
DIL_HEAD_DIM = D_MODEL // DIL_HEADS_PER_GROUP
DIL_HEADS = DIL_GROUPS * DIL_HEADS_PER_GROUP
DIL_QKV = DIL_HEADS * DIL_HEAD_DIM
DIL_WIDTH = DIL_HEADS_PER_GROUP * DIL_HEAD_DIM
DIL_IN = 3 * DIL_QKV + DIL_WIDTH
ALIBI_MAX_EXP = 8.0

RMS_EPS = 1e-6

kernel_name = "fox_dilated_hybrid_trunk"


def rms_norm(x, g):
    xf = x.astype(jnp.float32)
    y = xf * lax.rsqrt(jnp.mean(xf * xf, axis=-1, keepdims=True) + RMS_EPS)
    return (y * g.astype(jnp.float32)).astype(x.dtype)


def alibi_slopes(n):
    return 2.0 ** (-ALIBI_MAX_EXP * jnp.arange(1, n + 1, dtype=jnp.float32) / n)


def fox_mixer(h, w_in, b_f, w_out):
    B, S, _ = h.shape
    proj = h @ w_in
    q = proj[..., :FOX_WIDTH].reshape(B, S, FOX_HEADS, FOX_HEAD_DIM)
    k = proj[..., FOX_WIDTH:2 * FOX_WIDTH].reshape(B, S, FOX_HEADS, FOX_HEAD_DIM)
    v = proj[..., 2 * FOX_WIDTH:3 * FOX_WIDTH].reshape(B, S, FOX_HEADS, FOX_HEAD_DIM)
    z = proj[..., 3 * FOX_WIDTH:4 * FOX_WIDTH]
    f_logit = proj[..., 4 * FOX_WIDTH:]
    log_f = jax.nn.log_sigmoid((f_logit + b_f).astype(jnp.float32))
    c = jnp.cumsum(log_f, axis=1)
    nb = S // QUERY_BLOCK
    qb = q.reshape(B, nb, QUERY_BLOCK, FOX_HEADS, FOX_HEAD_DIM).transpose(1, 0, 2, 3, 4)
    cqb = c.reshape(B, nb, QUERY_BLOCK, FOX_HEADS).transpose(1, 0, 3, 2)
    qpos = jnp.arange(S).reshape(nb, QUERY_BLOCK)
    kpos = jnp.arange(S)
    ck = c.transpose(0, 2, 1)
    scale = FOX_HEAD_DIM ** -0.5

    def block(args):
        qi, ci, pi = args
        s = jnp.einsum('bqhd,bkhd->bhqk', qi, k).astype(jnp.float32) * scale
        s = s + ci[..., None] - ck[:, :, None, :]
        s = jnp.where((kpos[None, :] <= pi[:, None])[None, None], s, -jnp.inf)
        pr = jax.nn.softmax(s, axis=-1)
        return jnp.einsum('bhqk,bkhd->bqhd', pr.astype(v.dtype), v)

    o = lax.map(block, (qb, cqb, qpos))
    o = o.transpose(1, 0, 2, 3, 4).reshape(B, S, FOX_WIDTH)
    return (o * jax.nn.silu(z)) @ w_out


def dilated_window_attention(q, k, v, slopes, window, dilation):
    B, S, Hg, hd = q.shape
    L = S // dilation
    nW = window // dilation
    nb = -(-L // nW)
    Lp = nb * nW
    Bd = B * dilation

    def to_blocks(t):
        t = t.reshape(B, L, dilation, Hg, hd).transpose(0, 2, 1, 3, 4).reshape(Bd, L, Hg, hd)
        t = jnp.pad(t, ((0, 0), (0, Lp - L), (0, 0), (0, 0)))
        return t.reshape(Bd, nb, nW, Hg, hd)

    def with_prev(t):
        prev = jnp.pad(t, ((0, 0), (1, 0), (0, 0), (0, 0), (0, 0)))[:, :-1]
        return jnp.concatenate([prev, t], axis=2)

    qb = to_blocks(q)
    kk = with_prev(to_blocks(k))
    vv = with_prev(to_blocks(v))
    i = jnp.arange(nW)[:, None]
    j = jnp.arange(2 * nW)[None, :]
    dist = nW + i - j
    key_pos = (jnp.arange(nb)[:, None, None] - 1) * nW + j[None]
    valid = (dist >= 0)[None] & (dist <= nW)[None] & (key_pos >= 0)
    bias = -slopes.astype(jnp.float32)[:, None, None] * (dist * dilation).astype(jnp.float32)
    scale = hd ** -0.5
    s = jnp.einsum('znqhd,znkhd->znhqk', qb, kk).astype(jnp.float32) * scale + bias[None, None]
    s = jnp.where(valid[None, :, None], s, -jnp.inf)
    lse = jax.nn.logsumexp(s, axis=-1, keepdims=True)
    pr = jnp.exp(s - lse)
    o = jnp.einsum('znhqk,znkhd->znqhd', pr.astype(v.dtype), vv)
    o = o.reshape(Bd, Lp, Hg, hd)[:, :L]
    o = o.reshape(B, dilation, L, Hg, hd).transpose(0, 2, 1, 3, 4).reshape(B, S, Hg, hd)
    lse = lse[..., 0].transpose(0, 1, 3, 2).reshape(Bd, Lp, Hg)[:, :L]
    lse = lse.reshape(B, dilation, L, Hg).transpose(0, 2, 1, 3).reshape(B, S, Hg)
    return o, lse


def dilated_mixer(h, w_in, w_out):
    B, S, _ = h.shape
    proj = h @ w_in
    shp = (B, S, DIL_GROUPS, DIL_HEADS_PER_GROUP, DIL_HEAD_DIM)
    q = proj[..., :DIL_QKV].reshape(shp)
    k = proj[..., DIL_QKV:2 * DIL_QKV].reshape(shp)
    v = proj[..., 2 * DIL_QKV:3 * DIL_QKV].reshape(shp)
    z = proj[..., 3 * DIL_QKV:]
    slopes = alibi_slopes(DIL_HEADS).reshape(DIL_GROUPS, DIL_HEADS_PER_GROUP)
    outs, lses = [], []
    for g, (window, dilation) in enumerate(DIL_PATTERN):
        o, l = dilated_window_attention(q[:, :, g], k[:, :, g], v[:, :, g], slopes[g], window, dilation)
        outs.append(o)
        lses.append(l)
    wts = jax.nn.softmax(jnp.stack(lses), axis=0)
    o = jnp.sum(wts[..., None] * jnp.stack(outs).astype(jnp.float32), axis=0)
    o = o.astype(h.dtype).reshape(B, S, DIL_WIDTH)
    return (o * jax.nn.silu(z)) @ w_out


def setup_inputs(seed: int = 0) -> dict:
    key = jax.random.key(seed)
    ks = jax.random.split(key, 14)
    f32 = jnp.float32
    nrm = lambda k, shape, fan_in: jax.random.normal(k, shape, f32) * fan_in ** -0.5
    return {
        "x": jax.random.normal(ks[0], (BATCH, SEQ, D_MODEL), f32),
        "p": jax.random.normal(ks[1], (DEPTH, BATCH, SEQ, PLE_DIM), f32),
        "fox_norm": 1.0 + 0.02 * jax.random.normal(ks[2], (N_FOX_LAYERS, D_MODEL), f32),
        "fox_w_in": nrm(ks[3], (N_FOX_LAYERS, D_MODEL, FOX_IN), D_MODEL),
        "fox_b_f": FORGET_BIAS_CENTER + 0.5 * jax.random.normal(ks[4], (N_FOX_LAYERS, FOX_HEADS), f32),
        "fox_w_out": nrm(ks[5], (N_FOX_LAYERS, FOX_WIDTH, D_MODEL), FOX_WIDTH),
        "dil_norm": 1.0 + 0.02 * jax.random.normal(ks[6], (N_DIL_LAYERS, D_MODEL), f32),
        "dil_w_in": nrm(ks[7], (N_DIL_LAYERS, D_MODEL, DIL_IN), D_MODEL),
        "dil_w_out": nrm(ks[8], (N_DIL_LAYERS, DIL_WIDTH, D_MODEL), DIL_WIDTH),
        "ple_w_up": nrm(ks[9], (DEPTH, PLE_DIM, D_MODEL), PLE_DIM),
        "ple_w_gate": nrm(ks[10], (DEPTH, D_MODEL, D_MODEL), D_MODEL),
        "final_norm": 1.0 + 0.02 * jax.random.normal(ks[11], (D_MODEL,), f32),
    }


def reference(x, p, fox_norm, fox_w_in, fox_b_f, fox_w_out, dil_norm, dil_w_in, dil_w_out,
              ple_w_up, ple_w_gate, final_norm):
    h = x
    for i in range(DEPTH):
        j = i // N_MIXERS
        if i % N_MIXERS == 0:
            h = h + fox_mixer(rms_norm(h, fox_norm[j]), fox_w_in[j], fox_b_f[j], fox_w_out[j])
        else:
            h = h + dilated_mixer(rms_norm(h, dil_norm[j]), dil_w_in[j], dil_w_out[j])
        h = h + (p[i] @ ple_w_up[i]) * jax.nn.sigmoid(h @ ple_w_gate[i])
    return rms_norm(h, final_norm)
```

```python
from contextlib import ExitStack
import numpy as np
import ml_dtypes
import concourse.bass as bass
import concourse.mybir as mybir
from concourse.bass_utils import run_bass_kernel_spmd

F32 = mybir.dt.float32
BF16 = mybir.dt.bfloat16
AF = mybir.ActivationFunctionType
ALU = mybir.AluOpType
AX = mybir.AxisListType

NCORES = 8
S = 8192
D = 1024
NTL = 2048
NLB = 16
EPS = 1e-6
NEG = -30000.0


class _Op:
    __slots__ = ("eng", "fn", "deps", "inc", "is_dma", "sem", "val", "idx", "grp")


class Prog:
    CE = ("pe", "act", "dve", "pool")

    def __init__(self, nc):
        self.nc = nc
        self.esem = {e: nc.alloc_semaphore("es_" + e) for e in self.CE}
        self.ecnt = {e: 0 for e in self.CE}
        self.waited = {}
        self.dsem = {}
        self.ops = []
        self.tw = {}
        self.tr = {}
        self.alias = {}
        self.marks = {}
        self.no_pool_cast = False
        self.n_total = 0

    @staticmethod
    def _need(x_eng, x_dma, w):
        return w.is_dma or x_dma or w.eng != x_eng

    def op(self, eng, fn, reads=(), writes=(), dma=None, inc=16):
        if self.alias:
            ex = lambda ks: [kk for k in ks for kk in self.alias.get(k, [k])]
            reads, writes = ex(reads), ex(writes)
        x = _Op()
        x.eng = eng
        x.fn = fn
        x.is_dma = dma is not None
        x.grp = dma
        x.inc = False
        x.idx = len(self.ops)
        deps = set()
        for r in reads:
            w = self.tw.get(r)
            if w is not None and (self._need(eng, x.is_dma, w) or eng != "pe"):
                deps.add(w)
        for k in writes:
            w = self.tw.get(k)
            if w is not None and self._need(eng, x.is_dma, w):
                if not (x.is_dma and w.is_dma and w.eng == eng and getattr(w, "grp", None) == dma):
                    deps.add(w)
            for r in self.tr.get(k, ()):
                if self._need(eng, x.is_dma, r):
                    deps.add(r)
        x.deps = sorted(deps, key=lambda o: o.idx)
        for r in reads:
            self.tr.setdefault(r, []).append(x)
        for k in writes:
            self.tw[k] = x
            self.tr[k] = []
        if x.is_dma:
            if dma not in self.dsem:
                self.dsem[dma] = [self.nc.alloc_semaphore("ds_" + str(dma)), 0]
            ent = self.dsem[dma]
            ent[1] += inc
            x.sem = ent[0]
            x.val = ent[1]
            x.inc = inc
        self.ops.append(x)
        return x

    def mark(self, group):
        k = (group, len(self.marks.setdefault(group, [])))
        self.marks[group].append(k)
        return k

    def pe(self, fn, reads=(), writes=()):
        return self.op("pe", fn, reads, writes)

    def act(self, fn, reads=(), writes=()):
        return self.op("act", fn, reads, writes)

    def dve(self, fn, reads=(), writes=()):
        return self.op("dve", fn, reads, writes)

    def pool(self, fn, reads=(), writes=()):
        return self.op("pool", fn, reads, writes)

    def dma(self, q, sem, out, in_, reads=(), writes=(), **kw):
        return self.op(q, lambda e: e.dma_start(out=out, in_=in_, **kw), reads, writes, dma=sem)

    def flush(self):
        nc = self.nc
        ops = self.ops
        for x in ops:
            for d in x.deps:
                if not d.is_dma:
                    d.inc = True
        for x in ops:
            if not x.is_dma:
                if x.inc:
                    self.ecnt[x.eng] += 1
                x.sem = self.esem[x.eng]
                x.val = self.ecnt[x.eng]
        per = {}
        for x in ops:
            per.setdefault(x.eng, []).append(x)
        tail = [(ent[0], ent[1]) for ent in self.dsem.values() if ent[1] > 0]
        waited = self.waited

        def emit(ename, eng, lst):
            for x in lst:
                for d in x.deps:
                    key = (ename, id(d.sem))
                    if waited.get(key, 0) < d.val:
                        eng.wait_ge(d.sem, d.val)
                        waited[key] = d.val
                ins = x.fn(eng)
                if x.inc:
                    ins.then_inc(x.sem, int(x.inc) if x.is_dma else 1)
            if ename == "sp":
                for s, v in tail:
                    key = (ename, id(s))
                    if waited.get(key, 0) < v:
                        eng.wait_ge(s, v)
                        waited[key] = v

        with nc.Block() as block:
            per.setdefault("sp", [])
            for ename, lst in per.items():
                f = (lambda en, l: (lambda eng: emit(en, eng, l)))(ename, lst)
                {"pe": block.tensor, "act": block.scalar, "dve": block.vector,
                 "pool": block.gpsimd, "sp": block.sync}[ename](f)
        self.n_total += len(ops)
        self.ops = []
        self.tw = {}
        self.tr = {}
        self.alias = {}


class Rot:
    def __init__(self, items):
        self.items = items
        self.i = 0

    def next(self):
        it = self.items[self.i % len(self.items)]
        self.i += 1
        return it


def make_ident(P, identf, ident):
    P.pool(lambda e: e.memset(identf[:], 1.0), writes=["identf"])
    P.pool(lambda e: e.affine_select(out=identf[:], in_=identf[:], pattern=[[-1, 128]],
                                     compare_op=ALU.is_equal, fill=0.0, base=0, channel_multiplier=1),
           reads=["identf"], writes=["identf"])
    P.dve(lambda e: e.tensor_copy(out=ident[:], in_=identf[:]), reads=["identf"], writes=["ident"])


def load_weight_bf(P, q, wst_rot, dst, dst_key, src_ap, nk, ncols, split=True):
    st, skey = wst_rot.next()
    P.dma(q, skey, st[:, 0:nk, 0:ncols], src_ap.rearrange("(kc kp) c -> kp kc c", kp=128), writes=[skey])
    if split and nk >= 4:
        k1 = (nk * 5) // 8 if not P.no_pool_cast else nk // 2
        if P.no_pool_cast:
            P.act(lambda e: e.copy(out=dst[:, 0:k1, 0:ncols], in_=st[:, 0:k1, 0:ncols]),
                  reads=[skey], writes=[(dst_key, "a")])
        else:
            P.pool(lambda e: e.tensor_copy(out=dst[:, 0:k1, 0:ncols], in_=st[:, 0:k1, 0:ncols]),
                   reads=[skey], writes=[(dst_key, "a")])
        P.dve(lambda e: e.tensor_copy(out=dst[:, k1:nk, 0:ncols], in_=st[:, k1:nk, 0:ncols]),
              reads=[skey], writes=[(dst_key, "b")])
        P.alias[dst_key] = [(dst_key, "a"), (dst_key, "b")]
    else:
        P.pool(lambda e: e.tensor_copy(out=dst[:, 0:nk, 0:ncols], in_=st[:, 0:nk, 0:ncols]),
               reads=[skey], writes=[dst_key])
        P.alias.pop(dst_key, None)


def rmsnorm_to_xnT(P, nc, xs, xs_key, gb, xnT, lb, T, ident):
    sq, ss, ms, rstd, xnb, ptr = T["sq"], T["ss"], T["ms"], T["rstd"], T["xnb"], T["ptr"]
    P.act(lambda e: e.activation(out=sq[:], in_=xs[:], func=AF.Square, accum_out=ss[:, lb:lb + 1]),
          reads=[xs_key], writes=["sq", ("ss", lb)])
    P.dve(lambda e: e.tensor_scalar(out=ms[:, lb:lb + 1], in0=ss[:, lb:lb + 1], scalar1=1.0 / D, scalar2=EPS,
                                    op0=ALU.mult, op1=ALU.add), reads=[("ss", lb)], writes=[("ms", lb)])
    P.act(lambda e: e.activation(out=ms[:, lb:lb + 1], in_=ms[:, lb:lb + 1], func=AF.Sqrt),
          reads=[("ms", lb)], writes=[("ms", lb)])
    P.dve(lambda e: e.reciprocal(out=rstd[:, lb:lb + 1], in_=ms[:, lb:lb + 1]),
          reads=[("ms", lb)], writes=[("rstd", lb)])
    xb, xbk = xnb.next()
    P.dve(lambda e: e.scalar_tensor_tensor(out=xb[:], in0=xs[:], scalar=rstd[:, lb:lb + 1], in1=gb[:],
                                           op0=ALU.mult, op1=ALU.mult),
          reads=[xs_key, ("rstd", lb), "gb"], writes=[xbk])
    pt, ptk = ptr.next()
    for kc in range(8):
        P.pe(lambda e, kc=kc: e.transpose(out=pt[:, kc, :], in_=xb[:, kc * 128:(kc + 1) * 128], identity=ident[:]),
             reads=[xbk, "ident"], writes=[ptk])
    P.act(lambda e: e.copy(out=xnT[:, :, lb * 128:(lb + 1) * 128], in_=pt[:]),
          reads=[ptk], writes=[("xnT", lb)])


def phase_A(nc, P, io, gather=None):
    x, gnorm, w_in, b_f = io["x"], io["fox_norm"], io["fox_w_in"], io["fox_b_f"]
    qT0, kT0, v0, zs0, lf0 = io["qT0"], io["kT0"], io["v0"], io["zs0"], io["lf0"]
    with ExitStack() as es:
        identf = es.enter_context(nc.sbuf_tensor("A_identf", [128, 128], F32))
        ident = es.enter_context(nc.sbuf_tensor("A_ident", [128, 128], BF16))
        gb = es.enter_context(nc.sbuf_tensor("A_gb", [128, D], F32))
        bfb = es.enter_context(nc.sbuf_tensor("A_bfb", [128, 16], F32))
        xs0 = es.enter_context(nc.sbuf_tensor("A_xs0", [128, D], F32))
        xs1 = es.enter_context(nc.sbuf_tensor("A_xs1", [128, D], F32))
        sq = es.enter_context(nc.sbuf_tensor("A_sq", [128, D], F32))
        ss = es.enter_context(nc.sbuf_tensor("A_ss", [128, NLB], F32))
        ms = es.enter_context(nc.sbuf_tensor("A_ms", [128, NLB], F32))
        rstd = es.enter_context(nc.sbuf_tensor("A_rstd", [128, NLB], F32))
        xnb0 = es.enter_context(nc.sbuf_tensor("A_xnb0", [128, D], BF16))
        xnb1 = es.enter_context(nc.sbuf_tensor("A_xnb1", [128, D], BF16))
        xnT = es.enter_context(nc.sbuf_tensor("A_xnT", [128, 8, NTL], BF16))
        wst0 = es.enter_context(nc.sbuf_tensor("A_wst0", [128, 8, 512], F32))
        wst1 = es.enter_context(nc.sbuf_tensor("A_wst1", [128, 8, 512], F32))
        wbf0 = es.enter_context(nc.sbuf_tensor("A_wbf0", [128, 8, 512], BF16))
        wbf1 = es.enter_context(nc.sbuf_tensor("A_wbf1", [128, 8, 512], BF16))
        wf = es.enter_context(nc.sbuf_tensor("A_wf", [128, 8, 16], BF16))
        vz0 = es.enter_context(nc.sbuf_tensor("A_vz0", [128, NLB, 512], BF16))
        vz1 = es.enter_context(nc.sbuf_tensor("A_vz1", [128, NLB, 512], BF16))
        qk0 = es.enter_context(nc.sbuf_tensor("A_qk0", [128, 512], BF16))
        qk1 = es.enter_context(nc.sbuf_tensor("A_qk1", [128, 512], BF16))
        qk2 = es.enter_context(nc.sbuf_tensor("A_qk2", [128, 512], BF16))
        qk3 = es.enter_context(nc.sbuf_tensor("A_qk3", [128, 512], BF16))
        ft = es.enter_context(nc.sbuf_tensor("A_ft", [128, NLB, 16], F32))
        lf = es.enter_context(nc.sbuf_tensor("A_lf", [128, NLB, 16], F32))
        ptr0 = es.enter_context(nc.psum_tensor("A_ptr0", [128, 8, 128], BF16))
        ptr1 = es.enter_context(nc.psum_tensor("A_ptr1", [128, 8, 128], BF16))
        pm0 = es.enter_context(nc.psum_tensor("A_pm0", [128, 512], F32))
        pm1 = es.enter_context(nc.psum_tensor("A_pm1", [128, 512], F32))
        pm2 = es.enter_context(nc.psum_tensor("A_pm2", [128, 512], F32))
        pm3 = es.enter_context(nc.psum_tensor("A_pm3", [128, 512], F32))
        pf = es.enter_context(nc.psum_tensor("A_pf", [128, NLB, 16], F32))
        make_ident(P, identf, ident)
        P.dma("sp", "c0", gb[:], gnorm.partition_broadcast(128), writes=["gb"])
        P.dma("sp", "c0", bfb[:], b_f.partition_broadcast(128), writes=["bfb"])
        T = dict(sq=sq, ss=ss, ms=ms, rstd=rstd,
                 xnb=Rot([(xnb0, "xnb0"), (xnb1, "xnb1")]),
                 ptr=Rot([(ptr0, "ptr0"), (ptr1, "ptr1")]))
        xs_rot = Rot([(xs0, "xs0"), (xs1, "xs1")])
        wst = Rot([(wst0, "wst0"), (wst1, "wst1")])
        wbf = Rot([(wbf0, "wbf0"), (wbf1, "wbf1")])
        pm = Rot([(pm0, "pm0"), (pm1, "pm1"), (pm2, "pm2"), (pm3, "pm3")])
        qk = Rot([(qk0, "qk0"), (qk1, "qk1"), (qk2, "qk2"), (qk3, "qk3")])
        vz = Rot([(vz0, "vz0"), (vz1, "vz1")])
        xv = x.rearrange("(lb p) d -> lb p d", p=128)
        load_weight_bf(P, "sp", wst, wf, "wf", w_in[:, 4096:4112], 8, 16)
        chunks = [("k", 0), ("k", 1), ("v", 0), ("v", 1), ("q", 0), ("q", 1), ("z", 0), ("z", 1)]
        P.no_pool_cast = gather is not None
        col0 = {"q": 0, "k": 1024, "v": 2048, "z": 3072}
        wcur = []

        def issue_w(ci):
            kind, hh = chunks[ci]
            wb, wbk = wbf.next()
            c0 = col0[kind] + hh * 512
            load_weight_bf(P, "sp", wst, wb, wbk, w_in[:, c0:c0 + 512], 8, 512)
            wcur.append((wb, wbk))

        issue_w(0)
        for lb in range(NLB):
            xs, xsk = xs_rot.next()
            P.dma("sp", xsk, xs[:], xv[lb], writes=[xsk])
            rmsnorm_to_xnT(P, nc, xs, xsk, gb, xnT, lb, T, ident)
        allx = [("xnT", lb) for lb in range(NLB)]
        for lb in range(NLB):
            for kc in range(8):
                P.pe(lambda e, lb=lb, kc=kc: e.matmul(pf[:, lb, :], lhsT=xnT[:, kc, lb * 128:(lb + 1) * 128],
                                                      rhs=wf[:, kc, :], start=(kc == 0), stop=(kc == 7)),
                     reads=[("xnT", lb), "wf"], writes=["pf"])
        P.dve(lambda e: e.tensor_tensor(out=ft[:], in0=pf[:], in1=bfb[:].unsqueeze(1).broadcast_to([128, NLB, 16]),
                                        op=ALU.add), reads=["pf", "bfb"], writes=["ft"])
        P.act(lambda e: e.activation(out=ft[:], in_=ft[:], func=AF.Exp, scale=-1.0), reads=["ft"], writes=["ft"])
        P.act(lambda e: e.activation(out=ft[:], in_=ft[:], func=AF.Ln, bias=1.0), reads=["ft"], writes=["ft"])
        P.dve(lambda e: e.tensor_scalar(out=lf[:], in0=ft[:], scalar1=-1.0, scalar2=None, op0=ALU.mult),
              reads=["ft"], writes=["lf"])
        P.dma("sp", "stl", lf0, lf[:], reads=["lf"], writes=[P.mark("A:lf")])
        if gather is not None:
            gather("lf0", P.marks["A:lf"])
        ev = 0
        for ci, (kind, hh) in enumerate(chunks):
            if ci + 1 < len(chunks):
                issue_w(ci + 1)
            wb, wbk = wcur[ci]
            if kind in ("q", "k"):
                dstT = qT0 if kind == "q" else kT0
                for sl in range(4):
                    for tt in range(4):
                        ps, psk = pm.next()
                        for kc in range(8):
                            P.pe(lambda e, ps=ps, kc=kc, sl=sl, tt=tt, wb=wb: e.matmul(
                                ps[:], lhsT=wb[:, kc, sl * 128:(sl + 1) * 128], rhs=xnT[:, kc, tt * 512:(tt + 1) * 512],
                                start=(kc == 0), stop=(kc == 7)),
                                reads=[wbk] + allx[tt * 4:tt * 4 + 4], writes=[psk])
                        sb, sbk = qk.next()
                        sc = 0.125 if kind == "q" else 1.0
                        if ev % 2 == 0:
                            P.act(lambda e, sb=sb, ps=ps, sc=sc: e.activation(out=sb[:], in_=ps[:], func=AF.Copy, scale=sc),
                                  reads=[psk], writes=[sbk])
                        else:
                            P.dve(lambda e, sb=sb, ps=ps, sc=sc: e.tensor_scalar(out=sb[:], in0=ps[:], scalar1=sc, scalar2=None,
                                                                               op0=ALU.mult), reads=[psk], writes=[sbk])
                        ev += 1
                        r0 = hh * 512 + sl * 128
                        P.dma("sp", "st" + sbk, dstT[r0:r0 + 128, tt * 512:(tt + 1) * 512], sb[:], reads=[sbk],
                              writes=[P.mark("A:" + kind)])
            else:
                vs, vsk = vz.next()
                for lb in range(NLB):
                    ps, psk = pm.next()
                    for kc in range(8):
                        P.pe(lambda e, ps=ps, kc=kc, lb=lb, wb=wb: e.matmul(
                            ps[:], lhsT=xnT[:, kc, lb * 128:(lb + 1) * 128], rhs=wb[:, kc, :],
                            start=(kc == 0), stop=(kc == 7)), reads=[wbk, ("xnT", lb)], writes=[psk])
                    if kind == "v":
                        P.dve(lambda e, ps=ps, lb=lb, vs=vs: e.tensor_copy(out=vs[:, lb, :], in_=ps[:]),
                              reads=[psk], writes=[vsk])
                    else:
                        P.act(lambda e, ps=ps, lb=lb, vs=vs: e.activation(out=vs[:, lb, :], in_=ps[:], func=AF.Silu),
                              reads=[psk], writes=[vsk])
                if kind == "v":
                    for a4 in range(4):
                        P.dma("sp", "st" + vsk, v0[a4][:, :, hh * 512:(hh + 1) * 512], vs[:, 4 * a4:4 * a4 + 4, :], reads=[vsk],
                              writes=[P.mark("A:v")])
                else:
                    P.dma("sp", "st" + vsk, zs0[:, :, hh * 512:(hh + 1) * 512], vs[:], reads=[vsk], writes=[P.mark("A:" + kind)])
            if gather is not None and (kind, hh) == ("k", 1):
                gather("kT0", P.marks["A:k"])
            if gather is not None and (kind, hh) == ("v", 1):
                gather("v0", P.marks["A:v"])
        P.flush()


def own_tokens(j):
    return np.concatenate([np.arange(512 * (4 * a + j), 512 * (4 * a + j) + 512) for a in range(4)])


def build_A():
    nc = bass.Bass("TRN2", target_bir_lowering=False)
    io = {}
    io["x"] = nc.dram_tensor("x", [NTL, D], F32, kind="ExternalInput").ap()
    io["fox_norm"] = nc.dram_tensor("fox_norm", [D], F32, kind="ExternalInput").ap()
    io["fox_w_in"] = nc.dram_tensor("fox_w_in", [D, 4112], F32, kind="ExternalInput").ap()
    io["fox_b_f"] = nc.dram_tensor("fox_b_f", [16], F32, kind="ExternalInput").ap()
    io["qT0"] = nc.dram_tensor("qT0", [1024, NTL], BF16, kind="ExternalOutput").ap()
    io["kT0"] = nc.dram_tensor("kT0", [1024, NTL], BF16, kind="ExternalOutput").ap()
    io["v0"] = nc.dram_tensor("v0", [128, NLB, D], BF16, kind="ExternalOutput").ap()
    io["zs0"] = nc.dram_tensor("zs0", [128, NLB, D], BF16, kind="ExternalOutput").ap()
    io["lf0"] = nc.dram_tensor("lf0", [128, NLB, 16], F32, kind="ExternalOutput").ap()
    P = Prog(nc)
    phase_A(nc, P, io)
    return nc


def phase_B12(nc, P, io, gz, Wpre=None):
    kT0g, v0g, lf0g, qT0, zs0 = io["kT0g"], io["v0g"], io["lf0g"], io["qT0"], io["zs0"]
    caug, maskT_d, tri_d, ustrip_d = io["caug"], io["maskT"], io["tri"], io["ustrip"]
    caug_own = io["caug_own"]
    v0cm = io.get("cm", {}).get("v0")
    pid = nc.partition_id()
    jj = pid % 4
    with ExitStack() as es:
        lfg = es.enter_context(nc.sbuf_tensor("B1_lfg", [128, 64, 16], F32))
        tri = es.enter_context(nc.sbuf_tensor("B1_tri", [128, 128], F32))
        us = es.enter_context(nc.sbuf_tensor("B1_us", [128, 127], F32))
        carry = es.enter_context(nc.sbuf_tensor("B1_carry", [16, 64], F32))
        cc = es.enter_context(nc.sbuf_tensor("B1_cc", [16, 2048], F32))
        t1 = es.enter_context(nc.sbuf_tensor("B1_t1", [16, 2048], F32))
        t2 = es.enter_context(nc.sbuf_tensor("B1_t2", [16, 2048], F32))
        aug0 = es.enter_context(nc.sbuf_tensor("B1_aug0", [16, 6, 2048], BF16))
        aug1 = es.enter_context(nc.sbuf_tensor("B1_aug1", [16, 6, 2048], BF16))
        pc = es.enter_context(nc.psum_tensor("B1_pc", [16, 64], F32))
        pcs0 = es.enter_context(nc.psum_tensor("B1_pcs0", [16, 512], F32))
        pcs1 = es.enter_context(nc.psum_tensor("B1_pcs1", [16, 512], F32))
        P.dma("sp", "b1c", tri[:], tri_d, writes=["tri"])
        P.dma("sp", "b1c", us[:], ustrip_d, writes=["us"])
        P.dma("sp", "b1z", gz[:], zs0, writes=["gz"])
        lfv = lfg[:].rearrange("p (a r bl) h -> p a r bl h", a=4, r=4, bl=4)
        for r in range(4):
            P.dma("sp", "b1l", lfv[:, :, r], lf0g[r].rearrange("p (a bl) h -> p a bl h", bl=4), writes=["lfg"])
        for G in range(64):
            P.pe(lambda e, G=G: e.matmul(pc[:], lhsT=lfg[:, G, :], rhs=us[:, 63 - G:127 - G],
                                         start=(G == 0), stop=(G == 63)), reads=["lfg", "us"], writes=["pc"])
        P.dve(lambda e: e.tensor_copy(out=carry[:], in_=pc[:]), reads=["pc"], writes=["carry"])
        pcs = Rot([(pcs0, "pcs0"), (pcs1, "pcs1")])
        augr = Rot([(aug0, "aug0"), (aug1, "aug1")])
        for ch in range(4):
            for g4 in range(4):
                ps, psk = pcs.next()
                for bq in range(4):
                    G = 16 * ch + 4 * g4 + bq
                    P.pe(lambda e, ps=ps, bq=bq, G=G: e.matmul(ps[:, bq * 128:(bq + 1) * 128], lhsT=lfg[:, G, :], rhs=tri[:],
                                                              start=True, stop=True), reads=["lfg", "tri"], writes=[psk])
                for bq in range(4):
                    G = 16 * ch + 4 * g4 + bq
                    c0 = (4 * g4 + bq) * 128
                    P.dve(lambda e, ps=ps, bq=bq, G=G, c0=c0: e.tensor_scalar(
                        out=cc[:, c0:c0 + 128], in0=ps[:, bq * 128:(bq + 1) * 128], scalar1=carry[:, G:G + 1], scalar2=None,
                        op0=ALU.add), reads=[psk, "carry"], writes=["cc"])
            ag, agk = augr.next()
            P.dve(lambda e, ag=ag: e.tensor_copy(out=ag[:, 0, :], in_=cc[:]), reads=["cc"], writes=[agk])
            P.dve(lambda e, ag=ag: e.tensor_tensor(out=t1[:], in0=cc[:], in1=ag[:, 0, :], op=ALU.subtract),
                  reads=["cc", agk], writes=["t1"])
            P.dve(lambda e, ag=ag: e.tensor_copy(out=ag[:, 1, :], in_=t1[:]), reads=["t1"], writes=[agk])
            P.dve(lambda e, ag=ag: e.tensor_tensor(out=t2[:], in0=t1[:], in1=ag[:, 1, :], op=ALU.subtract),
                  reads=["t1", agk], writes=["t2"])
            P.dve(lambda e, ag=ag: e.tensor_copy(out=ag[:, 2, :], in_=t2[:]), reads=["t2"], writes=[agk])
            P.dve(lambda e, ag=ag: e.tensor_scalar(out=ag[:, 3:6, :], in0=ag[:, 0:3, :], scalar1=-1.0, scalar2=None,
                                                   op0=ALU.mult), reads=[agk], writes=[agk])
            P.dma("sp", "b1s", caug[:, :, ch * 2048:(ch + 1) * 2048], ag[:], reads=[agk], writes=["D:caug"])
        for a in range(4):
            P.dma("pool", "b1o", caug_own[:, :, a * 512:(a + 1) * 512], caug[:, 0:3, bass.ds(jj * 512 + 2048 * a, 512)],
                  reads=["D:caug"], writes=["D:caug_own"])
        P.flush()
    with ExitStack() as es:
        identf = es.enter_context(nc.sbuf_tensor("B2_identf", [128, 128], F32))
        ident = es.enter_context(nc.sbuf_tensor("B2_ident", [128, 128], BF16))
        maskT = es.enter_context(nc.sbuf_tensor("B2_mask", [128, 16, 512], BF16))
        kTa = es.enter_context(nc.sbuf_tensor("B2_kT0", [70, S], BF16))
        kTb = es.enter_context(nc.sbuf_tensor("B2_kT1", [70, S], BF16))
        qTa = es.enter_context(nc.sbuf_tensor("B2_qT0", [70, NTL], BF16))
        qTb = es.enter_context(nc.sbuf_tensor("B2_qT1", [70, NTL], BF16))
        vraw = es.enter_context(nc.sbuf_tensor("B2_vraw", [128, 64, 128], BF16))
        vA0 = es.enter_context(nc.sbuf_tensor("B2_vA0", [128, 64, 2, 65], BF16))
        vA1 = es.enter_context(nc.sbuf_tensor("B2_vA1", [128, 64, 2, 65], BF16))
        pT0 = es.enter_context(nc.sbuf_tensor("B2_pT0", [128, 2, 512], BF16))
        pT1 = es.enter_context(nc.sbuf_tensor("B2_pT1", [128, 2, 512], BF16))
        pT2 = es.enter_context(nc.sbuf_tensor("B2_pT2", [128, 2, 512], BF16))
        rc = es.enter_context(nc.sbuf_tensor("B2_rc", [128, 8], F32))
        pS0 = es.enter_context(nc.psum_tensor("B2_pS0", [128, 2, 512], F32))
        pS1 = es.enter_context(nc.psum_tensor("B2_pS1", [128, 2, 512], F32))
        pS2 = es.enter_context(nc.psum_tensor("B2_pS2", [128, 2, 512], F32))
        pO0 = es.enter_context(nc.psum_tensor("B2_pO0", [128, 4, 128], F32))
        pO1 = es.enter_context(nc.psum_tensor("B2_pO1", [128, 4, 128], F32))
        make_ident(P, identf, ident)
        P.dma("sp", "b2c", maskT[:], maskT_d, writes=["mask"])
        for kt, ktk in ((kTa, "kT0"), (kTb, "kT1")):
            P.pool(lambda e, kt=kt: e.memset(kt[64:67, :], 1.0), writes=[ktk])
        for qt, qtk in ((qTa, "qT0"), (qTb, "qT1")):
            P.pool(lambda e, qt=qt: e.memset(qt[64:70, :], 1.0), writes=[qtk])
        for va, vak in ((vA0, "vA0"), (vA1, "vA1")):
            P.pool(lambda e, va=va: e.memset(va[:, :, :, 64:65], 1.0), writes=[vak])
        kTr = Rot([(kTa, "kT0"), (kTb, "kT1")])
        qTr = Rot([(qTa, "qT0"), (qTb, "qT1")])
        pSr = Rot([(pS0, "pS0"), (pS1, "pS1"), (pS2, "pS2")])
        pTr = Rot([(pT0, "pT0"), (pT1, "pT1"), (pT2, "pT2")])
        pOr = Rot([(pO0, "pO0"), (pO1, "pO1")])
        rci = [0]

        def load_head(h):
            kt, ktk = kTr.next()
            qt, qtk = qTr.next()
            ktv = kt[0:64, :].rearrange("d (a r t) -> d a r t", a=4, r=4)
            kcm = io.get("cm", {}).get("kT0")
            for r in range(4):
                if kcm is None:
                    ksrc_ = kT0g[r, h * 64:(h + 1) * 64, :]
                else:
                    ksrc_ = kcm.bitcast(BF16)[h // 4, r * 256 + (h % 4) * 64:r * 256 + (h % 4) * 64 + 64, :]
                P.dma("sp", "ld" + ktk, ktv[:, :, r], ksrc_.rearrange("d (a t) -> d a t", a=4), writes=[ktk])
            P.dma("sp", "ld" + ktk, kt[67:70, :], caug[h, 3:6, :], reads=["D:caug"], writes=[ktk])
            P.dma("sp", "ld" + qtk, qt[0:64, :], qT0[h * 64:(h + 1) * 64, :], writes=[qtk])
            P.dma("sp", "ld" + qtk, qt[64:67, :], caug_own[h], reads=["D:caug_own"], writes=[qtk])
            return kt, ktk, qt, qtk

        vbufs = [(vA0, "vA0"), (vA1, "vA1")]

        def load_vpair(hp):
            va, vak = vbufs[hp % 2]
            vrv = vraw[:].rearrange("p (a r bl) c -> p a r bl c", a=4, r=4, bl=4)
            for r in range(4):
                for a in range(4):
                    vsrc_ = v0cm.bitcast(BF16)[a, r * 128:(r + 1) * 128, :].rearrange("p (bl c) -> p bl c", bl=4)[
                        :, :, hp * 128:(hp + 1) * 128]
                    P.dma("sp", "ldvr", vrv[:, a, r], vsrc_, writes=["vraw"])
            P.dve(lambda e, va=va: e.tensor_copy(out=va[:, :, :, 0:64], in_=vraw[:].rearrange("p g (hh c) -> p g hh c", hh=2)),
                  reads=["vraw"], writes=[vak])

        def prefetch_epilogue_weights():
            wst_ = es.enter_context(nc.sbuf_tensor("B2_wst", [128, 8, 512], F32))
            for (wsrc, wdst, key, nk) in ((io["fox_w_out"], Wpre["wout"], "wout", 8), (io["ple_w_gate0"], Wpre["wg"], "wg", 8),
                                          (io["ple_w_up0"], Wpre["wup"], "wup", 2)):
                for n in range(2):
                    P.dma("sp", "b2wst", wst_[:, 0:nk, :], wsrc[:, n * 512:(n + 1) * 512].rearrange("(kc kp) c -> kp kc c", kp=128),
                          writes=["b2wst"])
                    P.pool(lambda e, wdst=wdst, n=n, nk=nk: e.tensor_copy(out=wdst[:, :, n * 512:(n + 1) * 512], in_=wst_[:, 0:nk, :]),
                           reads=["b2wst"], writes=[key])

        DEPTH = 2
        state = {}

        def gen():
            nxt_head = load_head(0)
            load_vpair(0)
            if Wpre is not None:
                prefetch_epilogue_weights()
            for h in range(16):
                hp, hh = h // 2, h % 2
                kt, ktk, qt, qtk = nxt_head
                va, vak = vbufs[hp % 2]
                if h + 1 < 16:
                    nxt_head = load_head(h + 1)
                for a in range(4):
                    nJ = 16 * a + 16
                    po, pok = pOr.next()
                    for J2 in range(nJ // 2):
                        yield dict(h=h, hh=hh, a=a, J2=J2, nJ=nJ, kt=kt, ktk=ktk, qt=qt, qtk=qtk, va=va, vak=vak, po=po, pok=pok)

        def emit_S(st):
            ps, psk = pSr.next()
            st["ps"], st["psk"] = ps, psk
            a, kt, qt = st["a"], st["kt"], st["qt"]
            for u in range(2):
                J = 2 * st["J2"] + u
                masked = J >= 16 * a
                P.pe(lambda e, ps=ps, kt=kt, qt=qt, J=J, a=a, masked=masked, u=u: e.matmul(
                    ps[:, u, :], lhsT=kt[0:70, J * 128:(J + 1) * 128], rhs=qt[0:70, a * 512:(a + 1) * 512],
                    start=True, stop=(not masked)), reads=[st["ktk"], st["qtk"]], writes=[psk])
                if masked:
                    P.pe(lambda e, ps=ps, J=J, a=a, u=u: e.matmul(ps[:, u, :], lhsT=ident[:], rhs=maskT[:, J - 16 * a, :],
                                                                start=False, stop=True), reads=["ident", "mask"], writes=[psk])

        def emit_rest(st):
            ps, psk, po, pok, va, vak = st["ps"], st["psk"], st["po"], st["pok"], st["va"], st["vak"]
            h, hh, a, nJ = st["h"], st["hh"], st["a"], st["nJ"]
            if hh == 0 and a == 0 and st["J2"] == 0 and h // 2 + 1 < 8:
                load_vpair(h // 2 + 1)
            pt, ptk = pTr.next()
            P.act(lambda e, pt=pt, ps=ps: e.activation(out=pt[:], in_=ps[:], func=AF.Exp), reads=[psk], writes=[ptk])
            for u in range(2):
                J = 2 * st["J2"] + u
                for qb in range(4):
                    P.pe(lambda e, po=po, pt=pt, va=va, qb=qb, J=J, hh=hh, nJ=nJ, u=u: e.matmul(
                        po[:, qb, 0:65], lhsT=pt[:, u, qb * 128:(qb + 1) * 128], rhs=va[:, J, hh, :],
                        start=(J == 0 and qb == 0), stop=(J == nJ - 1), skip_group_check=True), reads=[ptk, vak], writes=[pok])
            if st["J2"] == nJ // 2 - 1:
                for qb in range(4):
                    lb = 4 * a + qb
                    ri = rci[0] % 8
                    rci[0] += 1
                    P.dve(lambda e, po=po, qb=qb, ri=ri: e.reciprocal(out=rc[:, ri:ri + 1], in_=po[:, qb, 64:65]),
                          reads=[pok], writes=[("rc", ri)])
                    P.dve(lambda e, po=po, qb=qb, ri=ri, lb=lb, h=h: e.scalar_tensor_tensor(
                        out=gz[:, lb, h * 64:(h + 1) * 64], in0=po[:, qb, 0:64], scalar=rc[:, ri:ri + 1],
                        in1=gz[:, lb, h * 64:(h + 1) * 64], op0=ALU.mult, op1=ALU.mult),
                        reads=[pok, ("rc", ri), "gz"], writes=["gz"])

        pend = []
        for st in gen():
            emit_S(st)
            pend.append(st)
            if len(pend) > DEPTH:
                emit_rest(pend.pop(0))
        while pend:
            emit_rest(pend.pop(0))
        P.flush()


def consts_B(j):
    k = np.arange(128)[:, None, None]
    Jr = np.arange(16)[None, :, None]
    q = np.arange(512)[None, None, :]
    maskT = np.where(128 * Jr + k <= 512 * j + q, 0.0, NEG).astype(ml_dtypes.bfloat16)
    s = np.arange(128)[:, None]
    t = np.arange(128)[None, :]
    tri = (s <= t).astype(np.float32)
    us = np.broadcast_to((np.arange(127) > 63).astype(np.float32)[None, :], (128, 127)).copy()
    return maskT, tri, us


def build_B12_test(dbg=False):
    nc = bass.Bass("TRN2", target_bir_lowering=False)
    io = {}
    if dbg:
        io["dbg_pt"] = nc.dram_tensor("dbg_pt", [2, 128, 512], BF16, kind="ExternalOutput").ap()
        io["dbg_ps"] = nc.dram_tensor("dbg_ps", [2, 128, 512], F32, kind="ExternalOutput").ap()
        io["dbg_po"] = nc.dram_tensor("dbg_po", [128, 512], F32, kind="ExternalOutput").ap()
    io["kT0g"] = nc.dram_tensor("kT0g", [4, 1024, NTL], BF16, kind="ExternalInput").ap()
    io["v0g"] = nc.dram_tensor("v0g", [4, 128, NLB, D], BF16, kind="ExternalInput").ap()
    io["lf0g"] = nc.dram_tensor("lf0g", [4, 128, NLB, 16], F32, kind="ExternalInput").ap()
    io["qT0"] = nc.dram_tensor("qT0", [1024, NTL], BF16, kind="ExternalInput").ap()
    io["zs0"] = nc.dram_tensor("zs0", [128, NLB, D], BF16, kind="ExternalInput").ap()
    io["maskT"] = nc.dram_tensor("maskT", [128, 16, 512], BF16, kind="ExternalInput").ap()
    io["tri"] = nc.dram_tensor("tri", [128, 128], F32, kind="ExternalInput").ap()
    io["ustrip"] = nc.dram_tensor("ustrip", [128, 127], F32, kind="ExternalInput").ap()
    io["caug"] = nc.dram_tensor("caug", [16, 6, S], BF16, kind="ExternalOutput").ap()
    io["caug_own"] = nc.dram_tensor("caug_own", [16, 3, NTL], BF16, kind="ExternalOutput").ap()
    gz_d = nc.dram_tensor("gz", [128, NLB, D], BF16, kind="ExternalOutput").ap()
    P = Prog(nc)
    with nc.sbuf_tensor("gz_sb", [128, NLB, D], BF16) as gz:
        phase_B12(nc, P, io, gz)
        P.dma("sp", "gzst", gz_d, gz[:], reads=["gz"])
        P.flush()
    return nc


def epilogue_block(P, nc, lb, mixT, mixT_keys, T, W, hres_src, p_src, h_out_dst, nk_mix):
    pm, ptr, ident = T["pm"], T["ptr"], T["ident"]
    xs, xsk = T["xs"].next()
    P.dma("sp", "ld" + xsk, xs[:], hres_src, writes=[xsk])
    pb, pbk = T["pb"].next()
    P.dma("sp", "ld" + pbk, pb[:], p_src, writes=[pbk])
    h1, h1k = T["h1"].next()
    for n in range(2):
        ps, psk = pm.next()
        for kc in range(nk_mix):
            P.pe(lambda e, ps=ps, kc=kc, n=n: e.matmul(ps[:], lhsT=mixT(kc), rhs=W["wout"][:, kc, n * 512:(n + 1) * 512],
                                                     start=(kc == 0), stop=(kc == nk_mix - 1)),
                 reads=list(mixT_keys) + ["wout"], writes=[psk])
        P.dve(lambda e, ps=ps, n=n, h1=h1, xs=xs: e.tensor_tensor(out=h1[:, n * 512:(n + 1) * 512], in0=ps[:],
                                                                in1=xs[:, n * 512:(n + 1) * 512], op=ALU.add),
              reads=[psk, xsk], writes=[h1k])
    hb, hbk = T["hb"].next()
    P.act(lambda e, hb=hb, h1=h1: e.copy(out=hb[:], in_=h1[:]), reads=[h1k], writes=[hbk])
    pt, ptk = ptr.next()
    for kc in range(8):
        P.pe(lambda e, kc=kc, pt=pt, hb=hb: e.transpose(out=pt[:, kc, :], in_=hb[:, kc * 128:(kc + 1) * 128], identity=ident[:]),
             reads=[hbk, "ident"], writes=[ptk])
    hT, hTk = T["hT"].next()
    P.dve(lambda e, hT=hT, pt=pt: e.tensor_copy(out=hT[:], in_=pt[:]), reads=[ptk], writes=[hTk])
    pbb, pbbk = T["pbb"].next()
    P.dve(lambda e, pbb=pbb, pb=pb: e.tensor_copy(out=pbb[:], in_=pb[:]), reads=[pbk], writes=[pbbk])
    pt2, pt2k = ptr.next()
    for k2 in range(2):
        P.pe(lambda e, k2=k2, pt2=pt2, pbb=pbb: e.transpose(out=pt2[:, k2, :], in_=pbb[:, k2 * 128:(k2 + 1) * 128], identity=ident[:]),
             reads=[pbbk, "ident"], writes=[pt2k])
    pT, pTk = T["pT"].next()
    P.dve(lambda e, pT=pT, pt2=pt2: e.tensor_copy(out=pT[:], in_=pt2[:, 0:2, :]), reads=[pt2k], writes=[pTk])
    gate, gk = T["gate"].next()
    for n in range(2):
        ps, psk = pm.next()
        for kc in range(8):
            P.pe(lambda e, ps=ps, kc=kc, n=n, hT=hT: e.matmul(ps[:], lhsT=hT[:, kc, :], rhs=W["wg"][:, kc, n * 512:(n + 1) * 512],
                                                            start=(kc == 0), stop=(kc == 7)), reads=[hTk, "wg"], writes=[psk])
        P.act(lambda e, ps=ps, n=n, gate=gate: e.activation(out=gate[:, n * 512:(n + 1) * 512], in_=ps[:], func=AF.Sigmoid),
              reads=[psk], writes=[gk])
    hn, hnk = T["hn"].next()
    for n in range(2):
        ps, psk = pm.next()
        for k2 in range(2):
            P.pe(lambda e, ps=ps, k2=k2, n=n, pT=pT: e.matmul(ps[:], lhsT=pT[:, k2, :], rhs=W["wup"][:, k2, n * 512:(n + 1) * 512],
                                                            start=(k2 == 0), stop=(k2 == 1)), reads=[pTk, "wup"], writes=[psk])
        P.dve(lambda e, ps=ps, n=n, gate=gate: e.tensor_tensor(out=gate[:, n * 512:(n + 1) * 512], in0=ps[:],
                                                             in1=gate[:, n * 512:(n + 1) * 512], op=ALU.mult),
              reads=[psk, gk], writes=[gk])
    P.dve(lambda e, hn=hn, gate=gate, h1=h1: e.tensor_tensor(out=hn[:], in0=gate[:], in1=h1[:], op=ALU.add),
          reads=[gk, h1k], writes=[hnk])
    if h_out_dst is not None:
        P.dma("sp", "st" + hnk, h_out_dst, hn[:], reads=[hnk], writes=["D:hout"])
    return hn, hnk


def epi_run(P, nblocks, pre, mixT_of, T, W, hres_of, p_of, hout_of, post, nk_mix=8):
    pm, ptr, ident = T["pm"], T["ptr"], T["ident"]
    ctx = {}

    def s_pre(i):
        c = ctx[i] = {}
        c["mk"] = pre(i) if pre is not None else list(T.get("mix_keys", []))
        c["xs"], c["xsk"] = T["xs"].next()
        P.dma("sp", "ld" + c["xsk"], c["xs"][:], hres_of(i), writes=[c["xsk"]])
        c["pb"], c["pbk"] = T["pb"].next()
        P.dma("sp", "ld" + c["pbk"], c["pb"][:], p_of(i), writes=[c["pbk"]])
        c["pbb"], c["pbbk"] = T["pbb"].next()
        P.dve(lambda e, c=c: e.tensor_copy(out=c["pbb"][:], in_=c["pb"][:]), reads=[c["pbk"]], writes=[c["pbbk"]])

    def s1(i):
        c = ctx[i]
        mixT = mixT_of(i)
        c["h1"], c["h1k"] = T["h1"].next()
        for n in range(2):
            ps, psk = pm.next()
            for kc in range(nk_mix):
                P.pe(lambda e, ps=ps, kc=kc, n=n: e.matmul(ps[:], lhsT=mixT(kc), rhs=W["wout"][:, kc, n * 512:(n + 1) * 512],
                                                         start=(kc == 0), stop=(kc == nk_mix - 1)),
                     reads=list(c["mk"]) + ["wout"], writes=[psk])
            P.dve(lambda e, ps=ps, n=n, c=c: e.tensor_tensor(out=c["h1"][:, n * 512:(n + 1) * 512], in0=ps[:],
                                                            in1=c["xs"][:, n * 512:(n + 1) * 512], op=ALU.add),
                  reads=[psk, c["xsk"]], writes=[c["h1k"]])
        c["hb"], c["hbk"] = T["hb"].next()
        P.act(lambda e, c=c: e.copy(out=c["hb"][:], in_=c["h1"][:]), reads=[c["h1k"]], writes=[c["hbk"]])

    def s2a(i):
        c = ctx[i]
        pt, ptk = ptr.next()
        for kc in range(8):
            P.pe(lambda e, kc=kc, pt=pt, c=c: e.transpose(out=pt[:, kc, :], in_=c["hb"][:, kc * 128:(kc + 1) * 128], identity=ident[:]),
                 reads=[c["hbk"], "ident"], writes=[ptk])
        c["hT"], c["hTk"] = T["hT"].next()
        P.dve(lambda e, c=c, pt=pt: e.tensor_copy(out=c["hT"][:], in_=pt[:]), reads=[ptk], writes=[c["hTk"]])
        pt2, pt2k = ptr.next()
        for k2 in range(2):
            P.pe(lambda e, k2=k2, pt2=pt2, c=c: e.transpose(out=pt2[:, k2, :], in_=c["pbb"][:, k2 * 128:(k2 + 1) * 128], identity=ident[:]),
                 reads=[c["pbbk"], "ident"], writes=[pt2k])
        c["pT"], c["pTk"] = T["pT"].next()
        P.act(lambda e, c=c, pt2=pt2: e.copy(out=c["pT"][:], in_=pt2[:, 0:2, :]), reads=[pt2k], writes=[c["pTk"]])

    def s2b(i):
        c = ctx[i]
        gate, gk = T["gate"].next()
        for n in range(2):
            ps, psk = pm.next()
            for kc in range(8):
                P.pe(lambda e, ps=ps, kc=kc, n=n, c=c: e.matmul(ps[:], lhsT=c["hT"][:, kc, :], rhs=W["wg"][:, kc, n * 512:(n + 1) * 512],
                                                              start=(kc == 0), stop=(kc == 7)), reads=[c["hTk"], "wg"], writes=[psk])
            P.act(lambda e, ps=ps, n=n, gate=gate: e.activation(out=gate[:, n * 512:(n + 1) * 512], in_=ps[:], func=AF.Sigmoid),
                  reads=[psk], writes=[gk])
        hn, hnk = T["hn"].next()
        for n in range(2):
            ps, psk = pm.next()
            for k2 in range(2):
                P.pe(lambda e, ps=ps, k2=k2, n=n, c=c: e.matmul(ps[:], lhsT=c["pT"][:, k2, :], rhs=W["wup"][:, k2, n * 512:(n + 1) * 512],
                                                              start=(k2 == 0), stop=(k2 == 1)), reads=[c["pTk"], "wup"], writes=[psk])
            P.dve(lambda e, ps=ps, n=n, gate=gate: e.tensor_tensor(out=gate[:, n * 512:(n + 1) * 512], in0=ps[:],
                                                                 in1=gate[:, n * 512:(n + 1) * 512], op=ALU.mult),
                  reads=[psk, gk], writes=[gk])
        P.dve(lambda e, hn=hn, gate=gate, c=c: e.tensor_tensor(out=hn[:], in0=gate[:], in1=c["h1"][:], op=ALU.add),
              reads=[gk, c["h1k"]], writes=[hnk])
        dst = hout_of(i) if hout_of is not None else None
        if dst is not None:
            P.dma("sp", "st" + hnk, dst, hn[:], reads=[hnk], writes=["D:hout"])
        c["hn"], c["hnk"] = hn, hnk

    for i in range(nblocks + 2):
        if i < nblocks:
            s_pre(i)
        if 0 <= i - 1 < nblocks:
            s2a(i - 1)
        if i < nblocks:
            s1(i)
        if 0 <= i - 1 < nblocks:
            s2b(i - 1)
        if 0 <= i - 2 < nblocks:
            c = ctx.pop(i - 2)
            post(i - 2, c["hn"], c["hnk"])


def alloc_epilogue(nc, es, pfx):
    sb = lambda n, shp, dt: es.enter_context(nc.sbuf_tensor(pfx + n, shp, dt))
    ps = lambda n, shp, dt: es.enter_context(nc.psum_tensor(pfx + n, shp, dt))
    T = {}
    T["identf"] = sb("identf", [128, 128], F32)
    T["ident_t"] = sb("ident", [128, 128], BF16)
    T["ident"] = T["ident_t"]
    mk = lambda n, shp, dt, k: Rot([(sb(f"{n}{i}", shp, dt), f"{pfx}{n}{i}") for i in range(k)])
    T["xs"] = mk("xs", [128, D], F32, 2)
    T["pb"] = mk("pb", [128, 256], F32, 2)
    T["pbb"] = mk("pbb", [128, 256], BF16, 2)
    T["h1"] = mk("h1", [128, D], F32, 2)
    T["hb"] = mk("hb", [128, D], BF16, 2)
    T["hT"] = mk("hT", [128, 8, 128], BF16, 1)
    T["pT"] = mk("pT", [128, 2, 128], BF16, 1)
    T["gate"] = mk("gate", [128, D], F32, 1)
    T["hn"] = mk("hn", [128, D], F32, 3)
    T["ptr"] = Rot([(ps(f"ptr{i}", [128, 8, 128], BF16), f"{pfx}ptr{i}") for i in range(3)])
    T["pm"] = Rot([(ps(f"pm{i}", [128, 512], F32), f"{pfx}pm{i}") for i in range(5)])
    T["sq"] = sb("sq", [128, D], BF16)
    T["ss"] = sb("ss", [128, NLB], F32)
    T["ms"] = sb("ms", [128, NLB], F32)
    T["rstd"] = sb("rstd", [128, NLB], F32)
    T["xnb"] = mk("xnb", [128, D], BF16, 2)
    return T


def phase_B3C(nc, P, io, gz, Wpre=None, gather=None):
    x, p0 = io["x"], io["p0"]
    w_out, w_up, w_gate, g1n, w_in1 = io["fox_w_out"], io["ple_w_up0"], io["ple_w_gate0"], io["dil_norm"], io["dil_w_in"]
    h2_d, q1T, k1T, v1, zs1T = io["h2"], io["q1T"], io["k1T"], io["v1"], io["zs1T"]
    k0tail, v0tail = io["k0tail"], io["v0tail"]
    with ExitStack() as es:
        sb = lambda n, shp, dt: es.enter_context(nc.sbuf_tensor("C_" + n, shp, dt))
        T = alloc_epilogue(nc, es, "C_")
        gb = sb("gb", [128, D], F32)
        if Wpre is None:
            wout = sb("wout", [128, 8, D], BF16)
            wg = sb("wg", [128, 8, D], BF16)
            wup = sb("wup", [128, 2, D], BF16)
        else:
            wout, wg, wup = Wpre["wout"], Wpre["wg"], Wpre["wup"]
        wst = Rot([(sb(f"wst{i}", [128, 8, 512], F32), f"C_wst{i}") for i in range(1)])
        wbf = Rot([(sb(f"wbf{i}", [128, 8, 512], BF16), f"C_wbf{i}") for i in range(2)])
        gT = Rot([(sb(f"gT{i}", [128, 8, 128], BF16), f"C_gT{i}") for i in range(2)])
        xn1T = sb("xn1T", [128, 8, NTL], BF16)
        qk = Rot([(sb(f"qk{i}", [128, 512], BF16), f"C_qk{i}") for i in range(4)])
        W = dict(wout=wout, wg=wg, wup=wup)
        make_ident(P, T["identf"], T["ident_t"])
        P.dma("sp", "c1", gb[:], g1n.partition_broadcast(128), writes=["gb"])
        for n in range(2 if Wpre is None else 0):
            st, stk = wst.next()
            P.dma("sp", stk, st[:], w_out[:, n * 512:(n + 1) * 512].rearrange("(kc kp) c -> kp kc c", kp=128), writes=[stk])
            P.pool(lambda e, st=st, n=n: e.tensor_copy(out=wout[:, :, n * 512:(n + 1) * 512], in_=st[:]), reads=[stk], writes=["wout"])
        for n in range(2 if Wpre is None else 0):
            st, stk = wst.next()
            P.dma("sp", stk, st[:], w_gate[:, n * 512:(n + 1) * 512].rearrange("(kc kp) c -> kp kc c", kp=128), writes=[stk])
            P.pool(lambda e, st=st, n=n: e.tensor_copy(out=wg[:, :, n * 512:(n + 1) * 512], in_=st[:]), reads=[stk], writes=["wg"])
        for n in range(2 if Wpre is None else 0):
            st, stk = wst.next()
            P.dma("sp", stk, st[:, 0:2, :], w_up[:, n * 512:(n + 1) * 512].rearrange("(kc kp) c -> kp kc c", kp=128), writes=[stk])
            P.pool(lambda e, st=st, n=n: e.tensor_copy(out=wup[:, :, n * 512:(n + 1) * 512], in_=st[:, 0:2, :]), reads=[stk], writes=["wup"])
        wcur = []
        chunks = [("k", i) for i in range(6)] + [("v", i) for i in range(6)] + [("q", i) for i in range(6)] + [("z", i) for i in range(2)]
        P.no_pool_cast = gather is not None
        col0 = {"q": 0, "k": 3072, "v": 6144, "z": 9216}

        def issue_w(ci):
            kind, i = chunks[ci]
            wb, wbk = wbf.next()
            c0 = col0[kind] + i * 512
            load_weight_bf(P, "sp", wst, wb, wbk, w_in1[:, c0:c0 + 512], 8, 512)
            wcur.append((wb, wbk))

        xv = x.rearrange("(lb p) d -> lb p d", p=128)
        pv = p0.rearrange("(lb p) d -> lb p d", p=128)
        hv = h2_d.rearrange("(lb p) d -> lb p d", p=128)
        gcur = {}

        def pre_c(lb):
            pt, ptk = T["ptr"].next()
            for kc in range(8):
                P.pe(lambda e, kc=kc, pt=pt, lb=lb: e.transpose(out=pt[:, kc, :], in_=gz[:, lb, kc * 128:(kc + 1) * 128],
                                                              identity=T["ident"][:]), reads=["gz", "ident"], writes=[ptk])
            g, gk = gT.next()
            P.act(lambda e, g=g, pt=pt: e.copy(out=g[:], in_=pt[:]), reads=[ptk], writes=[gk])
            gcur[lb] = g
            return [gk]

        def post_c(lb, hn, hnk):
            rmsnorm_to_xnT(P, nc, hn, hnk, gb, xn1T, lb, T, T["ident"])
            if lb == 8:
                issue_w(0)

        epi_run(P, NLB, pre_c, (lambda lb: (lambda kc: gcur[lb][:, kc, :])), T, W,
                (lambda lb: xv[lb]), (lambda lb: pv[lb]), (lambda lb: hv[lb]), post_c)
        allx = [("xnT", lb) for lb in range(NLB)]
        pm = T["pm"]
        ev = 0
        for ci, (kind, i) in enumerate(chunks):
            if ci + 1 < len(chunks):
                issue_w(ci + 1)
            wb, wbk = wcur[ci]
            if kind in ("q", "k", "z"):
                g = i // 2 if kind != "z" else 0
                dstT = {"q": q1T, "k": k1T, "z": zs1T}[kind]
                R = 1
                for sl in range(4):
                    for a in range(4):
                        ps, psk = pm.next()
                        for kc in range(8):
                            P.pe(lambda e, ps=ps, kc=kc, sl=sl, a=a, wb=wb: e.matmul(
                                ps[:], lhsT=wb[:, kc, sl * 128:(sl + 1) * 128], rhs=xn1T[:, kc, a * 512:(a + 1) * 512],
                                start=(kc == 0), stop=(kc == 7)), reads=[wbk] + allx[a * 4:a * 4 + 4], writes=[psk])
                        sbt, sbk = qk.next()
                        if R == 1:
                            o_ap, i_ap = sbt[:], ps[:]
                        else:
                            o_ap = sbt[:].rearrange("d (r i) -> d r i", r=R)
                            i_ap = ps[:].rearrange("d (i r) -> d r i", r=R)
                        if kind == "z":
                            P.act(lambda e, o_ap=o_ap, i_ap=i_ap: e.activation(out=o_ap, in_=i_ap, func=AF.Silu), reads=[psk], writes=[sbk])
                        elif ev % 2 == 0:
                            P.act(lambda e, o_ap=o_ap, i_ap=i_ap: e.copy(out=o_ap, in_=i_ap), reads=[psk], writes=[sbk])
                        else:
                            P.dve(lambda e, o_ap=o_ap, i_ap=i_ap: e.tensor_copy(out=o_ap, in_=i_ap), reads=[psk], writes=[sbk])
                        ev += 1
                        r0 = i * 512 + sl * 128
                        if kind == "k":
                            rk = (i % 2) * 512 + sl * 128
                            P.dma("sp", "st" + sbk, k1T[g, a][rk:rk + 128, :], sbt[:], reads=[sbk], writes=[P.mark("C:k")])
                        else:
                            P.dma("sp", "st" + sbk, dstT[r0:r0 + 128, a * 512:(a + 1) * 512], sbt[:], reads=[sbk], writes=[P.mark("C:" + kind)])
                        if kind == "k" and g == 0:
                            P.dma("sp", "st" + sbk, k0tail[r0:r0 + 128, a, :], sbt[:, 384:512], reads=[sbk], writes=[P.mark("C:kt")])
            else:
                g = i // 2
                hh = i % 2
                for a in range(4):
                    for b4 in range(4):
                        ps, psk = pm.next()
                        for kc in range(8):
                            if g == 0:
                                lt = xn1T[:, kc, a * 512 + b4 * 128:a * 512 + b4 * 128 + 128]
                            else:
                                lt = xn1T[:, kc, a * 512:(a + 1) * 512].rearrange("k (i r) -> k r i", r=4)[:, b4, :]
                            P.pe(lambda e, ps=ps, kc=kc, lt=lt, wb=wb: e.matmul(ps[:], lhsT=lt, rhs=wb[:, kc, :],
                                                                             start=(kc == 0), stop=(kc == 7)),
                                 reads=[wbk] + allx[a * 4:a * 4 + 4], writes=[psk])
                        sbt, sbk = qk.next()
                        if ev % 2 == 0:
                            P.act(lambda e, ps=ps, sbt=sbt: e.copy(out=sbt[:], in_=ps[:]), reads=[psk], writes=[sbk])
                        else:
                            P.dve(lambda e, ps=ps, sbt=sbt: e.tensor_copy(out=sbt[:], in_=ps[:]), reads=[psk], writes=[sbk])
                        ev += 1
                        P.dma("sp", "st" + sbk, v1[g, a][:, b4, hh * 512:(hh + 1) * 512], sbt[:], reads=[sbk], writes=[P.mark("C:v")])
                        if g == 0 and b4 == 3:
                            P.dma("sp", "st" + sbk, v0tail[:, a, hh * 512:(hh + 1) * 512], sbt[:], reads=[sbk], writes=[P.mark("C:vt")])
            if gather is not None and (kind, i) == ("k", 5):
                gather("k1T", P.marks["C:k"], (4096, 12288))
                gather("k0tail", P.marks["C:kt"])
            if gather is not None and (kind, i) == ("v", 5):
                gather("v1", P.marks["C:v"], (512, 1536))
                gather("v0tail", P.marks["C:vt"])
        P.flush()


def build_B3C_test():
    nc = bass.Bass("TRN2", target_bir_lowering=False)
    io = {}
    ei = lambda n, shp, dt: nc.dram_tensor(n, shp, dt, kind="ExternalInput").ap()
    eo = lambda n, shp, dt: nc.dram_tensor(n, shp, dt, kind="ExternalOutput").ap()
    io["x"] = ei("x", [NTL, D], F32)
    io["p0"] = ei("p0", [NTL, 256], F32)
    io["fox_w_out"] = ei("fox_w_out", [D, D], F32)
    io["ple_w_up0"] = ei("ple_w_up0", [256, D], F32)
    io["ple_w_gate0"] = ei("ple_w_gate0", [D, D], F32)
    io["dil_norm"] = ei("dil_norm", [D], F32)
    io["dil_w_in"] = ei("dil_w_in", [D, 10240], F32)
    gz_d = ei("gz", [128, NLB, D], BF16)
    io["h2"] = eo("h2", [NTL, D], F32)
    io["q1T"] = eo("q1T", [3072, NTL], BF16)
    io["k1T"] = eo("k1T", [3072, NTL], BF16)
    io["v1"] = eo("v1", [3, 128, NLB, D], BF16)
    io["zs1T"] = eo("zs1T", [D, NTL], BF16)
    P = Prog(nc)
    with nc.sbuf_tensor("gz_sb", [128, NLB, D], BF16) as gz:
        P.dma("sp", "gzld", gz[:], gz_d, writes=["gz"])
        phase_B3C(nc, P, io, gz)
    return nc


def consts_D(j):
    i = np.arange(24, dtype=np.float64)
    slopes = (2.0 ** (-8.0 * (i + 1) / 24)).reshape(3, 8)
    k = np.arange(128, dtype=np.float64)[:, None]
    q = np.arange(128, dtype=np.float64)[None, :]
    Et = np.zeros((128, 8, 6, 128), np.float64)
    for h in range(8):
        for g, dil in ((0, 1.0), (1, 4.0)):
            s = slopes[g, h] * dil
            Et[:, h, 2 * g, :] = np.where(k <= q, np.exp(-s * (q - k)), 0.0)
            Et[:, h, 2 * g + 1, :] = np.where(k >= q, np.exp(-s * (128 + q - k)), 0.0)
        s = slopes[2, h] * 16.0
        iq = 32 * j + np.arange(32, dtype=np.float64)[None, :]
        Et[:, h, 4, 0:32] = np.where(k <= iq, np.exp(-s * (iq - k)), 0.0)
        Et[:, h, 5, 0:32] = np.where(k >= iq, np.exp(-s * (128 + iq - k)), 0.0)
    flag = np.ones((128, 4), np.float32)
    if j == 0:
        flag[:, 0] = 0.0
    return Et.astype(np.float32), flag


def phase_D(nc, P, io):
    k1Tg, v1g, k1T, v1, q1T, zs1T = io["k1Tg"], io["v1g"], io["k1T"], io["v1"], io["q1T"], io["zs1T"]
    h2_d, p1, y_d = io["h2"], io["p1"], io["y"]
    w_out, w_up, w_gate, gfin = io["dil_w_out"], io["ple_w_up1"], io["ple_w_gate1"], io["final_norm"]
    Et_d, flag_d, hkT, hv = io["Et"], io["flag"], io["hkT"], io["hv"]
    k0tg, v0tg = io["k0tailg"], io["v0tailg"]
    k1cm = io.get("cm", {}).get("k1T")
    v1cm = io.get("cm", {}).get("v1")
    SC = float(128 ** -0.5)
    pid = nc.partition_id()
    jj = pid % 4
    with ExitStack() as es:
        sb = lambda n, shp, dt: es.enter_context(nc.sbuf_tensor("D_" + n, shp, dt))
        psm = lambda n, shp, dt: es.enter_context(nc.psum_tensor("D_" + n, shp, dt))
        rr = (jj + 3) % 4

        def halo_copy(a):
            ap_ = ((a - 1) + (jj + 3) // 4) if a >= 1 else 0
            if True:
                kb = k1cm.bitcast(BF16)
                koff = ap_ * (4 * D * 512) + rr * (D * 512)
                ksrc = bass.AP(tensor=kb.tensor, offset=koff, ap=[[512, 1024], [1, 512]])
                P.dma("sp", "dhk", hkT[a, 1024:2048, :], ksrc, writes=[("D:hkT", a)])
            else:
                kb = k1cm.bitcast(BF16)
                for c4 in range(4):
                    koff = (c4 * 1024 + rr * 256) * NTL + ap_ * 512
                    ksrc = bass.AP(tensor=kb.tensor, offset=koff, ap=[[NTL, 256], [1, 512]])
                    P.dma("sp", "dhk", hkT[a, 1024 + 256 * c4:1024 + 256 * (c4 + 1), :], ksrc, writes=[("D:hkT", a)])
            ktoff = rr * (D * 512) + ap_ * 128
            ktsrc = bass.AP(tensor=k0tg.tensor, offset=ktoff, ap=[[512, 1024], [1, 128]])
            P.dma("act", "dhkt", hkT[a, 0:1024, 384:512], ktsrc, writes=[("D:hkTt", a)])
            vb = v1cm.bitcast(BF16)
            voff = ap_ * (512 * 4 * D) + rr * (128 * 4 * D)
            vsrc = bass.AP(tensor=vb.tensor, offset=voff, ap=[[4 * D, 128], [1, 4 * D]])
            P.dma("pool", "dhv", hv[a, 1], vsrc, writes=[("D:hv", a)])
            vtoff = rr * (128 * 4 * D) + ap_ * D
            vtsrc = bass.AP(tensor=v0tg.tensor, offset=vtoff, ap=[[4 * D, 128], [1, D]])
            P.dma("pool", "dhv", hv[a, 0][:, 3 * D:4 * D], vtsrc, writes=[("D:hv", a)])

        halo_copy(0)
        T = {}
        T["identf"] = sb("identf", [128, 128], F32)
        T["ident"] = sb("ident", [128, 128], BF16)
        mk = lambda n, shp, dt, k: Rot([(sb(f"{n}{i}", shp, dt), f"D_{n}{i}") for i in range(k)])
        T["xs"] = mk("xs", [128, D], F32, 2)
        T["pb"] = mk("pb", [128, 256], F32, 2)
        T["pbb"] = mk("pbb", [128, 256], BF16, 2)
        T["h1"] = mk("h1", [128, D], F32, 2)
        T["hb"] = mk("hb", [128, D], BF16, 2)
        T["hT"] = mk("hT", [128, 8, 128], BF16, 1)
        T["pT"] = mk("pT", [128, 2, 128], BF16, 1)
        T["gate"] = mk("gate", [128, D], F32, 1)
        T["hn"] = mk("hn", [128, D], F32, 3)
        T["ptr"] = Rot([(psm("ptr0", [128, 8, 128], BF16), "D_ptr0")])
        pS = [(psm(f"pS{i}", [128, 512], F32), f"D_pS{i}") for i in range(3)]
        pN = [(psm(f"pN{i}", [128, 512], F32), f"D_pN{i}") for i in range(2)]
        pD = [(psm(f"pD{i}", [128, 512], F32), f"D_pD{i}") for i in range(2)]
        T["pm"] = Rot(pS)
        sq = sb("sq", [128, D], BF16)
        ss = sb("ss", [128, NLB], F32)
        ms = sb("ms", [128, NLB], F32)
        rstd = sb("rstd", [128, NLB], F32)
        yb = mk("yb", [128, D], F32, 2)
        gfb = sb("gfb", [128, D], F32)
        wout = sb("wout", [128, 8, D], BF16)
        wg = sb("wg", [128, 8, D], BF16)
        wup = sb("wup", [128, 2, D], BF16)
        wst = sb("wst", [128, 8, 256], F32)
        Et = sb("Et", [128, 8, 6, 128], F32)
        flag = sb("flag", [128, 4], F32)
        ones = sb("ones", [128, 128], BF16)
        g1T = Rot([(sb(f"g1T{i}", [128, 8, 512], BF16), f"D_g1T{i}") for i in range(1)])
        exr = Rot([(sb(f"ex{i}", [128, 512], F32), f"D_ex{i}") for i in range(2)])
        ptr_ = Rot([(sb(f"pt{i}", [128, 512], BF16), f"D_pt{i}") for i in range(3)])
        rDr = Rot([(sb(f"rD{i}", [128, 512], F32), f"D_rD{i}") for i in range(1)])
        tNr = Rot([(sb(f"tN{i}", [128, 512], F32), f"D_tN{i}") for i in range(1)])
        bund = []
        for i in range(2):
            bund.append(dict(
                q=(sb(f"bq{i}", [128, 3, 512], BF16), f"D_bq{i}"),
                k01=(sb(f"bk{i}", [128, 2, 512], BF16), f"D_bk{i}"),
                hk=(sb(f"bhk{i}", [128, 2, 512], BF16), f"D_bhk{i}"),
                k2=(sb(f"bk2{i}", [128, 2, 2048], BF16), f"D_bk2{i}"),
                z=(sb(f"bz{i}", [128, 512], BF16), f"D_bz{i}"),
                v01=(sb(f"bv{i}", [128, 2, 4, 128], BF16), f"D_bv{i}"),
                hvh=(sb(f"bhv{i}", [128, 2, 4, 128], BF16), f"D_bhv{i}"),
                v2=(sb(f"bv2{i}", [128, 2, 4, 4, 128], BF16), f"D_bv2{i}"),
            ))
        W = dict(wout=wout, wg=wg, wup=wup)
        make_ident(P, T["identf"], T["ident"])
        P.pool(lambda e: e.memset(ones[:], 1.0), writes=["ones"])
        P.dma("sp", "dc", Et[:], Et_d, writes=["Et"])
        P.dma("sp", "dc", flag[:], flag_d, writes=["flag"])
        P.dma("sp", "dc", gfb[:], gfin.partition_broadcast(128), writes=["gfb"])
        def load_epi_weights():
            for (wsrc, wdst, key, nk) in ((w_out, wout, "wout", 8), (w_gate, wg, "wg", 8), (w_up, wup, "wup", 2)):
                for n in range(4):
                    P.dma("sp", "dwst", wst[:, 0:nk, :], wsrc[:, n * 256:(n + 1) * 256].rearrange("(kc kp) c -> kp kc c", kp=128),
                          writes=["wst"])
                    P.pool(lambda e, wdst=wdst, n=n, nk=nk: e.tensor_copy(out=wdst[:, :, n * 256:(n + 1) * 256], in_=wst[:, 0:nk, :]),
                           reads=["wst"], writes=[key])

        def load_bundle(bi, a, h):
            B = bund[bi]
            qt, qk_ = B["q"]
            hs = slice(h * 128, (h + 1) * 128)
            cs = slice(a * 512, (a + 1) * 512)
            P.dma("sp", "l" + qk_, qt[:], q1T.rearrange("(g r) t -> r g t", g=3)[hs, :, cs], writes=[qk_])
            kt, kk = B["k01"]
            hk_, hkk = B["hk"]
            P.dma("sp", "l" + kk, kt[:], k1T[0:2, a, hs, :].rearrange("g d t -> d g t"), writes=[kk])
            P.dma("sp", "l" + hkk, hk_[:], hkT[a].rearrange("(g r) t -> r g t", g=2)[hs, :, :],
                  reads=[("D:hkT", a), ("D:hkTt", a)], writes=[hkk])
            k2, k2k = B["k2"]
            r0 = 2048 + h * 128
            for sp_, aa in ((0, a - 1), (1, a)):
                if aa < 0:
                    continue
                k2src = k1cm.bitcast(BF16)[4 + aa].rearrange("(r x) t -> x r t", r=4)[h * 128:(h + 1) * 128, :, :]
                P.dma("sp", "l" + k2k, k2[:, sp_, :].rearrange("d (r t) -> d r t", r=4), k2src, writes=[k2k])
            zt, zk = B["z"]
            P.dma("sp", "l" + zk, zt[:], zs1T[h * 128:(h + 1) * 128, a * 512:(a + 1) * 512], writes=[zk])
            vt, vk = B["v01"]
            hvt, hvk = B["hvh"]
            for g in range(2):
                P.dma("sp", "l" + vk, vt[:, g], v1[g, a][:, :, h * 128:(h + 1) * 128], writes=[vk])
                P.dma("sp", "l" + hvk, hvt[:, g], hv[a, g].rearrange("p (b c) -> p b c", b=4)[:, :, h * 128:(h + 1) * 128],
                      reads=[("D:hv", a)], writes=[hvk])
            v2, v2k = B["v2"]
            for sp_, aa in ((0, a - 1), (1, a)):
                if aa < 0:
                    continue
                for r in range(4):
                    for u in range(4):
                        src = v1cm.bitcast(BF16)[4 + aa, r * 128:(r + 1) * 128, :].rearrange(
                            "(i u) (b c) -> u i b c", u=4, b=4)[u][:, :, h * 128:(h + 1) * 128]
                        qq = "act" if (r + u) % 2 == 0 else "sp"
                        P.dma(qq, "l" + v2k + qq, v2[32 * r:32 * r + 32, sp_, u], src, writes=[v2k + qq])
            return B

        s4 = lambda t3, r1: t3.rearrange("p (i r) -> p r i", r=4)[:, r1, :]
        s16 = lambda t3, r2: t3.rearrange("p (i r) -> p r i", r=16)[:, r2, :]

        def pairs_of(B, a, h, ni, g1, g1k):
            qt, qk_ = B["q"]; kt, kk = B["k01"]; hk_, hkk = B["hk"]; k2, k2k = B["k2"]
            vt, vk = B["v01"]; hvt, hvk = B["hvh"]; v2, v2k = B["v2"]
            lst = []
            lst.append(dict(kind=0, nblk=4, w=128, lhs=lambda b: kt[:, 0, b * 128:(b + 1) * 128],
                            rhs=lambda b: qt[:, 0, b * 128:(b + 1) * 128], v=lambda b: vt[:, 0, b, :],
                            out=lambda t, b: t[:, b * 128:(b + 1) * 128], nf=0, sk=[kk, qk_], vkeys=[vk]))
            lst.append(dict(kind=1, nblk=4, w=128,
                            lhs=lambda b: (hk_[:, 0, 384:512] if b == 0 else kt[:, 0, (b - 1) * 128:b * 128]),
                            rhs=lambda b: qt[:, 0, b * 128:(b + 1) * 128],
                            v=lambda b: (hvt[:, 0, 3, :] if b == 0 else vt[:, 0, b - 1, :]),
                            out=lambda t, b: t[:, b * 128:(b + 1) * 128], nf=1, sk=[kk, hkk, qk_], vkeys=[vk, hvk]))
            lst.append(dict(kind=2, nblk=4, w=128, lhs=lambda b: s4(kt[:, 1, :], b), rhs=lambda b: s4(qt[:, 1, :], b),
                            v=lambda b: vt[:, 1, b, :], out=lambda t, b: s4(t[:], b), nf=0, sk=[kk, qk_], vkeys=[vk]))
            lst.append(dict(kind=3, nblk=4, w=128, lhs=lambda b: s4(hk_[:, 1, :], b), rhs=lambda b: s4(qt[:, 1, :], b),
                            v=lambda b: hvt[:, 1, b, :], out=lambda t, b: s4(t[:], b), nf=4, sk=[hkk, qk_], vkeys=[hvk]))
            for sp_ in ((1, 0) if a >= 1 else (1,)):
                lst.append(dict(kind=(4 if sp_ == 1 else 5), nblk=16, w=32, lhs=lambda b, sp_=sp_: s16(k2[:, sp_, :], b),
                                rhs=lambda b: s16(qt[:, 2, :], b), v=lambda b, sp_=sp_: v2[:, sp_, b // 4, b % 4, :],
                                out=lambda t, b: s16(t[:], b), nf=0, sk=[k2k, qk_], vkeys=[v2k + "act", v2k + "sp"]))
            for i, pr in enumerate(lst):
                pr.update(a=a, h=h, ni=ni, g1=g1, g1k=g1k, B=B, first=(i == 0), last=(i == len(lst) - 1))
                yield pr

        def emit_S(pr):
            ps, psk = T["pm"].next()
            pr["ps"], pr["psk"] = ps, psk
            w = pr["w"]
            for b in range(pr["nblk"]):
                P.pe(lambda e, ps=ps, b=b, pr=pr, w=w: e.matmul(ps[:, b * w:(b + 1) * w], lhsT=pr["lhs"](b), rhs=pr["rhs"](b),
                                                             start=True, stop=True), reads=pr["sk"], writes=[psk])

        def emit_rest(pr):
            ps, psk, w, nblk, a, h, ni = pr["ps"], pr["psk"], pr["w"], pr["nblk"], pr["a"], pr["h"], pr["ni"]
            if pr["first"] and ni + 1 < len(order) and ni >= 1:
                load_bundle((ni + 1) % 2, *order[ni + 1])
            pn, pnk = pN[ni % 2]
            pd, pdk = pD[ni % 2]
            ex, exk = exr.next()
            P.act(lambda e, ex=ex, ps=ps: e.activation(out=ex[:], in_=ps[:], func=AF.Exp, scale=SC), reads=[psk], writes=[exk])
            pt, ptk = ptr_.next()
            Eb = Et[:, h, pr["kind"], 0:w]
            nf = pr["nf"]
            if nf == 0:
                P.dve(lambda e, pt=pt, ex=ex: e.tensor_tensor(
                    out=pt[:].rearrange("p (b w) -> p b w", w=w), in0=ex[:].rearrange("p (b w) -> p b w", w=w),
                    in1=Eb.unsqueeze(1).broadcast_to([128, nblk, w]), op=ALU.mult), reads=[exk, "Et"], writes=[ptk])
            else:
                P.dve(lambda e, pt=pt, ex=ex: e.scalar_tensor_tensor(
                    out=pt[:, 0:nf * w].rearrange("p (b w) -> p b w", w=w),
                    in0=ex[:, 0:nf * w].rearrange("p (b w) -> p b w", w=w), scalar=flag[:, a:a + 1],
                    in1=Eb.unsqueeze(1).broadcast_to([128, nf, w]), op0=ALU.mult, op1=ALU.mult),
                    reads=[exk, "Et", "flag"], writes=[ptk])
                if nf < nblk:
                    P.dve(lambda e, pt=pt, ex=ex: e.tensor_tensor(
                        out=pt[:, nf * w:].rearrange("p (b w) -> p b w", w=w),
                        in0=ex[:, nf * w:].rearrange("p (b w) -> p b w", w=w),
                        in1=Eb.unsqueeze(1).broadcast_to([128, nblk - nf, w]), op=ALU.mult),
                        reads=[exk, "Et"], writes=[ptk])
            for b in range(nblk):
                st = pr["first"] and b == 0
                P.pe(lambda e, b=b, pt=pt, st=st, pr=pr: e.matmul(pr["out"](pn, b), lhsT=pr["v"](b), rhs=pt[:, b * w:(b + 1) * w],
                                                                 start=st, stop=False, skip_group_check=True),
                     reads=[ptk] + pr["vkeys"], writes=[pnk])
                P.pe(lambda e, b=b, pt=pt, st=st, pr=pr: e.matmul(pr["out"](pd, b), lhsT=ones[:], rhs=pt[:, b * w:(b + 1) * w],
                                                                 start=st, stop=False, skip_group_check=True),
                     reads=[ptk, "ones"], writes=[pdk])
            if pr["last"]:
                zt, zk = pr["B"]["z"]
                g1, g1k = pr["g1"], pr["g1k"]
                rD, rDk = rDr.next()
                P.dve(lambda e, rD=rD: e.reciprocal(out=rD[:], in_=pd[:]), reads=[pdk], writes=[rDk])
                tN, tNk = tNr.next()
                P.dve(lambda e, tN=tN, rD=rD: e.tensor_tensor(out=tN[:], in0=pn[:], in1=rD[:], op=ALU.mult),
                      reads=[pnk, rDk], writes=[tNk])
                P.pool(lambda e, tN=tN, g1=g1, zt=zt: e.tensor_tensor(out=g1[:, h, :], in0=tN[:], in1=zt[:], op=ALU.mult),
                       reads=[tNk, zk], writes=[g1k])

        hv2 = h2_d.rearrange("(lb p) d -> lb p d", p=128)
        pv = p1.rearrange("(lb p) d -> lb p d", p=128)
        yv = y_d.rearrange("(lb p) d -> lb p d", p=128)
        order = [(a, h) for a in range(4) for h in range(8)]
        DEPTH = 2
        bundles = {0: load_bundle(0, *order[0]), 1: load_bundle(1, *order[1])}
        load_epi_weights()
        pend = []
        for ni, (a, h) in enumerate(order):
            B = bund[ni % 2]
            if h == 0:
                g1, g1k = g1T.next()
                if a + 1 < 4:
                    halo_copy(a + 1)
            for pr in pairs_of(B, a, h, ni, g1, g1k):
                emit_S(pr)
                pend.append(pr)
                if len(pend) > DEPTH:
                    emit_rest(pend.pop(0))
            if h == 7:
                while pend:
                    emit_rest(pend.pop(0))
                def post_d(bl, hn, hnk, a=a):
                    lb = 4 * a + bl
                    P.act(lambda e, hn=hn, lb=lb: e.activation(out=sq[:], in_=hn[:], func=AF.Square, accum_out=ss[:, lb:lb + 1]),
                          reads=[hnk], writes=["sq", ("ss", lb)])
                    P.dve(lambda e, lb=lb: e.tensor_scalar(out=ms[:, lb:lb + 1], in0=ss[:, lb:lb + 1], scalar1=1.0 / D, scalar2=EPS,
                                                          op0=ALU.mult, op1=ALU.add), reads=[("ss", lb)], writes=[("ms", lb)])
                    P.act(lambda e, lb=lb: e.activation(out=ms[:, lb:lb + 1], in_=ms[:, lb:lb + 1], func=AF.Sqrt),
                          reads=[("ms", lb)], writes=[("ms", lb)])
                    P.dve(lambda e, lb=lb: e.reciprocal(out=rstd[:, lb:lb + 1], in_=ms[:, lb:lb + 1]),
                          reads=[("ms", lb)], writes=[("rstd", lb)])
                    y, yk = yb.next()
                    P.dve(lambda e, y=y, hn=hn, lb=lb: e.scalar_tensor_tensor(out=y[:], in0=hn[:], scalar=rstd[:, lb:lb + 1], in1=gfb[:],
                                                                             op0=ALU.mult, op1=ALU.mult),
                          reads=[hnk, ("rstd", lb), "gfb"], writes=[yk])
                    P.dma("sp", "sty", yv[lb], y[:], reads=[yk], writes=["D:y"])

                T["mix_keys"] = [g1k]
                T["pm"] = Rot(pS + pN + pD)
                epi_run(P, 4, None, (lambda bl, g1=g1: (lambda kc: g1[:, kc, bl * 128:(bl + 1) * 128])), T, W,
                        (lambda bl, a=a: hv2[4 * a + bl]), (lambda bl, a=a: pv[4 * a + bl]), None, post_d)
                T["pm"] = Rot(pS)
        P.flush()


def build_D_test():
    nc = bass.Bass("TRN2", target_bir_lowering=False)
    io = {}
    ei = lambda n, shp, dt: nc.dram_tensor(n, shp, dt, kind="ExternalInput").ap()
    eo = lambda n, shp, dt: nc.dram_tensor(n, shp, dt, kind="ExternalOutput").ap()
    it = lambda n, shp, dt: nc.dram_tensor(n, shp, dt).ap()
    io["k1Tg"] = ei("k1Tg", [4, 3072, NTL], BF16)
    io["v1g"] = ei("v1g", [4, 3, 128, NLB, D], BF16)
    io["k1T"] = ei("k1T", [3072, NTL], BF16)
    io["v1"] = ei("v1", [3, 128, NLB, D], BF16)
    io["q1T"] = ei("q1T", [3072, NTL], BF16)
    io["zs1T"] = ei("zs1T", [D, NTL], BF16)
    io["h2"] = ei("h2", [NTL, D], F32)
    io["p1"] = ei("p1", [NTL, 256], F32)
    io["dil_w_out"] = ei("dil_w_out", [D, D], F32)
    io["ple_w_up1"] = ei("ple_w_up1", [256, D], F32)
    io["ple_w_gate1"] = ei("ple_w_gate1", [D, D], F32)
    io["final_norm"] = ei("final_norm", [D], F32)
    io["Et"] = ei("Et", [128, 8, 6, 128], F32)
    io["flag"] = ei("flag", [128, 4], F32)
    io["hkT"] = it("hkT", [4, 2048, 512], BF16)
    io["hv"] = it("hv", [4, 2, 128, 4096], BF16)
    io["y"] = eo("y", [NTL, D], F32)
    P = Prog(nc)
    phase_D(nc, P, io)
    return nc


def _io_common(nc, ei):
    io = {}
    io["x"] = ei("x", [NTL, D], F32)
    io["p0"] = ei("p0", [NTL, 256], F32)
    io["p1"] = ei("p1", [NTL, 256], F32)
    io["fox_norm"] = ei("fox_norm", [D], F32)
    io["fox_w_in"] = ei("fox_w_in", [D, 4112], F32)
    io["fox_b_f"] = ei("fox_b_f", [16], F32)
    io["fox_w_out"] = ei("fox_w_out", [D, D], F32)
    io["dil_norm"] = ei("dil_norm", [D], F32)
    io["dil_w_in"] = ei("dil_w_in", [D, 10240], F32)
    io["dil_w_out"] = ei("dil_w_out", [D, D], F32)
    io["ple_w_up0"] = ei("ple_w_up0", [256, D], F32)
    io["ple_w_up1"] = ei("ple_w_up1", [256, D], F32)
    io["ple_w_gate0"] = ei("ple_w_gate0", [D, D], F32)
    io["ple_w_gate1"] = ei("ple_w_gate1", [D, D], F32)
    io["final_norm"] = ei("final_norm", [D], F32)
    io["maskT"] = ei("maskT", [128, 16, 512], BF16)
    io["tri"] = ei("tri", [128, 128], F32)
    io["ustrip"] = ei("ustrip", [128, 127], F32)
    io["Et"] = ei("Et", [128, 8, 6, 128], F32)
    io["flag"] = ei("flag", [128, 4], F32)
    return io


LOCAL0 = [("qT0", [1024, NTL], BF16), ("kT0", [1024, NTL], BF16), ("v0", [4, 128, 4, D], BF16),
          ("zs0", [128, NLB, D], BF16), ("lf0", [128, NLB, 16], F32)]
LOCAL1 = [("h2", [NTL, D], F32), ("q1T", [3072, NTL], BF16), ("k1T", [3, 4, D, 512], BF16),
          ("v1", [3, 4, 128, 4, D], BF16), ("zs1T", [D, NTL], BF16),
          ("k0tail", [D, 4, 128], BF16), ("v0tail", [128, 4, D], BF16)]
GATH0 = [("kT0g", [4, 1024, NTL], BF16), ("v0g", [4, 4, 128, 4, D], BF16), ("lf0g", [4, 128, NLB, 16], F32)]
GATH1 = [("k1Tg", [4, 3, 4, D, 512], BF16), ("v1g", [4, 3, 4, 128, 4, D], BF16),
         ("k0tailg", [4, D, 4, 128], BF16), ("v0tailg", [4, 128, 4, D], BF16)]
SCR = [("caug", [16, 6, S], BF16), ("caug_own", [16, 3, NTL], BF16), ("hkT", [4, 2048, 512], BF16),
       ("hv", [4, 2, 128, 4096], BF16)]


def build_stage(stage):
    nc = bass.Bass("TRN2", target_bir_lowering=False)
    used = {}

    def ei(n, shp, dt):
        return nc.dram_tensor(n, shp, dt, kind="ExternalInput").ap()

    eo = lambda n, shp, dt: nc.dram_tensor(n, shp, dt, kind="ExternalOutput").ap()
    it = lambda n, shp, dt: nc.dram_tensor(n, shp, dt).ap()
    need = {1: ["x", "fox_norm", "fox_w_in", "fox_b_f"],
            2: ["x", "p0", "fox_w_out", "ple_w_up0", "ple_w_gate0", "dil_norm", "dil_w_in", "maskT", "tri", "ustrip"],
            3: ["p1", "dil_w_out", "ple_w_up1", "ple_w_gate1", "final_norm", "Et", "flag"]}[stage]
    shapes = {}
    _io_common(None, lambda n, shp, dt: shapes.setdefault(n, (shp, dt)))
    io = {n: ei(n, *shapes[n]) for n in need}
    P = Prog(nc)
    if stage == 1:
        for n, shp, dt in LOCAL0:
            io[n] = eo(n, shp, dt)
        phase_A(nc, P, io)
    elif stage == 2:
        for n, shp, dt in GATH0:
            io[n] = ei(n, shp, dt)
        for n in ("qT0", "zs0"):
            io[n] = ei(n, *[(s_, d_) for (m_, s_, d_) in LOCAL0 if m_ == n][0])
        for n, shp, dt in SCR[:2]:
            io[n] = it(n, shp, dt)
        for n, shp, dt in LOCAL1:
            io[n] = eo(n, shp, dt)
        with ExitStack() as es2:
            gz = es2.enter_context(nc.sbuf_tensor("gz_sb", [128, NLB, D], BF16))
            Wpre = dict(wout=es2.enter_context(nc.sbuf_tensor("W0_out", [128, 8, D], BF16)),
                        wg=es2.enter_context(nc.sbuf_tensor("W0_g", [128, 8, D], BF16)),
                        wup=es2.enter_context(nc.sbuf_tensor("W0_up", [128, 2, D], BF16)))
            phase_B12(nc, P, io, gz, Wpre)
            phase_B3C(nc, P, io, gz, Wpre)
    else:
        for n, shp, dt in GATH1:
            io[n] = ei(n, shp, dt)
        for n in ("k1T", "v1", "q1T", "zs1T", "h2"):
            io[n] = ei(n, *[(s_, d_) for (m_, s_, d_) in LOCAL1 if m_ == n][0])
        for n, shp, dt in SCR[2:]:
            io[n] = it(n, shp, dt)
        io["y"] = eo("y", [NTL, D], F32)
        phase_D(nc, P, io)
    return nc, need


def own_tokens(j):
    return np.concatenate([np.arange(512 * (4 * a + j), 512 * (4 * a + j) + 512) for a in range(4)])


def host_inputs(inputs):
    x = np.asarray(inputs["x"], np.float32)
    p = np.asarray(inputs["p"], np.float32)
    maps = []
    for c in range(NCORES):
        b, j = c // 4, c % 4
        own = own_tokens(j)
        maskT, tri, us = consts_B(j)
        Et, flag = consts_D(j)
        m = {
            "x": np.ascontiguousarray(x[b][own]), "p0": np.ascontiguousarray(p[0, b][own]),
            "p1": np.ascontiguousarray(p[1, b][own]),
            "fox_norm": np.asarray(inputs["fox_norm"], np.float32)[0], "fox_w_in": np.asarray(inputs["fox_w_in"], np.float32)[0],
            "fox_b_f": np.asarray(inputs["fox_b_f"], np.float32)[0], "fox_w_out": np.asarray(inputs["fox_w_out"], np.float32)[0],
            "dil_norm": np.asarray(inputs["dil_norm"], np.float32)[0], "dil_w_in": np.asarray(inputs["dil_w_in"], np.float32)[0],
            "dil_w_out": np.asarray(inputs["dil_w_out"], np.float32)[0],
            "ple_w_up0": np.asarray(inputs["ple_w_up"], np.float32)[0], "ple_w_up1": np.asarray(inputs["ple_w_up"], np.float32)[1],
            "ple_w_gate0": np.asarray(inputs["ple_w_gate"], np.float32)[0], "ple_w_gate1": np.asarray(inputs["ple_w_gate"], np.float32)[1],
            "final_norm": np.asarray(inputs["final_norm"], np.float32),
            "maskT": maskT, "tri": tri, "ustrip": us, "Et": Et, "flag": flag,
        }
        maps.append(m)
    return maps


def assemble_output(ys):
    out = np.zeros((2, S, D), np.float32)
    for c in range(NCORES):
        b, j = c // 4, c % 4
        out[b, own_tokens(j)] = ys[c]
    return out


def kernel_unfused(**inputs):
    maps = host_inputs(inputs)
    cores = list(range(NCORES))
    nc1, need1 = build_stage(1)
    r1 = run_bass_kernel_spmd(nc1, [{k: m[k] for k in need1} for m in maps], core_ids=cores).results
    nc2, need2 = build_stage(2)
    in2 = []
    for c in range(NCORES):
        b = c // 4
        d2 = {k: maps[c][k] for k in need2}
        d2["kT0g"] = np.stack([r1[4 * b + r]["kT0"] for r in range(4)])
        d2["v0g"] = np.stack([r1[4 * b + r]["v0"] for r in range(4)])
        d2["lf0g"] = np.stack([r1[4 * b + r]["lf0"] for r in range(4)])
        d2["qT0"] = r1[c]["qT0"]
        d2["zs0"] = r1[c]["zs0"]
        in2.append(d2)
    r2 = run_bass_kernel_spmd(nc2, in2, core_ids=cores).results
    nc3, need3 = build_stage(3)
    in3 = []
    for c in range(NCORES):
        b = c // 4
        d3 = {k: maps[c][k] for k in need3}
        d3["k1Tg"] = np.stack([r2[4 * b + r]["k1T"] for r in range(4)])
        d3["v1g"] = np.stack([r2[4 * b + r]["v1"] for r in range(4)])
        d3["k0tailg"] = np.stack([r2[4 * b + r]["k0tail"] for r in range(4)])
        d3["v0tailg"] = np.stack([r2[4 * b + r]["v0tail"] for r in range(4)])
        for k in ("k1T", "v1", "q1T", "zs1T", "h2"):
            d3[k] = r2[c][k]
        in3.append(d3)
    r3 = run_bass_kernel_spmd(nc3, in3, core_ids=cores).results
    return assemble_output([r3[c]["y"] for c in range(NCORES)])


RG = [[0, 1, 2, 3], [4, 5, 6, 7]]


def _flat2(ap):
    n = len(ap.shape)
    if n == 2:
        return ap
    return ap.rearrange({3: "a b c -> (a b) c", 4: "a b c d -> (a b) (c d)", 5: "a b c d e -> (a b c) (d e)",
                         6: "a b c d e f -> (a b c d) (e f)"}[n])


CC_CHUNK_BYTES = 1 << 20
CC_INFLIGHT = 4


def gather_tensor(P, nc, loc, gat, tag, dep_keys, q_scatter, rng=None, cm_out=None):
    g2 = _flat2(gat)
    l2 = _flat2(loc) if len(loc.shape) != 3 else loc.rearrange("a b c -> a (b c)")
    if l2.dtype == BF16:
        l2, g2 = l2.bitcast(F32), g2.bitcast(F32)
    g3 = g2.rearrange("(r a) c -> r a c", r=4)
    if rng is not None:
        l2 = l2[rng[0]:rng[1], :]
        g3 = g3[:, rng[0]:rng[1], :]
    rows, cols = l2.shape
    n = max(1, min(rows, CC_CHUNK_BYTES // (cols * 4)))
    assert rows % n == 0
    nch = rows // n
    keys = []
    if nch == 1 and rng is None:
        P.op("pool", (lambda e: e.collective_compute("AllGather", ALU.bypass, replica_groups=RG,
                                                     ins=[l2.opt()], outs=[g2.opt()])),
             reads=list(dep_keys), writes=[("D:gath_" + tag, 0)], dma="cc", inc=1)
        P.pool(lambda e: e.engine_nop(), reads=[("D:gath_" + tag, 0)], writes=["cc_nop"])
        return None
    tmp = nc.dram_tensor("cc_tmp_" + tag, [nch, 4 * n, cols], F32).ap()
    if cm_out is not None:
        cm_out[tag] = tmp
        for c in range(nch):
            key = ("D:gath_" + tag, c)
            src = l2[c * n:(c + 1) * n, :]
            dst = tmp[c]
            P.op("pool", (lambda e, src=src, dst=dst: e.collective_compute("AllGather", ALU.bypass, replica_groups=RG,
                                                                           ins=[src.opt()], outs=[dst.opt()])),
                 reads=list(dep_keys), writes=[key], dma="cc", inc=1)
            keys.append(key)
            if c - CC_INFLIGHT + 1 >= 0:
                P.pool(lambda e: e.engine_nop(), reads=[keys[c - CC_INFLIGHT + 1]], writes=["cc_nop"])
        for c in range(max(0, nch - CC_INFLIGHT + 1), nch):
            P.pool(lambda e: e.engine_nop(), reads=[keys[c]], writes=["cc_nop"])
        return tmp

    def scatter(c):
        P.dma(q_scatter, "ccs_" + tag, g3[:, c * n:(c + 1) * n, :], tmp[c].rearrange("(r a) c -> r a c", r=4),
              reads=[keys[c]], writes=["D:gath_" + tag])
        if q_scatter != "pool":
            P.pool(lambda e: e.engine_nop(), reads=[keys[c]], writes=["cc_nop"])

    for c in range(nch):
        key = "D:cc_%s_%d" % (tag, c)
        src = l2[c * n:(c + 1) * n, :]
        dst = tmp[c]
        P.op("pool", (lambda e, src=src, dst=dst: e.collective_compute("AllGather", ALU.bypass, replica_groups=RG,
                                                                       ins=[src.opt()], outs=[dst.opt()])),
             reads=list(dep_keys), writes=[key], dma="cc", inc=1)
        keys.append(key)
        if c - CC_INFLIGHT + 1 >= 0:
            scatter(c - CC_INFLIGHT + 1)
    for c in range(max(0, nch - CC_INFLIGHT + 1), nch):
        scatter(c)


def all_gather(P, nc, pairs, dummy, mode="ag", tag=""):
    for i, (loc, gat, stg) in enumerate(pairs):
        gather_tensor(P, nc, loc, gat, "%s%d" % (tag, i), (), "sp")
    P.flush()


def build_fused(mode="ag"):
    nc = bass.Bass("TRN2", target_bir_lowering=False)
    ei = lambda n, shp, dt: nc.dram_tensor(n, shp, dt, kind="ExternalInput").ap()
    it = lambda n, shp, dt: nc.dram_tensor(n, shp, dt).ap()
    io = _io_common(nc, ei)
    for n, shp, dt in LOCAL0 + LOCAL1 + GATH0 + GATH1 + SCR:
        io[n] = it(n, shp, dt)
    io["y"] = nc.dram_tensor("y", [NTL, D], F32, kind="ExternalOutput").ap()
    P = Prog(nc)
    g0 = [("kT0", "kT0g"), ("v0", "v0g"), ("lf0", "lf0g")]
    g1 = [("k1T", "k1Tg"), ("v1", "v1g")]
    stg = {}
    with ExitStack() as es:
        dummy = es.enter_context(nc.sbuf_tensor("cc_dummy", [128, 8], F32))
        if mode == "ar":
            pid = nc.partition_id()
            jj = pid % 4
            zt = es.enter_context(nc.sbuf_tensor("cc_zero", [128, 8192], BF16))
            P.pool(lambda e: e.memset(zt[:], 0.0), writes=["zt"])
            for (ln, gn) in g0 + g1:
                shp = list(io[gn].shape)
                dt = F32 if ln == "lf0" else BF16
                stg[ln] = it("stg_" + ln, shp, dt)
                flat = _flat2(stg[ln])
                rows, cols = flat.shape
                if dt == F32:
                    zsrc = zt[:].bitcast(F32)[:, 0:cols]
                    for r0 in range(0, rows, 128):
                        P.dma("pool", "zf", flat[r0:r0 + 128, :], zsrc, reads=["zt"], writes=["D:stg" + ln])
                elif cols > 8192:
                    for r0 in range(0, rows, 128):
                        for c0 in range(0, cols, 8192):
                            P.dma("pool", "zf", flat[r0:r0 + 128, c0:c0 + 8192], zt[:], reads=["zt"], writes=["D:stg" + ln])
                else:
                    per = max(1, 8192 // cols)
                    for r0 in range(0, rows, 128 * per):
                        nb = min(per, (rows - r0) // 128)
                        P.dma("pool", "zf", flat[r0:r0 + 128 * nb, :].rearrange("(n p) c -> p n c", p=128),
                              zt[:, 0:nb * cols].rearrange("p (n c) -> p n c", c=cols), reads=["zt"], writes=["D:stg" + ln])

        def exchange(names):
            if mode == "ar":
                for (ln, gn) in names:
                    loc = io[ln]
                    n_el = 1
                    for d_ in loc.shape:
                        n_el *= d_
                    l2 = _flat2(loc) if len(loc.shape) != 3 else loc.rearrange("a b c -> a (b c)")
                    dst = bass.AP(tensor=stg[ln].tensor, offset=jj * n_el, ap=[[l2.shape[1], l2.shape[0]], [1, l2.shape[1]]])
                    P.dma("sp", "sx", dst, l2, writes=["D:stgw" + ln])
                P.flush()
            all_gather(P, nc, [(io[ln], io[gn], stg.get(ln)) for (ln, gn) in names], dummy, mode, tag=names[0][0])

        OVERLAP = (mode == "ag")
        gmap = dict(g0 + g1 + [("k0tail", "k0tailg"), ("v0tail", "v0tailg")])

        cm = {}
        io["cm"] = cm

        def gather_cb(ln, dep_keys, rng=None):
            loc_, gat_ = io[ln], io[gmap[ln]]
            if ln == "k1T":
                loc_ = loc_.rearrange("g a x t -> (g a x) t")
                gat_ = gat_.rearrange("r g a x t -> (r g a x) t")
            gather_tensor(P, nc, loc_, gat_, ln, dep_keys, "pool", rng,
                          cm_out=(cm if ln in ("kT0", "k1T", "v1", "v0") else None))

        phase_A(nc, P, io, gather_cb if OVERLAP else None)
        if not OVERLAP:
            exchange(g0)
        with ExitStack() as es2:
            gz = es2.enter_context(nc.sbuf_tensor("gz_sb", [128, NLB, D], BF16))
            Wpre = dict(wout=es2.enter_context(nc.sbuf_tensor("W0_out", [128, 8, D], BF16)),
                        wg=es2.enter_context(nc.sbuf_tensor("W0_g", [128, 8, D], BF16)),
                        wup=es2.enter_context(nc.sbuf_tensor("W0_up", [128, 2, D], BF16)))
            phase_B12(nc, P, io, gz, Wpre)
            phase_B3C(nc, P, io, gz, Wpre, gather_cb if OVERLAP else None)
        if not OVERLAP:
            exchange(g1)
        phase_D(nc, P, io)
    return nc, list(_io_common(None, lambda n, shp, dt: None).keys())


FUSED_MODE = "ag"
USE_FUSED = True


def kernel_fused(**inputs):
    maps = host_inputs(inputs)
    nc, need = build_fused(FUSED_MODE)
    res = run_bass_kernel_spmd(nc, [{k: m[k] for k in need} for m in maps], core_ids=list(range(NCORES))).results
    return assemble_output([res[c]["y"] for c in range(NCORES)])


def kernel(**inputs):
    return (kernel_fused if USE_FUSED else kernel_unfused)(**inputs)
```

```python
from contextlib import ExitStack
import numpy as np
import ml_dtypes
import concourse.bass as bass
import concourse.mybir as mybir
from concourse.bass_utils import run_bass_kernel_spmd

F32 = mybir.dt.float32
BF16 = mybir.dt.bfloat16
AF = mybir.ActivationFunctionType
ALU = mybir.AluOpType
AX = mybir.AxisListType

NCORES = 8
S = 8192
D = 1024
NTL = 2048
NLB = 16
EPS = 1e-6
NEG = -30000.0


class _Op:
    __slots__ = ("eng", "fn", "deps", "inc", "is_dma", "sem", "val", "idx", "grp")


class Prog:
    CE = ("pe", "act", "dve", "pool")

    def __init__(self, nc):
        self.nc = nc
        self.esem = {e: nc.alloc_semaphore("es_" + e) for e in self.CE}
        self.ecnt = {e: 0 for e in self.CE}
        self.waited = {}
        self.dsem = {}
        self.ops = []
        self.tw = {}
        self.tr = {}
        self.alias = {}
        self.marks = {}
        self.no_pool_cast = False
        self.n_total = 0

    @staticmethod
    def _need(x_eng, x_dma, w):
        return w.is_dma or x_dma or w.eng != x_eng

    def op(self, eng, fn, reads=(), writes=(), dma=None, inc=16):
        if self.alias:
            ex = lambda ks: [kk for k in ks for kk in self.alias.get(k, [k])]
            reads, writes = ex(reads), ex(writes)
        x = _Op()
        x.eng = eng
        x.fn = fn
        x.is_dma = dma is not None
        x.grp = dma
        x.inc = False
        x.idx = len(self.ops)
        deps = set()
        for r in reads:
            w = self.tw.get(r)
            if w is not None and (self._need(eng, x.is_dma, w) or eng != "pe"):
                deps.add(w)
        for k in writes:
            w = self.tw.get(k)
            if w is not None and self._need(eng, x.is_dma, w):
                if not (x.is_dma and w.is_dma and w.eng == eng and getattr(w, "grp", None) == dma):
                    deps.add(w)
            for r in self.tr.get(k, ()):
                if self._need(eng, x.is_dma, r):
                    deps.add(r)
        x.deps = sorted(deps, key=lambda o: o.idx)
        for r in reads:
            self.tr.setdefault(r, []).append(x)
        for k in writes:
            self.tw[k] = x
            self.tr[k] = []
        if x.is_dma:
            if dma not in self.dsem:
                self.dsem[dma] = [self.nc.alloc_semaphore("ds_" + str(dma)), 0]
            ent = self.dsem[dma]
            ent[1] += inc
            x.sem = ent[0]
            x.val = ent[1]
            x.inc = inc
        self.ops.append(x)
        return x

    def mark(self, group):
        k = (group, len(self.marks.setdefault(group, [])))
        self.marks[group].append(k)
        return k

    def pe(self, fn, reads=(), writes=()):
        return self.op("pe", fn, reads, writes)

    def act(self, fn, reads=(), writes=()):
        return self.op("act", fn, reads, writes)

    def dve(self, fn, reads=(), writes=()):
        return self.op("dve", fn, reads, writes)

    def pool(self, fn, reads=(), writes=()):
        return self.op("pool", fn, reads, writes)

    def dma(self, q, sem, out, in_, reads=(), writes=(), **kw):
        return self.op(q, lambda e: e.dma_start(out=out, in_=in_, **kw), reads, writes, dma=sem)

    def flush(self):
        nc = self.nc
        ops = self.ops
        for x in ops:
            for d in x.deps:
                if not d.is_dma:
                    d.inc = True
        for x in ops:
            if not x.is_dma:
                if x.inc:
                    self.ecnt[x.eng] += 1
                x.sem = self.esem[x.eng]
                x.val = self.ecnt[x.eng]
        per = {}
        for x in ops:
            per.setdefault(x.eng, []).append(x)
        tail = [(ent[0], ent[1]) for ent in self.dsem.values() if ent[1] > 0]
        waited = self.waited

        def emit(ename, eng, lst):
            for x in lst:
                for d in x.deps:
                    key = (ename, id(d.sem))
                    if waited.get(key, 0) < d.val:
                        eng.wait_ge(d.sem, d.val)
                        waited[key] = d.val
                ins = x.fn(eng)
                if x.inc:
                    ins.then_inc(x.sem, int(x.inc) if x.is_dma else 1)
            if ename == "sp":
                for s, v in tail:
                    key = (ename, id(s))
                    if waited.get(key, 0) < v:
                        eng.wait_ge(s, v)
                        waited[key] = v

        with nc.Block() as block:
            per.setdefault("sp", [])
            for ename, lst in per.items():
                f = (lambda en, l: (lambda eng: emit(en, eng, l)))(ename, lst)
                {"pe": block.tensor, "act": block.scalar, "dve": block.vector,
                 "pool": block.gpsimd, "sp": block.sync}[ename](f)
        self.n_total += len(ops)
        self.ops = []
        self.tw = {}
        self.tr = {}
        self.alias = {}


class Rot:
    def __init__(self, items):
        self.items = items
        self.i = 0

    def next(self):
        it = self.items[self.i % len(self.items)]
        self.i += 1
        return it


def make_ident(P, identf, ident):
    P.pool(lambda e: e.memset(identf[:], 1.0), writes=["identf"])
    P.pool(lambda e: e.affine_select(out=identf[:], in_=identf[:], pattern=[[-1, 128]],
                                     compare_op=ALU.is_equal, fill=0.0, base=0, channel_multiplier=1),
           reads=["identf"], writes=["identf"])
    P.dve(lambda e: e.tensor_copy(out=ident[:], in_=identf[:]), reads=["identf"], writes=["ident"])


def load_weight_bf(P, q, wst_rot, dst, dst_key, src_ap, nk, ncols, split=True):
    st, skey = wst_rot.next()
    P.dma(q, skey, st[:, 0:nk, 0:ncols], src_ap.rearrange("(kc kp) c -> kp kc c", kp=128), writes=[skey])
    if split and nk >= 4:
        k1 = (nk * 5) // 8 if not P.no_pool_cast else nk // 2
        if P.no_pool_cast:
            P.act(lambda e: e.copy(out=dst[:, 0:k1, 0:ncols], in_=st[:, 0:k1, 0:ncols]),
                  reads=[skey], writes=[(dst_key, "a")])
        else:
            P.pool(lambda e: e.tensor_copy(out=dst[:, 0:k1, 0:ncols], in_=st[:, 0:k1, 0:ncols]),
                   reads=[skey], writes=[(dst_key, "a")])
        P.dve(lambda e: e.tensor_copy(out=dst[:, k1:nk, 0:ncols], in_=st[:, k1:nk, 0:ncols]),
              reads=[skey], writes=[(dst_key, "b")])
        P.alias[dst_key] = [(dst_key, "a"), (dst_key, "b")]
    else:
        P.pool(lambda e: e.tensor_copy(out=dst[:, 0:nk, 0:ncols], in_=st[:, 0:nk, 0:ncols]),
               reads=[skey], writes=[dst_key])
        P.alias.pop(dst_key, None)


def rmsnorm_to_xnT(P, nc, xs, xs_key, gb, xnT, lb, T, ident):
    sq, ss, ms, rstd, xnb, ptr = T["sq"], T["ss"], T["ms"], T["rstd"], T["xnb"], T["ptr"]
    P.act(lambda e: e.activation(out=sq[:], in_=xs[:], func=AF.Square, accum_out=ss[:, lb:lb + 1]),
          reads=[xs_key], writes=["sq", ("ss", lb)])
    P.dve(lambda e: e.tensor_scalar(out=ms[:, lb:lb + 1], in0=ss[:, lb:lb + 1], scalar1=1.0 / D, scalar2=EPS,
                                    op0=ALU.mult, op1=ALU.add), reads=[("ss", lb)], writes=[("ms", lb)])
    P.act(lambda e: e.activation(out=ms[:, lb:lb + 1], in_=ms[:, lb:lb + 1], func=AF.Sqrt),
          reads=[("ms", lb)], writes=[("ms", lb)])
    P.dve(lambda e: e.reciprocal(out=rstd[:, lb:lb + 1], in_=ms[:, lb:lb + 1]),
          reads=[("ms", lb)], writes=[("rstd", lb)])
    xb, xbk = xnb.next()
    P.dve(lambda e: e.scalar_tensor_tensor(out=xb[:], in0=xs[:], scalar=rstd[:, lb:lb + 1], in1=gb[:],
                                           op0=ALU.mult, op1=ALU.mult),
          reads=[xs_key, ("rstd", lb), "gb"], writes=[xbk])
    pt, ptk = ptr.next()
    for kc in range(8):
        P.pe(lambda e, kc=kc: e.transpose(out=pt[:, kc, :], in_=xb[:, kc * 128:(kc + 1) * 128], identity=ident[:]),
             reads=[xbk, "ident"], writes=[ptk])
    P.act(lambda e: e.copy(out=xnT[:, :, lb * 128:(lb + 1) * 128], in_=pt[:]),
          reads=[ptk], writes=[("xnT", lb)])


def phase_A(nc, P, io, gather=None):
    x, gnorm, w_in, b_f = io["x"], io["fox_norm"], io["fox_w_in"], io["fox_b_f"]
    qT0, kT0, v0, zs0, lf0 = io["qT0"], io["kT0"], io["v0"], io["zs0"], io["lf0"]
    with ExitStack() as es:
        identf = es.enter_context(nc.sbuf_tensor("A_identf", [128, 128], F32))
        ident = es.enter_context(nc.sbuf_tensor("A_ident", [128, 128], BF16))
        gb = es.enter_context(nc.sbuf_tensor("A_gb", [128, D], F32))
        bfb = es.enter_context(nc.sbuf_tensor("A_bfb", [128, 16], F32))
        xs0 = es.enter_context(nc.sbuf_tensor("A_xs0", [128, D], F32))
        xs1 = es.enter_context(nc.sbuf_tensor("A_xs1", [128, D], F32))
        sq = es.enter_context(nc.sbuf_tensor("A_sq", [128, D], F32))
        ss = es.enter_context(nc.sbuf_tensor("A_ss", [128, NLB], F32))
        ms = es.enter_context(nc.sbuf_tensor("A_ms", [128, NLB], F32))
        rstd = es.enter_context(nc.sbuf_tensor("A_rstd", [128, NLB], F32))
        xnb0 = es.enter_context(nc.sbuf_tensor("A_xnb0", [128, D], BF16))
        xnb1 = es.enter_context(nc.sbuf_tensor("A_xnb1", [128, D], BF16))
        xnT = es.enter_context(nc.sbuf_tensor("A_xnT", [128, 8, NTL], BF16))
        wst0 = es.enter_context(nc.sbuf_tensor("A_wst0", [128, 8, 512], F32))
        wst1 = es.enter_context(nc.sbuf_tensor("A_wst1", [128, 8, 512], F32))
        wbf0 = es.enter_context(nc.sbuf_tensor("A_wbf0", [128, 8, 512], BF16))
        wbf1 = es.enter_context(nc.sbuf_tensor("A_wbf1", [128, 8, 512], BF16))
        wf = es.enter_context(nc.sbuf_tensor("A_wf", [128, 8, 16], BF16))
        vz0 = es.enter_context(nc.sbuf_tensor("A_vz0", [128, NLB, 512], BF16))
        vz1 = es.enter_context(nc.sbuf_tensor("A_vz1", [128, NLB, 512], BF16))
        qk0 = es.enter_context(nc.sbuf_tensor("A_qk0", [128, 512], BF16))
        qk1 = es.enter_context(nc.sbuf_tensor("A_qk1", [128, 512], BF16))
        qk2 = es.enter_context(nc.sbuf_tensor("A_qk2", [128, 512], BF16))
        qk3 = es.enter_context(nc.sbuf_tensor("A_qk3", [128, 512], BF16))
        ft = es.enter_context(nc.sbuf_tensor("A_ft", [128, NLB, 16], F32))
        lf = es.enter_context(nc.sbuf_tensor("A_lf", [128, NLB, 16], F32))
        ptr0 = es.enter_context(nc.psum_tensor("A_ptr0", [128, 8, 128], BF16))
        ptr1 = es.enter_context(nc.psum_tensor("A_ptr1", [128, 8, 128], BF16))
        pm0 = es.enter_context(nc.psum_tensor("A_pm0", [128, 512], F32))
        pm1 = es.enter_context(nc.psum_tensor("A_pm1", [128, 512], F32))
        pm2 = es.enter_context(nc.psum_tensor("A_pm2", [128, 512], F32))
        pm3 = es.enter_context(nc.psum_tensor("A_pm3", [128, 512], F32))
        pf = es.enter_context(nc.psum_tensor("A_pf", [128, NLB, 16], F32))
        make_ident(P, identf, ident)
        P.dma("sp", "c0", gb[:], gnorm.partition_broadcast(128), writes=["gb"])
        P.dma("sp", "c0", bfb[:], b_f.partition_broadcast(128), writes=["bfb"])
        T = dict(sq=sq, ss=ss, ms=ms, rstd=rstd,
                 xnb=Rot([(xnb0, "xnb0"), (xnb1, "xnb1")]),
                 ptr=Rot([(ptr0, "ptr0"), (ptr1, "ptr1")]))
        xs_rot = Rot([(xs0, "xs0"), (xs1, "xs1")])
        wst = Rot([(wst0, "wst0"), (wst1, "wst1")])
        wbf = Rot([(wbf0, "wbf0"), (wbf1, "wbf1")])
        pm = Rot([(pm0, "pm0"), (pm1, "pm1"), (pm2, "pm2"), (pm3, "pm3")])
        qk = Rot([(qk0, "qk0"), (qk1, "qk1"), (qk2, "qk2"), (qk3, "qk3")])
        vz = Rot([(vz0, "vz0"), (vz1, "vz1")])
        xv = x.rearrange("(lb p) d -> lb p d", p=128)
        load_weight_bf(P, "sp", wst, wf, "wf", w_in[:, 4096:4112], 8, 16)
        chunks = [("k", 0), ("k", 1), ("v", 0), ("v", 1), ("q", 0), ("q", 1), ("z", 0), ("z", 1)]
        P.no_pool_cast = gather is not None
        col0 = {"q": 0, "k": 1024, "v": 2048, "z": 3072}
        wcur = []

        def issue_w(ci):
            kind, hh = chunks[ci]
            wb, wbk = wbf.next()
            c0 = col0[kind] + hh * 512
            load_weight_bf(P, "sp", wst, wb, wbk, w_in[:, c0:c0 + 512], 8, 512)
            wcur.append((wb, wbk))

        issue_w(0)
        for lb in range(NLB):
            xs, xsk = xs_rot.next()
            P.dma("sp", xsk, xs[:], xv[lb], writes=[xsk])
            rmsnorm_to_xnT(P, nc, xs, xsk, gb, xnT, lb, T, ident)
        allx = [("xnT", lb) for lb in range(NLB)]
        for lb in range(NLB):
            for kc in range(8):
                P.pe(lambda e, lb=lb, kc=kc: e.matmul(pf[:, lb, :], lhsT=xnT[:, kc, lb * 128:(lb + 1) * 128],
                                                      rhs=wf[:, kc, :], start=(kc == 0), stop=(kc == 7)),
                     reads=[("xnT", lb), "wf"], writes=["pf"])
        P.dve(lambda e: e.tensor_tensor(out=ft[:], in0=pf[:], in1=bfb[:].unsqueeze(1).broadcast_to([128, NLB, 16]),
                                        op=ALU.add), reads=["pf", "bfb"], writes=["ft"])
        P.act(lambda e: e.activation(out=ft[:], in_=ft[:], func=AF.Exp, scale=-1.0), reads=["ft"], writes=["ft"])
        P.act(lambda e: e.activation(out=ft[:], in_=ft[:], func=AF.Ln, bias=1.0), reads=["ft"], writes=["ft"])
        P.dve(lambda e: e.tensor_scalar(out=lf[:], in0=ft[:], scalar1=-1.0, scalar2=None, op0=ALU.mult),
              reads=["ft"], writes=["lf"])
        P.dma("sp", "stl", lf0, lf[:], reads=["lf"], writes=[P.mark("A:lf")])
        if gather is not None:
            gather("lf0", P.marks["A:lf"])
        ev = 0
        for ci, (kind, hh) in enumerate(chunks):
            if ci + 1 < len(chunks):
                issue_w(ci + 1)
            wb, wbk = wcur[ci]
            if kind in ("q", "k"):
                dstT = qT0 if kind == "q" else kT0
                for sl in range(4):
                    for tt in range(4):
                        ps, psk = pm.next()
                        for kc in range(8):
                            P.pe(lambda e, ps=ps, kc=kc, sl=sl, tt=tt, wb=wb: e.matmul(
                                ps[:], lhsT=wb[:, kc, sl * 128:(sl + 1) * 128], rhs=xnT[:, kc, tt * 512:(tt + 1) * 512],
                                start=(kc == 0), stop=(kc == 7)),
                                reads=[wbk] + allx[tt * 4:tt * 4 + 4], writes=[psk])
                        sb, sbk = qk.next()
                        sc = 0.125 if kind == "q" else 1.0
                        if ev % 2 == 0:
                            P.act(lambda e, sb=sb, ps=ps, sc=sc: e.activation(out=sb[:], in_=ps[:], func=AF.Copy, scale=sc),
                                  reads=[psk], writes=[sbk])
                        else:
                            P.dve(lambda e, sb=sb, ps=ps, sc=sc: e.tensor_scalar(out=sb[:], in0=ps[:], scalar1=sc, scalar2=None,
                                                                               op0=ALU.mult), reads=[psk], writes=[sbk])
                        ev += 1
                        r0 = hh * 512 + sl * 128
                        P.dma("sp", "st" + sbk, dstT[r0:r0 + 128, tt * 512:(tt + 1) * 512], sb[:], reads=[sbk],
                              writes=[P.mark("A:" + kind)])
            else:
                vs, vsk = vz.next()
                for lb in range(NLB):
                    ps, psk = pm.next()
                    for kc in range(8):
                        P.pe(lambda e, ps=ps, kc=kc, lb=lb, wb=wb: e.matmul(
                            ps[:], lhsT=xnT[:, kc, lb * 128:(lb + 1) * 128], rhs=wb[:, kc, :],
                            start=(kc == 0), stop=(kc == 7)), reads=[wbk, ("xnT", lb)], writes=[psk])
                    if kind == "v":
                        P.dve(lambda e, ps=ps, lb=lb, vs=vs: e.tensor_copy(out=vs[:, lb, :], in_=ps[:]),
                              reads=[psk], writes=[vsk])
                    else:
                        P.act(lambda e, ps=ps, lb=lb, vs=vs: e.activation(out=vs[:, lb, :], in_=ps[:], func=AF.Silu),
                              reads=[psk], writes=[vsk])
                if kind == "v":
                    for a4 in range(4):
                        P.dma("sp", "st" + vsk, v0[a4][:, :, hh * 512:(hh + 1) * 512], vs[:, 4 * a4:4 * a4 + 4, :], reads=[vsk],
                              writes=[P.mark("A:v")])
                else:
                    P.dma("sp", "st" + vsk, zs0[:, :, hh * 512:(hh + 1) * 512], vs[:], reads=[vsk], writes=[P.mark("A:" + kind)])
            if gather is not None and (kind, hh) == ("k", 1):
                gather("kT0", P.marks["A:k"])
            if gather is not None and (kind, hh) == ("v", 1):
                gather("v0", P.marks["A:v"])
        P.flush()


def own_tokens(j):
    return np.concatenate([np.arange(512 * (4 * a + j), 512 * (4 * a + j) + 512) for a in range(4)])


def build_A():
    nc = bass.Bass("TRN2", target_bir_lowering=False)
    io = {}
    io["x"] = nc.dram_tensor("x", [NTL, D], F32, kind="ExternalInput").ap()
    io["fox_norm"] = nc.dram_tensor("fox_norm", [D], F32, kind="ExternalInput").ap()
    io["fox_w_in"] = nc.dram_tensor("fox_w_in", [D, 4112], F32, kind="ExternalInput").ap()
    io["fox_b_f"] = nc.dram_tensor("fox_b_f", [16], F32, kind="ExternalInput").ap()
    io["qT0"] = nc.dram_tensor("qT0", [1024, NTL], BF16, kind="ExternalOutput").ap()
    io["kT0"] = nc.dram_tensor("kT0", [1024, NTL], BF16, kind="ExternalOutput").ap()
    io["v0"] = nc.dram_tensor("v0", [128, NLB, D], BF16, kind="ExternalOutput").ap()
    io["zs0"] = nc.dram_tensor("zs0", [128, NLB, D], BF16, kind="ExternalOutput").ap()
    io["lf0"] = nc.dram_tensor("lf0", [128, NLB, 16], F32, kind="ExternalOutput").ap()
    P = Prog(nc)
    phase_A(nc, P, io)
    return nc


def phase_B12(nc, P, io, gz, Wpre=None):
    kT0g, v0g, lf0g, qT0, zs0 = io["kT0g"], io["v0g"], io["lf0g"], io["qT0"], io["zs0"]
    caug, maskT_d, tri_d, ustrip_d = io["caug"], io["maskT"], io["tri"], io["ustrip"]
    caug_own = io["caug_own"]
    v0cm = io.get("cm", {}).get("v0")
    pid = nc.partition_id()
    jj = pid % 4
    with ExitStack() as es:
        lfg = es.enter_context(nc.sbuf_tensor("B1_lfg", [128, 64, 16], F32))
        tri = es.enter_context(nc.sbuf_tensor("B1_tri", [128, 128], F32))
        us = es.enter_context(nc.sbuf_tensor("B1_us", [128, 127], F32))
        carry = es.enter_context(nc.sbuf_tensor("B1_carry", [16, 64], F32))
        cc = es.enter_context(nc.sbuf_tensor("B1_cc", [16, 2048], F32))
        t1 = es.enter_context(nc.sbuf_tensor("B1_t1", [16, 2048], F32))
        t2 = es.enter_context(nc.sbuf_tensor("B1_t2", [16, 2048], F32))
        aug0 = es.enter_context(nc.sbuf_tensor("B1_aug0", [16, 6, 2048], BF16))
        aug1 = es.enter_context(nc.sbuf_tensor("B1_aug1", [16, 6, 2048], BF16))
        pc = es.enter_context(nc.psum_tensor("B1_pc", [16, 64], F32))
        pcs0 = es.enter_context(nc.psum_tensor("B1_pcs0", [16, 512], F32))
        pcs1 = es.enter_context(nc.psum_tensor("B1_pcs1", [16, 512], F32))
        P.dma("sp", "b1c", tri[:], tri_d, writes=["tri"])
        P.dma("sp", "b1c", us[:], ustrip_d, writes=["us"])
        P.dma("sp", "b1z", gz[:], zs0, writes=["gz"])
        lfv = lfg[:].rearrange("p (a r bl) h -> p a r bl h", a=4, r=4, bl=4)
        for r in range(4):
            P.dma("sp", "b1l", lfv[:, :, r], lf0g[r].rearrange("p (a bl) h -> p a bl h", bl=4), writes=["lfg"])
        for G in range(64):
            P.pe(lambda e, G=G: e.matmul(pc[:], lhsT=lfg[:, G, :], rhs=us[:, 63 - G:127 - G],
                                         start=(G == 0), stop=(G == 63)), reads=["lfg", "us"], writes=["pc"])
        P.dve(lambda e: e.tensor_copy(out=carry[:], in_=pc[:]), reads=["pc"], writes=["carry"])
        pcs = Rot([(pcs0, "pcs0"), (pcs1, "pcs1")])
        augr = Rot([(aug0, "aug0"), (aug1, "aug1")])
        for ch in range(4):
            for g4 in range(4):
                ps, psk = pcs.next()
                for bq in range(4):
                    G = 16 * ch + 4 * g4 + bq
                    P.pe(lambda e, ps=ps, bq=bq, G=G: e.matmul(ps[:, bq * 128:(bq + 1) * 128], lhsT=lfg[:, G, :], rhs=tri[:],
                                                              start=True, stop=True), reads=["lfg", "tri"], writes=[psk])
                for bq in range(4):
                    G = 16 * ch + 4 * g4 + bq
                    c0 = (4 * g4 + bq) * 128
                    P.dve(lambda e, ps=ps, bq=bq, G=G, c0=c0: e.tensor_scalar(
                        out=cc[:, c0:c0 + 128], in0=ps[:, bq * 128:(bq + 1) * 128], scalar1=carry[:, G:G + 1], scalar2=None,
                        op0=ALU.add), reads=[psk, "carry"], writes=["cc"])
            ag, agk = augr.next()
            P.dve(lambda e, ag=ag: e.tensor_copy(out=ag[:, 0, :], in_=cc[:]), reads=["cc"], writes=[agk])
            P.dve(lambda e, ag=ag: e.tensor_tensor(out=t1[:], in0=cc[:], in1=ag[:, 0, :], op=ALU.subtract),
                  reads=["cc", agk], writes=["t1"])
            P.dve(lambda e, ag=ag: e.tensor_copy(out=ag[:, 1, :], in_=t1[:]), reads=["t1"], writes=[agk])
            P.dve(lambda e, ag=ag: e.tensor_tensor(out=t2[:], in0=t1[:], in1=ag[:, 1, :], op=ALU.subtract),
                  reads=["t1", agk], writes=["t2"])
            P.dve(lambda e, ag=ag: e.tensor_copy(out=ag[:, 2, :], in_=t2[:]), reads=["t2"], writes=[agk])
            P.dve(lambda e, ag=ag: e.tensor_scalar(out=ag[:, 3:6, :], in0=ag[:, 0:3, :], scalar1=-1.0, scalar2=None,
                                                   op0=ALU.mult), reads=[agk], writes=[agk])
            P.dma("sp", "b1s", caug[:, :, ch * 2048:(ch + 1) * 2048], ag[:], reads=[agk], writes=["D:caug"])
        for a in range(4):
            P.dma("pool", "b1o", caug_own[:, :, a * 512:(a + 1) * 512], caug[:, 0:3, bass.ds(jj * 512 + 2048 * a, 512)],
                  reads=["D:caug"], writes=["D:caug_own"])
        P.flush()
    with ExitStack() as es:
        identf = es.enter_context(nc.sbuf_tensor("B2_identf", [128, 128], F32))
        ident = es.enter_context(nc.sbuf_tensor("B2_ident", [128, 128], BF16))
        maskT = es.enter_context(nc.sbuf_tensor("B2_mask", [128, 16, 512], BF16))
        kTa = es.enter_context(nc.sbuf_tensor("B2_kT0", [70, S], BF16))
        kTb = es.enter_context(nc.sbuf_tensor("B2_kT1", [70, S], BF16))
        qTa = es.enter_context(nc.sbuf_tensor("B2_qT0", [70, NTL], BF16))
        qTb = es.enter_context(nc.sbuf_tensor("B2_qT1", [70, NTL], BF16))
        vraw = es.enter_context(nc.sbuf_tensor("B2_vraw", [128, 64, 128], BF16))
        vA0 = es.enter_context(nc.sbuf_tensor("B2_vA0", [128, 64, 2, 65], BF16))
        vA1 = es.enter_context(nc.sbuf_tensor("B2_vA1", [128, 64, 2, 65], BF16))
        pT0 = es.enter_context(nc.sbuf_tensor("B2_pT0", [128, 2, 512], BF16))
        pT1 = es.enter_context(nc.sbuf_tensor("B2_pT1", [128, 2, 512], BF16))
        pT2 = es.enter_context(nc.sbuf_tensor("B2_pT2", [128, 2, 512], BF16))
        rc = es.enter_context(nc.sbuf_tensor("B2_rc", [128, 8], F32))
        pS0 = es.enter_context(nc.psum_tensor("B2_pS0", [128, 2, 512], F32))
        pS1 = es.enter_context(nc.psum_tensor("B2_pS1", [128, 2, 512], F32))
        pS2 = es.enter_context(nc.psum_tensor("B2_pS2", [128, 2, 512], F32))
        pO0 = es.enter_context(nc.psum_tensor("B2_pO0", [128, 4, 128], F32))
        pO1 = es.enter_context(nc.psum_tensor("B2_pO1", [128, 4, 128], F32))
        make_ident(P, identf, ident)
        P.dma("sp", "b2c", maskT[:], maskT_d, writes=["mask"])
        for kt, ktk in ((kTa, "kT0"), (kTb, "kT1")):
            P.pool(lambda e, kt=kt: e.memset(kt[64:67, :], 1.0), writes=[ktk])
        for qt, qtk in ((qTa, "qT0"), (qTb, "qT1")):
            P.pool(lambda e, qt=qt: e.memset(qt[64:70, :], 1.0), writes=[qtk])
        for va, vak in ((vA0, "vA0"), (vA1, "vA1")):
            P.pool(lambda e, va=va: e.memset(va[:, :, :, 64:65], 1.0), writes=[vak])
        kTr = Rot([(kTa, "kT0"), (kTb, "kT1")])
        qTr = Rot([(qTa, "qT0"), (qTb, "qT1")])
        pSr = Rot([(pS0, "pS0"), (pS1, "pS1"), (pS2, "pS2")])
        pTr = Rot([(pT0, "pT0"), (pT1, "pT1"), (pT2, "pT2")])
        pOr = Rot([(pO0, "pO0"), (pO1, "pO1")])
        rci = [0]

        def load_head(h):
            kt, ktk = kTr.next()
            qt, qtk = qTr.next()
            ktv = kt[0:64, :].rearrange("d (a r t) -> d a r t", a=4, r=4)
            kcm = io.get("cm", {}).get("kT0")
            for r in range(4):
                if kcm is None:
                    ksrc_ = kT0g[r, h * 64:(h + 1) * 64, :]
                else:
                    ksrc_ = kcm.bitcast(BF16)[h // 4, r * 256 + (h % 4) * 64:r * 256 + (h % 4) * 64 + 64, :]
                P.dma("sp", "ld" + ktk, ktv[:, :, r], ksrc_.rearrange("d (a t) -> d a t", a=4), writes=[ktk])
            P.dma("sp", "ld" + ktk, kt[67:70, :], caug[h, 3:6, :], reads=["D:caug"], writes=[ktk])
            P.dma("sp", "ld" + qtk, qt[0:64, :], qT0[h * 64:(h + 1) * 64, :], writes=[qtk])
            P.dma("sp", "ld" + qtk, qt[64:67, :], caug_own[h], reads=["D:caug_own"], writes=[qtk])
            return kt, ktk, qt, qtk

        vbufs = [(vA0, "vA0"), (vA1, "vA1")]

        def load_vpair(hp):
            va, vak = vbufs[hp % 2]
            vrv = vraw[:].rearrange("p (a r bl) c -> p a r bl c", a=4, r=4, bl=4)
            for r in range(4):
                for a in range(4):
                    vsrc_ = v0cm.bitcast(BF16)[a, r * 128:(r + 1) * 128, :].rearrange("p (bl c) -> p bl c", bl=4)[
                        :, :, hp * 128:(hp + 1) * 128]
                    P.dma("sp", "ldvr", vrv[:, a, r], vsrc_, writes=["vraw"])
            P.pool(lambda e, va=va: e.tensor_copy(out=va[:, :, :, 0:64], in_=vraw[:].rearrange("p g (hh c) -> p g hh c", hh=2)),
                   reads=["vraw"], writes=[vak])

        def prefetch_epilogue_weights():
            wst_ = es.enter_context(nc.sbuf_tensor("B2_wst", [128, 8, 512], F32))
            for (wsrc, wdst, key, nk) in ((io["fox_w_out"], Wpre["wout"], "wout", 8), (io["ple_w_gate0"], Wpre["wg"], "wg", 8),
                                          (io["ple_w_up0"], Wpre["wup"], "wup", 2)):
                for n in range(2):
                    P.dma("sp", "b2wst", wst_[:, 0:nk, :], wsrc[:, n * 512:(n + 1) * 512].rearrange("(kc kp) c -> kp kc c", kp=128),
                          writes=["b2wst"])
                    P.pool(lambda e, wdst=wdst, n=n, nk=nk: e.tensor_copy(out=wdst[:, :, n * 512:(n + 1) * 512], in_=wst_[:, 0:nk, :]),
                           reads=["b2wst"], writes=[key])

        DEPTH = 2
        state = {}

        def gen():
            nxt_head = load_head(0)
            load_vpair(0)
            if Wpre is not None:
                prefetch_epilogue_weights()
            for h in range(16):
                hp, hh = h // 2, h % 2
                kt, ktk, qt, qtk = nxt_head
                va, vak = vbufs[hp % 2]
                if h + 1 < 16:
                    nxt_head = load_head(h + 1)
                for a in range(4):
                    nJ = 16 * a + 16
                    po, pok = pOr.next()
                    for J2 in range(nJ // 2):
                        yield dict(h=h, hh=hh, a=a, J2=J2, nJ=nJ, kt=kt, ktk=ktk, qt=qt, qtk=qtk, va=va, vak=vak, po=po, pok=pok)

        def emit_S(st):
            ps, psk = pSr.next()
            st["ps"], st["psk"] = ps, psk
            a, kt, qt = st["a"], st["kt"], st["qt"]
            for u in range(2):
                J = 2 * st["J2"] + u
                masked = J >= 16 * a
                P.pe(lambda e, ps=ps, kt=kt, qt=qt, J=J, a=a, masked=masked, u=u: e.matmul(
                    ps[:, u, :], lhsT=kt[0:70, J * 128:(J + 1) * 128], rhs=qt[0:70, a * 512:(a + 1) * 512],
                    start=True, stop=(not masked)), reads=[st["ktk"], st["qtk"]], writes=[psk])
                if masked:
                    P.pe(lambda e, ps=ps, J=J, a=a, u=u: e.matmul(ps[:, u, :], lhsT=ident[:], rhs=maskT[:, J - 16 * a, :],
                                                                start=False, stop=True), reads=["ident", "mask"], writes=[psk])

        def emit_rest(st):
            ps, psk, po, pok, va, vak = st["ps"], st["psk"], st["po"], st["pok"], st["va"], st["vak"]
            h, hh, a, nJ = st["h"], st["hh"], st["a"], st["nJ"]
            if hh == 0 and a == 0 and st["J2"] == 0 and h // 2 + 1 < 8:
                load_vpair(h // 2 + 1)
            pt, ptk = pTr.next()
            P.act(lambda e, pt=pt, ps=ps: e.activation(out=pt[:], in_=ps[:], func=AF.Exp), reads=[psk], writes=[ptk])
            for u in range(2):
                J = 2 * st["J2"] + u
                for qb in range(4):
                    P.pe(lambda e, po=po, pt=pt, va=va, qb=qb, J=J, hh=hh, nJ=nJ, u=u: e.matmul(
                        po[:, qb, 0:65], lhsT=pt[:, u, qb * 128:(qb + 1) * 128], rhs=va[:, J, hh, :],
                        start=(J == 0 and qb == 0), stop=(J == nJ - 1), skip_group_check=True), reads=[ptk, vak], writes=[pok])
            if st["J2"] == nJ // 2 - 1:
                for qb in range(4):
                    lb = 4 * a + qb
                    ri = rci[0] % 8
                    rci[0] += 1
                    P.dve(lambda e, po=po, qb=qb, ri=ri: e.reciprocal(out=rc[:, ri:ri + 1], in_=po[:, qb, 64:65]),
                          reads=[pok], writes=[("rc", ri)])
                    P.dve(lambda e, po=po, qb=qb, ri=ri, lb=lb, h=h: e.scalar_tensor_tensor(
                        out=gz[:, lb, h * 64:(h + 1) * 64], in0=po[:, qb, 0:64], scalar=rc[:, ri:ri + 1],
                        in1=gz[:, lb, h * 64:(h + 1) * 64], op0=ALU.mult, op1=ALU.mult),
                        reads=[pok, ("rc", ri), "gz"], writes=["gz"])

        pend = []
        for st in gen():
            emit_S(st)
            pend.append(st)
            if len(pend) > DEPTH:
                emit_rest(pend.pop(0))
        while pend:
            emit_rest(pend.pop(0))
        P.flush()


def consts_B(j):
    k = np.arange(128)[:, None, None]
    Jr = np.arange(16)[None, :, None]
    q = np.arange(512)[None, None, :]
    maskT = np.where(128 * Jr + k <= 512 * j + q, 0.0, NEG).astype(ml_dtypes.bfloat16)
    s = np.arange(128)[:, None]
    t = np.arange(128)[None, :]
    tri = (s <= t).astype(np.float32)
    us = np.broadcast_to((np.arange(127) > 63).astype(np.float32)[None, :], (128, 127)).copy()
    return maskT, tri, us


def build_B12_test(dbg=False):
    nc = bass.Bass("TRN2", target_bir_lowering=False)
    io = {}
    if dbg:
        io["dbg_pt"] = nc.dram_tensor("dbg_pt", [2, 128, 512], BF16, kind="ExternalOutput").ap()
        io["dbg_ps"] = nc.dram_tensor("dbg_ps", [2, 128, 512], F32, kind="ExternalOutput").ap()
        io["dbg_po"] = nc.dram_tensor("dbg_po", [128, 512], F32, kind="ExternalOutput").ap()
    io["kT0g"] = nc.dram_tensor("kT0g", [4, 1024, NTL], BF16, kind="ExternalInput").ap()
    io["v0g"] = nc.dram_tensor("v0g", [4, 128, NLB, D], BF16, kind="ExternalInput").ap()
    io["lf0g"] = nc.dram_tensor("lf0g", [4, 128, NLB, 16], F32, kind="ExternalInput").ap()
    io["qT0"] = nc.dram_tensor("qT0", [1024, NTL], BF16, kind="ExternalInput").ap()
    io["zs0"] = nc.dram_tensor("zs0", [128, NLB, D], BF16, kind="ExternalInput").ap()
    io["maskT"] = nc.dram_tensor("maskT", [128, 16, 512], BF16, kind="ExternalInput").ap()
    io["tri"] = nc.dram_tensor("tri", [128, 128], F32, kind="ExternalInput").ap()
    io["ustrip"] = nc.dram_tensor("ustrip", [128, 127], F32, kind="ExternalInput").ap()
    io["caug"] = nc.dram_tensor("caug", [16, 6, S], BF16, kind="ExternalOutput").ap()
    io["caug_own"] = nc.dram_tensor("caug_own", [16, 3, NTL], BF16, kind="ExternalOutput").ap()
    gz_d = nc.dram_tensor("gz", [128, NLB, D], BF16, kind="ExternalOutput").ap()
    P = Prog(nc)
    with nc.sbuf_tensor("gz_sb", [128, NLB, D], BF16) as gz:
        phase_B12(nc, P, io, gz)
        P.dma("sp", "gzst", gz_d, gz[:], reads=["gz"])
        P.flush()
    return nc


def epilogue_block(P, nc, lb, mixT, mixT_keys, T, W, hres_src, p_src, h_out_dst, nk_mix):
    pm, ptr, ident = T["pm"], T["ptr"], T["ident"]
    xs, xsk = T["xs"].next()
    P.dma("sp", "ld" + xsk, xs[:], hres_src, writes=[xsk])
    pb, pbk = T["pb"].next()
    P.dma("sp", "ld" + pbk, pb[:], p_src, writes=[pbk])
    h1, h1k = T["h1"].next()
    for n in range(2):
        ps, psk = pm.next()
        for kc in range(nk_mix):
            P.pe(lambda e, ps=ps, kc=kc, n=n: e.matmul(ps[:], lhsT=mixT(kc), rhs=W["wout"][:, kc, n * 512:(n + 1) * 512],
                                                     start=(kc == 0), stop=(kc == nk_mix - 1)),
                 reads=list(mixT_keys) + ["wout"], writes=[psk])
        P.dve(lambda e, ps=ps, n=n, h1=h1, xs=xs: e.tensor_tensor(out=h1[:, n * 512:(n + 1) * 512], in0=ps[:],
                                                                in1=xs[:, n * 512:(n + 1) * 512], op=ALU.add),
              reads=[psk, xsk], writes=[h1k])
    hb, hbk = T["hb"].next()
    P.act(lambda e, hb=hb, h1=h1: e.copy(out=hb[:], in_=h1[:]), reads=[h1k], writes=[hbk])
    pt, ptk = ptr.next()
    for kc in range(8):
        P.pe(lambda e, kc=kc, pt=pt, hb=hb: e.transpose(out=pt[:, kc, :], in_=hb[:, kc * 128:(kc + 1) * 128], identity=ident[:]),
             reads=[hbk, "ident"], writes=[ptk])
    hT, hTk = T["hT"].next()
    P.dve(lambda e, hT=hT, pt=pt: e.tensor_copy(out=hT[:], in_=pt[:]), reads=[ptk], writes=[hTk])
    pbb, pbbk = T["pbb"].next()
    P.dve(lambda e, pbb=pbb, pb=pb: e.tensor_copy(out=pbb[:], in_=pb[:]), reads=[pbk], writes=[pbbk])
    pt2, pt2k = ptr.next()
    for k2 in range(2):
        P.pe(lambda e, k2=k2, pt2=pt2, pbb=pbb: e.transpose(out=pt2[:, k2, :], in_=pbb[:, k2 * 128:(k2 + 1) * 128], identity=ident[:]),
             reads=[pbbk, "ident"], writes=[pt2k])
    pT, pTk = T["pT"].next()
    P.dve(lambda e, pT=pT, pt2=pt2: e.tensor_copy(out=pT[:], in_=pt2[:, 0:2, :]), reads=[pt2k], writes=[pTk])
    gate, gk = T["gate"].next()
    for n in range(2):
        ps, psk = pm.next()
        for kc in range(8):
            P.pe(lambda e, ps=ps, kc=kc, n=n, hT=hT: e.matmul(ps[:], lhsT=hT[:, kc, :], rhs=W["wg"][:, kc, n * 512:(n + 1) * 512],
                                                            start=(kc == 0), stop=(kc == 7)), reads=[hTk, "wg"], writes=[psk])
        P.act(lambda e, ps=ps, n=n, gate=gate: e.activation(out=gate[:, n * 512:(n + 1) * 512], in_=ps[:], func=AF.Sigmoid),
              reads=[psk], writes=[gk])
    hn, hnk = T["hn"].next()
    for n in range(2):
        ps, psk = pm.next()
        for k2 in range(2):
            P.pe(lambda e, ps=ps, k2=k2, n=n, pT=pT: e.matmul(ps[:], lhsT=pT[:, k2, :], rhs=W["wup"][:, k2, n * 512:(n + 1) * 512],
                                                            start=(k2 == 0), stop=(k2 == 1)), reads=[pTk, "wup"], writes=[psk])
        P.dve(lambda e, ps=ps, n=n, gate=gate: e.tensor_tensor(out=gate[:, n * 512:(n + 1) * 512], in0=ps[:],
                                                             in1=gate[:, n * 512:(n + 1) * 512], op=ALU.mult),
              reads=[psk, gk], writes=[gk])
    P.dve(lambda e, hn=hn, gate=gate, h1=h1: e.tensor_tensor(out=hn[:], in0=gate[:], in1=h1[:], op=ALU.add),
          reads=[gk, h1k], writes=[hnk])
    if h_out_dst is not None:
        P.dma("sp", "st" + hnk, h_out_dst, hn[:], reads=[hnk], writes=["D:hout"])
    return hn, hnk


def epi_run(P, nblocks, pre, mixT_of, T, W, hres_of, p_of, hout_of, post, nk_mix=8):
    pm, ptr, ident = T["pm"], T["ptr"], T["ident"]
    ctx = {}

    def s_pre(i):
        c = ctx[i] = {}
        c["mk"] = pre(i) if pre is not None else list(T.get("mix_keys", []))
        c["xs"], c["xsk"] = T["xs"].next()
        P.dma("sp", "ld" + c["xsk"], c["xs"][:], hres_of(i), writes=[c["xsk"]])
        c["pb"], c["pbk"] = T["pb"].next()
        P.dma("sp", "ld" + c["pbk"], c["pb"][:], p_of(i), writes=[c["pbk"]])
        c["pbb"], c["pbbk"] = T["pbb"].next()
        P.dve(lambda e, c=c: e.tensor_copy(out=c["pbb"][:], in_=c["pb"][:]), reads=[c["pbk"]], writes=[c["pbbk"]])

    def s1(i):
        c = ctx[i]
        mixT = mixT_of(i)
        c["h1"], c["h1k"] = T["h1"].next()
        for n in range(2):
            ps, psk = pm.next()
            for kc in range(nk_mix):
                P.pe(lambda e, ps=ps, kc=kc, n=n: e.matmul(ps[:], lhsT=mixT(kc), rhs=W["wout"][:, kc, n * 512:(n + 1) * 512],
                                                         start=(kc == 0), stop=(kc == nk_mix - 1)),
                     reads=list(c["mk"]) + ["wout"], writes=[psk])
            P.dve(lambda e, ps=ps, n=n, c=c: e.tensor_tensor(out=c["h1"][:, n * 512:(n + 1) * 512], in0=ps[:],
                                                            in1=c["xs"][:, n * 512:(n + 1) * 512], op=ALU.add),
                  reads=[psk, c["xsk"]], writes=[c["h1k"]])
        c["hb"], c["hbk"] = T["hb"].next()
        P.act(lambda e, c=c: e.copy(out=c["hb"][:], in_=c["h1"][:]), reads=[c["h1k"]], writes=[c["hbk"]])

    def s2a(i):
        c = ctx[i]
        pt, ptk = ptr.next()
        for kc in range(8):
            P.pe(lambda e, kc=kc, pt=pt, c=c: e.transpose(out=pt[:, kc, :], in_=c["hb"][:, kc * 128:(kc + 1) * 128], identity=ident[:]),
                 reads=[c["hbk"], "ident"], writes=[ptk])
        c["hT"], c["hTk"] = T["hT"].next()
        P.dve(lambda e, c=c, pt=pt: e.tensor_copy(out=c["hT"][:], in_=pt[:]), reads=[ptk], writes=[c["hTk"]])
        pt2, pt2k = ptr.next()
        for k2 in range(2):
            P.pe(lambda e, k2=k2, pt2=pt2, c=c: e.transpose(out=pt2[:, k2, :], in_=c["pbb"][:, k2 * 128:(k2 + 1) * 128], identity=ident[:]),
                 reads=[c["pbbk"], "ident"], writes=[pt2k])
        c["pT"], c["pTk"] = T["pT"].next()
        P.act(lambda e, c=c, pt2=pt2: e.copy(out=c["pT"][:], in_=pt2[:, 0:2, :]), reads=[pt2k], writes=[c["pTk"]])

    def s2b(i):
        c = ctx[i]
        gate, gk = T["gate"].next()
        for n in range(2):
            ps, psk = pm.next()
            for kc in range(8):
                P.pe(lambda e, ps=ps, kc=kc, n=n, c=c: e.matmul(ps[:], lhsT=c["hT"][:, kc, :], rhs=W["wg"][:, kc, n * 512:(n + 1) * 512],
                                                              start=(kc == 0), stop=(kc == 7)), reads=[c["hTk"], "wg"], writes=[psk])
            P.act(lambda e, ps=ps, n=n, gate=gate: e.activation(out=gate[:, n * 512:(n + 1) * 512], in_=ps[:], func=AF.Sigmoid),
                  reads=[psk], writes=[gk])
        hn, hnk = T["hn"].next()
        for n in range(2):
            ps, psk = pm.next()
            for k2 in range(2):
                P.pe(lambda e, ps=ps, k2=k2, n=n, c=c: e.matmul(ps[:], lhsT=c["pT"][:, k2, :], rhs=W["wup"][:, k2, n * 512:(n + 1) * 512],
                                                              start=(k2 == 0), stop=(k2 == 1)), reads=[c["pTk"], "wup"], writes=[psk])
            P.dve(lambda e, ps=ps, n=n, gate=gate: e.tensor_tensor(out=gate[:, n * 512:(n + 1) * 512], in0=ps[:],
                                                                 in1=gate[:, n * 512:(n + 1) * 512], op=ALU.mult),
                  reads=[psk, gk], writes=[gk])
        P.dve(lambda e, hn=hn, gate=gate, c=c: e.tensor_tensor(out=hn[:], in0=gate[:], in1=c["h1"][:], op=ALU.add),
              reads=[gk, c["h1k"]], writes=[hnk])
        dst = hout_of(i) if hout_of is not None else None
        if dst is not None:
            P.dma("sp", "st" + hnk, dst, hn[:], reads=[hnk], writes=["D:hout"])
        c["hn"], c["hnk"] = hn, hnk

    for i in range(nblocks + 2):
        if i < nblocks:
            s_pre(i)
        if 0 <= i - 1 < nblocks:
            s2a(i - 1)
        if i < nblocks:
            s1(i)
        if 0 <= i - 1 < nblocks:
            s2b(i - 1)
        if 0 <= i - 2 < nblocks:
            c = ctx.pop(i - 2)
            post(i - 2, c["hn"], c["hnk"])


def alloc_epilogue(nc, es, pfx):
    sb = lambda n, shp, dt: es.enter_context(nc.sbuf_tensor(pfx + n, shp, dt))
    ps = lambda n, shp, dt: es.enter_context(nc.psum_tensor(pfx + n, shp, dt))
    T = {}
    T["identf"] = sb("identf", [128, 128], F32)
    T["ident_t"] = sb("ident", [128, 128], BF16)
    T["ident"] = T["ident_t"]
    mk = lambda n, shp, dt, k: Rot([(sb(f"{n}{i}", shp, dt), f"{pfx}{n}{i}") for i in range(k)])
    T["xs"] = mk("xs", [128, D], F32, 2)
    T["pb"] = mk("pb", [128, 256], F32, 2)
    T["pbb"] = mk("pbb", [128, 256], BF16, 2)
    T["h1"] = mk("h1", [128, D], F32, 2)
    T["hb"] = mk("hb", [128, D], BF16, 2)
    T["hT"] = mk("hT", [128, 8, 128], BF16, 1)
    T["pT"] = mk("pT", [128, 2, 128], BF16, 1)
    T["gate"] = mk("gate", [128, D], F32, 1)
    T["hn"] = mk("hn", [128, D], F32, 3)
    T["ptr"] = Rot([(ps(f"ptr{i}", [128, 8, 128], BF16), f"{pfx}ptr{i}") for i in range(3)])
    T["pm"] = Rot([(ps(f"pm{i}", [128, 512], F32), f"{pfx}pm{i}") for i in range(5)])
    T["sq"] = sb("sq", [128, D], BF16)
    T["ss"] = sb("ss", [128, NLB], F32)
    T["ms"] = sb("ms", [128, NLB], F32)
    T["rstd"] = sb("rstd", [128, NLB], F32)
    T["xnb"] = mk("xnb", [128, D], BF16, 2)
    return T


def phase_B3C(nc, P, io, gz, Wpre=None, gather=None):
    x, p0 = io["x"], io["p0"]
    w_out, w_up, w_gate, g1n, w_in1 = io["fox_w_out"], io["ple_w_up0"], io["ple_w_gate0"], io["dil_norm"], io["dil_w_in"]
    h2_d, q1T, k1T, v1, zs1T = io["h2"], io["q1T"], io["k1T"], io["v1"], io["zs1T"]
    k0tail, v0tail = io["k0tail"], io["v0tail"]
    with ExitStack() as es:
        sb = lambda n, shp, dt: es.enter_context(nc.sbuf_tensor("C_" + n, shp, dt))
        T = alloc_epilogue(nc, es, "C_")
        gb = sb("gb", [128, D], F32)
        if Wpre is None:
            wout = sb("wout", [128, 8, D], BF16)
            wg = sb("wg", [128, 8, D], BF16)
            wup = sb("wup", [128, 2, D], BF16)
        else:
            wout, wg, wup = Wpre["wout"], Wpre["wg"], Wpre["wup"]
        wst = Rot([(sb(f"wst{i}", [128, 8, 512], F32), f"C_wst{i}") for i in range(1)])
        wbf = Rot([(sb(f"wbf{i}", [128, 8, 512], BF16), f"C_wbf{i}") for i in range(2)])
        gT = Rot([(sb(f"gT{i}", [128, 8, 128], BF16), f"C_gT{i}") for i in range(2)])
        xn1T = sb("xn1T", [128, 8, NTL], BF16)
        qk = Rot([(sb(f"qk{i}", [128, 512], BF16), f"C_qk{i}") for i in range(4)])
        W = dict(wout=wout, wg=wg, wup=wup)
        make_ident(P, T["identf"], T["ident_t"])
        P.dma("sp", "c1", gb[:], g1n.partition_broadcast(128), writes=["gb"])
        for n in range(2 if Wpre is None else 0):
            st, stk = wst.next()
            P.dma("sp", stk, st[:], w_out[:, n * 512:(n + 1) * 512].rearrange("(kc kp) c -> kp kc c", kp=128), writes=[stk])
            P.pool(lambda e, st=st, n=n: e.tensor_copy(out=wout[:, :, n * 512:(n + 1) * 512], in_=st[:]), reads=[stk], writes=["wout"])
        for n in range(2 if Wpre is None else 0):
            st, stk = wst.next()
            P.dma("sp", stk, st[:], w_gate[:, n * 512:(n + 1) * 512].rearrange("(kc kp) c -> kp kc c", kp=128), writes=[stk])
            P.pool(lambda e, st=st, n=n: e.tensor_copy(out=wg[:, :, n * 512:(n + 1) * 512], in_=st[:]), reads=[stk], writes=["wg"])
        for n in range(2 if Wpre is None else 0):
            st, stk = wst.next()
            P.dma("sp", stk, st[:, 0:2, :], w_up[:, n * 512:(n + 1) * 512].rearrange("(kc kp) c -> kp kc c", kp=128), writes=[stk])
            P.pool(lambda e, st=st, n=n: e.tensor_copy(out=wup[:, :, n * 512:(n + 1) * 512], in_=st[:, 0:2, :]), reads=[stk], writes=["wup"])
        wcur = []
        chunks = [("k", i) for i in range(6)] + [("v", i) for i in range(6)] + [("q", i) for i in range(6)] + [("z", i) for i in range(2)]
        P.no_pool_cast = gather is not None
        col0 = {"q": 0, "k": 3072, "v": 6144, "z": 9216}

        def issue_w(ci):
            kind, i = chunks[ci]
            wb, wbk = wbf.next()
            c0 = col0[kind] + i * 512
            load_weight_bf(P, "sp", wst, wb, wbk, w_in1[:, c0:c0 + 512], 8, 512)
            wcur.append((wb, wbk))

        xv = x.rearrange("(lb p) d -> lb p d", p=128)
        pv = p0.rearrange("(lb p) d -> lb p d", p=128)
        hv = h2_d.rearrange("(lb p) d -> lb p d", p=128)
        gcur = {}

        def pre_c(lb):
            pt, ptk = T["ptr"].next()
            for kc in range(8):
                P.pe(lambda e, kc=kc, pt=pt, lb=lb: e.transpose(out=pt[:, kc, :], in_=gz[:, lb, kc * 128:(kc + 1) * 128],
                                                              identity=T["ident"][:]), reads=["gz", "ident"], writes=[ptk])
            g, gk = gT.next()
            P.act(lambda e, g=g, pt=pt: e.copy(out=g[:], in_=pt[:]), reads=[ptk], writes=[gk])
            gcur[lb] = g
            return [gk]

        def post_c(lb, hn, hnk):
            rmsnorm_to_xnT(P, nc, hn, hnk, gb, xn1T, lb, T, T["ident"])
            if lb == 8:
                issue_w(0)

        epi_run(P, NLB, pre_c, (lambda lb: (lambda kc: gcur[lb][:, kc, :])), T, W,
                (lambda lb: xv[lb]), (lambda lb: pv[lb]), (lambda lb: hv[lb]), post_c)
        allx = [("xnT", lb) for lb in range(NLB)]
        pm = T["pm"]
        ev = 0
        for ci, (kind, i) in enumerate(chunks):
            if ci + 1 < len(chunks):
                issue_w(ci + 1)
            wb, wbk = wcur[ci]
            if kind in ("q", "k", "z"):
                g = i // 2 if kind != "z" else 0
                dstT = {"q": q1T, "k": k1T, "z": zs1T}[kind]
                R = 1
                for sl in range(4):
                    for a in range(4):
                        ps, psk = pm.next()
                        for kc in range(8):
                            P.pe(lambda e, ps=ps, kc=kc, sl=sl, a=a, wb=wb: e.matmul(
                                ps[:], lhsT=wb[:, kc, sl * 128:(sl + 1) * 128], rhs=xn1T[:, kc, a * 512:(a + 1) * 512],
                                start=(kc == 0), stop=(kc == 7)), reads=[wbk] + allx[a * 4:a * 4 + 4], writes=[psk])
                        sbt, sbk = qk.next()
                        if R == 1:
                            o_ap, i_ap = sbt[:], ps[:]
                        else:
                            o_ap = sbt[:].rearrange("d (r i) -> d r i", r=R)
                            i_ap = ps[:].rearrange("d (i r) -> d r i", r=R)
                        if kind == "z":
                            P.act(lambda e, o_ap=o_ap, i_ap=i_ap: e.activation(out=o_ap, in_=i_ap, func=AF.Silu), reads=[psk], writes=[sbk])
                        elif ev % 2 == 0:
                            P.act(lambda e, o_ap=o_ap, i_ap=i_ap: e.copy(out=o_ap, in_=i_ap), reads=[psk], writes=[sbk])
                        else:
                            P.dve(lambda e, o_ap=o_ap, i_ap=i_ap: e.tensor_copy(out=o_ap, in_=i_ap), reads=[psk], writes=[sbk])
                        ev += 1
                        r0 = i * 512 + sl * 128
                        if kind == "k":
                            rk = (i % 2) * 512 + sl * 128
                            P.dma("sp", "st" + sbk, k1T[g, a][rk:rk + 128, :], sbt[:], reads=[sbk], writes=[P.mark("C:k")])
                        else:
                            P.dma("sp", "st" + sbk, dstT[r0:r0 + 128, a * 512:(a + 1) * 512], sbt[:], reads=[sbk], writes=[P.mark("C:" + kind)])
                        if kind == "k" and g == 0:
                            P.dma("sp", "st" + sbk, k0tail[r0:r0 + 128, a, :], sbt[:, 384:512], reads=[sbk], writes=[P.mark("C:kt")])
            else:
                g = i // 2
                hh = i % 2
                for a in range(4):
                    for b4 in range(4):
                        ps, psk = pm.next()
                        for kc in range(8):
                            if g == 0:
                                lt = xn1T[:, kc, a * 512 + b4 * 128:a * 512 + b4 * 128 + 128]
                            else:
                                lt = xn1T[:, kc, a * 512:(a + 1) * 512].rearrange("k (i r) -> k r i", r=4)[:, b4, :]
                            P.pe(lambda e, ps=ps, kc=kc, lt=lt, wb=wb: e.matmul(ps[:], lhsT=lt, rhs=wb[:, kc, :],
                                                                             start=(kc == 0), stop=(kc == 7)),
                                 reads=[wbk] + allx[a * 4:a * 4 + 4], writes=[psk])
                        sbt, sbk = qk.next()
                        if ev % 2 == 0:
                            P.act(lambda e, ps=ps, sbt=sbt: e.copy(out=sbt[:], in_=ps[:]), reads=[psk], writes=[sbk])
                        else:
                            P.dve(lambda e, ps=ps, sbt=sbt: e.tensor_copy(out=sbt[:], in_=ps[:]), reads=[psk], writes=[sbk])
                        ev += 1
                        P.dma("sp", "st" + sbk, v1[g, a][:, b4, hh * 512:(hh + 1) * 512], sbt[:], reads=[sbk], writes=[P.mark("C:v")])
                        if g == 0 and b4 == 3:
                            P.dma("sp", "st" + sbk, v0tail[:, a, hh * 512:(hh + 1) * 512], sbt[:], reads=[sbk], writes=[P.mark("C:vt")])
            if gather is not None and (kind, i) == ("k", 5):
                gather("k1T", P.marks["C:k"], (4096, 12288))
                gather("k0tail", P.marks["C:kt"])
            if gather is not None and (kind, i) == ("v", 5):
                gather("v1", P.marks["C:v"], (512, 1536))
                gather("v0tail", P.marks["C:vt"])
        P.flush()


def build_B3C_test():
    nc = bass.Bass("TRN2", target_bir_lowering=False)
    io = {}
    ei = lambda n, shp, dt: nc.dram_tensor(n, shp, dt, kind="ExternalInput").ap()
    eo = lambda n, shp, dt: nc.dram_tensor(n, shp, dt, kind="ExternalOutput").ap()
    io["x"] = ei("x", [NTL, D], F32)
    io["p0"] = ei("p0", [NTL, 256], F32)
    io["fox_w_out"] = ei("fox_w_out", [D, D], F32)
    io["ple_w_up0"] = ei("ple_w_up0", [256, D], F32)
    io["ple_w_gate0"] = ei("ple_w_gate0", [D, D], F32)
    io["dil_norm"] = ei("dil_norm", [D], F32)
    io["dil_w_in"] = ei("dil_w_in", [D, 10240], F32)
    gz_d = ei("gz", [128, NLB, D], BF16)
    io["h2"] = eo("h2", [NTL, D], F32)
    io["q1T"] = eo("q1T", [3072, NTL], BF16)
    io["k1T"] = eo("k1T", [3072, NTL], BF16)
    io["v1"] = eo("v1", [3, 128, NLB, D], BF16)
    io["zs1T"] = eo("zs1T", [D, NTL], BF16)
    P = Prog(nc)
    with nc.sbuf_tensor("gz_sb", [128, NLB, D], BF16) as gz:
        P.dma("sp", "gzld", gz[:], gz_d, writes=["gz"])
        phase_B3C(nc, P, io, gz)
    return nc


def consts_D(j):
    i = np.arange(24, dtype=np.float64)
    slopes = (2.0 ** (-8.0 * (i + 1) / 24)).reshape(3, 8)
    k = np.arange(128, dtype=np.float64)[:, None]
    q = np.arange(128, dtype=np.float64)[None, :]
    Et = np.zeros((128, 8, 6, 128), np.float64)
    for h in range(8):
        for g, dil in ((0, 1.0), (1, 4.0)):
            s = slopes[g, h] * dil
            Et[:, h, 2 * g, :] = np.where(k <= q, np.exp(-s * (q - k)), 0.0)
            Et[:, h, 2 * g + 1, :] = np.where(k >= q, np.exp(-s * (128 + q - k)), 0.0)
        s = slopes[2, h] * 16.0
        iq = 32 * j + np.arange(32, dtype=np.float64)[None, :]
        Et[:, h, 4, 0:32] = np.where(k <= iq, np.exp(-s * (iq - k)), 0.0)
        Et[:, h, 5, 0:32] = np.where(k >= iq, np.exp(-s * (128 + iq - k)), 0.0)
    flag = np.ones((128, 4), np.float32)
    if j == 0:
        flag[:, 0] = 0.0
    return Et.astype(np.float32), flag


def phase_D(nc, P, io):
    k1Tg, v1g, k1T, v1, q1T, zs1T = io["k1Tg"], io["v1g"], io["k1T"], io["v1"], io["q1T"], io["zs1T"]
    h2_d, p1, y_d = io["h2"], io["p1"], io["y"]
    w_out, w_up, w_gate, gfin = io["dil_w_out"], io["ple_w_up1"], io["ple_w_gate1"], io["final_norm"]
    Et_d, flag_d, hkT, hv = io["Et"], io["flag"], io["hkT"], io["hv"]
    k0tg, v0tg = io["k0tailg"], io["v0tailg"]
    k1cm = io.get("cm", {}).get("k1T")
    v1cm = io.get("cm", {}).get("v1")
    SC = float(128 ** -0.5)
    pid = nc.partition_id()
    jj = pid % 4
    with ExitStack() as es:
        sb = lambda n, shp, dt: es.enter_context(nc.sbuf_tensor("D_" + n, shp, dt))
        psm = lambda n, shp, dt: es.enter_context(nc.psum_tensor("D_" + n, shp, dt))
        rr = (jj + 3) % 4

        def halo_copy(a):
            ap_ = ((a - 1) + (jj + 3) // 4) if a >= 1 else 0
            if True:
                kb = k1cm.bitcast(BF16)
                koff = ap_ * (4 * D * 512) + rr * (D * 512)
                ksrc = bass.AP(tensor=kb.tensor, offset=koff, ap=[[512, 1024], [1, 512]])
                P.dma("sp", "dhk", hkT[a, 1024:2048, :], ksrc, writes=[("D:hkT", a)])
            else:
                kb = k1cm.bitcast(BF16)
                for c4 in range(4):
                    koff = (c4 * 1024 + rr * 256) * NTL + ap_ * 512
                    ksrc = bass.AP(tensor=kb.tensor, offset=koff, ap=[[NTL, 256], [1, 512]])
                    P.dma("sp", "dhk", hkT[a, 1024 + 256 * c4:1024 + 256 * (c4 + 1), :], ksrc, writes=[("D:hkT", a)])
            ktoff = rr * (D * 512) + ap_ * 128
            ktsrc = bass.AP(tensor=k0tg.tensor, offset=ktoff, ap=[[512, 1024], [1, 128]])
            P.dma("act", "dhkt", hkT[a, 0:1024, 384:512], ktsrc, writes=[("D:hkTt", a)])
            vb = v1cm.bitcast(BF16)
            voff = ap_ * (512 * 4 * D) + rr * (128 * 4 * D)
            vsrc = bass.AP(tensor=vb.tensor, offset=voff, ap=[[4 * D, 128], [1, 4 * D]])
            P.dma("pool", "dhv", hv[a, 1], vsrc, writes=[("D:hv", a)])
            vtoff = rr * (128 * 4 * D) + ap_ * D
            vtsrc = bass.AP(tensor=v0tg.tensor, offset=vtoff, ap=[[4 * D, 128], [1, D]])
            P.dma("pool", "dhv", hv[a, 0][:, 3 * D:4 * D], vtsrc, writes=[("D:hv", a)])

        halo_copy(0)
        T = {}
        T["identf"] = sb("identf", [128, 128], F32)
        T["ident"] = sb("ident", [128, 128], BF16)
        mk = lambda n, shp, dt, k: Rot([(sb(f"{n}{i}", shp, dt), f"D_{n}{i}") for i in range(k)])
        T["xs"] = mk("xs", [128, D], F32, 2)
        T["pb"] = mk("pb", [128, 256], F32, 2)
        T["pbb"] = mk("pbb", [128, 256], BF16, 2)
        T["h1"] = mk("h1", [128, D], F32, 2)
        T["hb"] = mk("hb", [128, D], BF16, 2)
        T["hT"] = mk("hT", [128, 8, 128], BF16, 1)
        T["pT"] = mk("pT", [128, 2, 128], BF16, 1)
        T["gate"] = mk("gate", [128, D], F32, 1)
        T["hn"] = mk("hn", [128, D], F32, 3)
        T["ptr"] = Rot([(psm("ptr0", [128, 8, 128], BF16), "D_ptr0")])
        pS = [(psm(f"pS{i}", [128, 512], F32), f"D_pS{i}") for i in range(3)]
        pN = [(psm(f"pN{i}", [128, 512], F32), f"D_pN{i}") for i in range(2)]
        pD = [(psm(f"pD{i}", [128, 512], F32), f"D_pD{i}") for i in range(2)]
        T["pm"] = Rot(pS)
        sq = sb("sq", [128, D], BF16)
        ss = sb("ss", [128, NLB], F32)
        ms = sb("ms", [128, NLB], F32)
        rstd = sb("rstd", [128, NLB], F32)
        yb = mk("yb", [128, D], F32, 2)
        gfb = sb("gfb", [128, D], F32)
        wout = sb("wout", [128, 8, D], BF16)
        wg = sb("wg", [128, 8, D], BF16)
        wup = sb("wup", [128, 2, D], BF16)
        wst = sb("wst", [128, 8, 256], F32)
        Et = sb("Et", [128, 8, 6, 128], F32)
        flag = sb("flag", [128, 4], F32)
        ones = sb("ones", [128, 128], BF16)
        g1T = Rot([(sb(f"g1T{i}", [128, 8, 512], BF16), f"D_g1T{i}") for i in range(1)])
        exr = Rot([(sb(f"ex{i}", [128, 512], F32), f"D_ex{i}") for i in range(3)])
        ptr_ = Rot([(sb(f"pt{i}", [128, 512], BF16), f"D_pt{i}") for i in range(3)])
        rDr = Rot([(sb(f"rD{i}", [128, 512], F32), f"D_rD{i}") for i in range(1)])
        tNr = Rot([(sb(f"tN{i}", [128, 512], F32), f"D_tN{i}") for i in range(1)])
        bund = []
        for i in range(2):
            bund.append(dict(
                q=(sb(f"bq{i}", [128, 3, 512], BF16), f"D_bq{i}"),
                k01=(sb(f"bk{i}", [128, 2, 512], BF16), f"D_bk{i}"),
                hk=(sb(f"bhk{i}", [128, 2, 512], BF16), f"D_bhk{i}"),
                k2=(sb(f"bk2{i}", [128, 2, 2048], BF16), f"D_bk2{i}"),
                z=(sb(f"bz{i}", [128, 512], BF16), f"D_bz{i}"),
                v01=(sb(f"bv{i}", [128, 2, 4, 128], BF16), f"D_bv{i}"),
                hvh=(sb(f"bhv{i}", [128, 2, 4, 128], BF16), f"D_bhv{i}"),
                v2=(sb(f"bv2{i}", [128, 2, 4, 4, 128], BF16), f"D_bv2{i}"),
            ))
        W = dict(wout=wout, wg=wg, wup=wup)
        make_ident(P, T["identf"], T["ident"])
        P.pool(lambda e: e.memset(ones[:], 1.0), writes=["ones"])
        P.dma("sp", "dc", Et[:], Et_d, writes=["Et"])
        P.dma("sp", "dc", flag[:], flag_d, writes=["flag"])
        P.dma("sp", "dc", gfb[:], gfin.partition_broadcast(128), writes=["gfb"])
        def load_epi_weights():
            for (wsrc, wdst, key, nk) in ((w_out, wout, "wout", 8), (w_gate, wg, "wg", 8), (w_up, wup, "wup", 2)):
                for n in range(4):
                    P.dma("sp", "dwst", wst[:, 0:nk, :], wsrc[:, n * 256:(n + 1) * 256].rearrange("(kc kp) c -> kp kc c", kp=128),
                          writes=["wst"])
                    P.pool(lambda e, wdst=wdst, n=n, nk=nk: e.tensor_copy(out=wdst[:, :, n * 256:(n + 1) * 256], in_=wst[:, 0:nk, :]),
                           reads=["wst"], writes=[key])

        def load_bundle(bi, a, h):
            B = bund[bi]
            qt, qk_ = B["q"]
            hs = slice(h * 128, (h + 1) * 128)
            cs = slice(a * 512, (a + 1) * 512)
            P.dma("sp", "l" + qk_, qt[:], q1T.rearrange("(g r) t -> r g t", g=3)[hs, :, cs], writes=[qk_])
            kt, kk = B["k01"]
            hk_, hkk = B["hk"]
            P.dma("sp", "l" + kk, kt[:], k1T[0:2, a, hs, :].rearrange("g d t -> d g t"), writes=[kk])
            P.dma("sp", "l" + hkk, hk_[:], hkT[a].rearrange("(g r) t -> r g t", g=2)[hs, :, :],
                  reads=[("D:hkT", a), ("D:hkTt", a)], writes=[hkk])
            k2, k2k = B["k2"]
            r0 = 2048 + h * 128
            for sp_, aa in ((0, a - 1), (1, a)):
                if aa < 0:
                    continue
                k2src = k1cm.bitcast(BF16)[4 + aa].rearrange("(r x) t -> x r t", r=4)[h * 128:(h + 1) * 128, :, :]
                P.dma("sp", "l" + k2k, k2[:, sp_, :].rearrange("d (r t) -> d r t", r=4), k2src, writes=[k2k])
            zt, zk = B["z"]
            P.dma("sp", "l" + zk, zt[:], zs1T[h * 128:(h + 1) * 128, a * 512:(a + 1) * 512], writes=[zk])
            vt, vk = B["v01"]
            hvt, hvk = B["hvh"]
            for g in range(2):
                P.dma("sp", "l" + vk, vt[:, g], v1[g, a][:, :, h * 128:(h + 1) * 128], writes=[vk])
                P.dma("sp", "l" + hvk, hvt[:, g], hv[a, g].rearrange("p (b c) -> p b c", b=4)[:, :, h * 128:(h + 1) * 128],
                      reads=[("D:hv", a)], writes=[hvk])
            v2, v2k = B["v2"]
            for sp_, aa in ((0, a - 1), (1, a)):
                if aa < 0:
                    continue
                for r in range(4):
                    for u in range(4):
                        src = v1cm.bitcast(BF16)[4 + aa, r * 128:(r + 1) * 128, :].rearrange(
                            "(i u) (b c) -> u i b c", u=4, b=4)[u][:, :, h * 128:(h + 1) * 128]
                        qq = "act" if (r + u) % 2 == 0 else "sp"
                        P.dma(qq, "l" + v2k + qq, v2[32 * r:32 * r + 32, sp_, u], src, writes=[v2k + qq])
            return B

        s4 = lambda t3, r1: t3.rearrange("p (i r) -> p r i", r=4)[:, r1, :]
        s16 = lambda t3, r2: t3.rearrange("p (i r) -> p r i", r=16)[:, r2, :]

        def pairs_of(B, a, h, ni, g1, g1k):
            qt, qk_ = B["q"]; kt, kk = B["k01"]; hk_, hkk = B["hk"]; k2, k2k = B["k2"]
            vt, vk = B["v01"]; hvt, hvk = B["hvh"]; v2, v2k = B["v2"]
            lst = []
            lst.append(dict(kind=0, nblk=4, w=128, lhs=lambda b: kt[:, 0, b * 128:(b + 1) * 128],
                            rhs=lambda b: qt[:, 0, b * 128:(b + 1) * 128], v=lambda b: vt[:, 0, b, :],
                            out=lambda t, b: t[:, b * 128:(b + 1) * 128], nf=0, sk=[kk, qk_], vkeys=[vk]))
            lst.append(dict(kind=1, nblk=4, w=128,
                            lhs=lambda b: (hk_[:, 0, 384:512] if b == 0 else kt[:, 0, (b - 1) * 128:b * 128]),
                            rhs=lambda b: qt[:, 0, b * 128:(b + 1) * 128],
                            v=lambda b: (hvt[:, 0, 3, :] if b == 0 else vt[:, 0, b - 1, :]),
                            out=lambda t, b: t[:, b * 128:(b + 1) * 128], nf=1, sk=[kk, hkk, qk_], vkeys=[vk, hvk]))
            lst.append(dict(kind=2, nblk=4, w=128, lhs=lambda b: s4(kt[:, 1, :], b), rhs=lambda b: s4(qt[:, 1, :], b),
                            v=lambda b: vt[:, 1, b, :], out=lambda t, b: s4(t[:], b), nf=0, sk=[kk, qk_], vkeys=[vk]))
            lst.append(dict(kind=3, nblk=4, w=128, lhs=lambda b: s4(hk_[:, 1, :], b), rhs=lambda b: s4(qt[:, 1, :], b),
                            v=lambda b: hvt[:, 1, b, :], out=lambda t, b: s4(t[:], b), nf=4, sk=[hkk, qk_], vkeys=[hvk]))
            for sp_ in ((1, 0) if a >= 1 else (1,)):
                lst.append(dict(kind=(4 if sp_ == 1 else 5), nblk=16, w=32, lhs=lambda b, sp_=sp_: s16(k2[:, sp_, :], b),
                                rhs=lambda b: s16(qt[:, 2, :], b), v=lambda b, sp_=sp_: v2[:, sp_, b // 4, b % 4, :],
                                out=lambda t, b: s16(t[:], b), nf=0, sk=[k2k, qk_], vkeys=[v2k + "act", v2k + "sp"]))
            for i, pr in enumerate(lst):
                pr.update(a=a, h=h, ni=ni, g1=g1, g1k=g1k, B=B, first=(i == 0), last=(i == len(lst) - 1))
                yield pr

        def emit_S(pr):
            ps, psk = T["pm"].next()
            pr["ps"], pr["psk"] = ps, psk
            w = pr["w"]
            for b in range(pr["nblk"]):
                P.pe(lambda e, ps=ps, b=b, pr=pr, w=w: e.matmul(ps[:, b * w:(b + 1) * w], lhsT=pr["lhs"](b), rhs=pr["rhs"](b),
                                                             start=True, stop=True), reads=pr["sk"], writes=[psk])

        def emit_rest(pr):
            ps, psk, w, nblk, a, h, ni = pr["ps"], pr["psk"], pr["w"], pr["nblk"], pr["a"], pr["h"], pr["ni"]
            if pr["first"] and ni + 1 < len(order) and ni >= 1:
                load_bundle((ni + 1) % 2, *order[ni + 1])
            pn, pnk = pN[ni % 2]
            pd, pdk = pD[ni % 2]
            ex, exk = exr.next()
            P.act(lambda e, ex=ex, ps=ps: e.activation(out=ex[:], in_=ps[:], func=AF.Exp, scale=SC), reads=[psk], writes=[exk])
            pt, ptk = ptr_.next()
            Eb = Et[:, h, pr["kind"], 0:w]
            nf = pr["nf"]
            if nf == 0:
                P.dve(lambda e, pt=pt, ex=ex: e.tensor_tensor(
                    out=pt[:].rearrange("p (b w) -> p b w", w=w), in0=ex[:].rearrange("p (b w) -> p b w", w=w),
                    in1=Eb.unsqueeze(1).broadcast_to([128, nblk, w]), op=ALU.mult), reads=[exk, "Et"], writes=[ptk])
            else:
                P.dve(lambda e, pt=pt, ex=ex: e.scalar_tensor_tensor(
                    out=pt[:, 0:nf * w].rearrange("p (b w) -> p b w", w=w),
                    in0=ex[:, 0:nf * w].rearrange("p (b w) -> p b w", w=w), scalar=flag[:, a:a + 1],
                    in1=Eb.unsqueeze(1).broadcast_to([128, nf, w]), op0=ALU.mult, op1=ALU.mult),
                    reads=[exk, "Et", "flag"], writes=[ptk])
                if nf < nblk:
                    P.dve(lambda e, pt=pt, ex=ex: e.tensor_tensor(
                        out=pt[:, nf * w:].rearrange("p (b w) -> p b w", w=w),
                        in0=ex[:, nf * w:].rearrange("p (b w) -> p b w", w=w),
                        in1=Eb.unsqueeze(1).broadcast_to([128, nblk - nf, w]), op=ALU.mult),
                        reads=[exk, "Et"], writes=[ptk])
            for b in range(nblk):
                st = pr["first"] and b == 0
                P.pe(lambda e, b=b, pt=pt, st=st, pr=pr: e.matmul(pr["out"](pn, b), lhsT=pr["v"](b), rhs=pt[:, b * w:(b + 1) * w],
                                                                 start=st, stop=False, skip_group_check=True),
                     reads=[ptk] + pr["vkeys"], writes=[pnk])
                P.pe(lambda e, b=b, pt=pt, st=st, pr=pr: e.matmul(pr["out"](pd, b), lhsT=ones[:], rhs=pt[:, b * w:(b + 1) * w],
                                                                 start=st, stop=False, skip_group_check=True),
                     reads=[ptk, "ones"], writes=[pdk])
            if pr["last"]:
                zt, zk = pr["B"]["z"]
                g1, g1k = pr["g1"], pr["g1k"]
                rD, rDk = rDr.next()
                P.dve(lambda e, rD=rD: e.reciprocal(out=rD[:], in_=pd[:]), reads=[pdk], writes=[rDk])
                tN, tNk = tNr.next()
                P.dve(lambda e, tN=tN, rD=rD: e.tensor_tensor(out=tN[:], in0=pn[:], in1=rD[:], op=ALU.mult),
                      reads=[pnk, rDk], writes=[tNk])
                P.pool(lambda e, tN=tN, g1=g1, zt=zt: e.tensor_tensor(out=g1[:, h, :], in0=tN[:], in1=zt[:], op=ALU.mult),
                       reads=[tNk, zk], writes=[g1k])

        hv2 = h2_d.rearrange("(lb p) d -> lb p d", p=128)
        pv = p1.rearrange("(lb p) d -> lb p d", p=128)
        yv = y_d.rearrange("(lb p) d -> lb p d", p=128)
        order = [(a, h) for a in range(4) for h in range(8)]
        DEPTH = 2
        bundles = {0: load_bundle(0, *order[0]), 1: load_bundle(1, *order[1])}
        load_epi_weights()
        pend = []
        for ni, (a, h) in enumerate(order):
            B = bund[ni % 2]
            if h == 0:
                g1, g1k = g1T.next()
                if a + 1 < 4:
                    halo_copy(a + 1)
            for pr in pairs_of(B, a, h, ni, g1, g1k):
                emit_S(pr)
                pend.append(pr)
                if len(pend) > DEPTH:
                    emit_rest(pend.pop(0))
            if h == 7:
                while pend:
                    emit_rest(pend.pop(0))
                def post_d(bl, hn, hnk, a=a):
                    lb = 4 * a + bl
                    P.act(lambda e, hn=hn, lb=lb: e.activation(out=sq[:], in_=hn[:], func=AF.Square, accum_out=ss[:, lb:lb + 1]),
                          reads=[hnk], writes=["sq", ("ss", lb)])
                    P.dve(lambda e, lb=lb: e.tensor_scalar(out=ms[:, lb:lb + 1], in0=ss[:, lb:lb + 1], scalar1=1.0 / D, scalar2=EPS,
                                                          op0=ALU.mult, op1=ALU.add), reads=[("ss", lb)], writes=[("ms", lb)])
                    P.act(lambda e, lb=lb: e.activation(out=ms[:, lb:lb + 1], in_=ms[:, lb:lb + 1], func=AF.Sqrt),
                          reads=[("ms", lb)], writes=[("ms", lb)])
                    P.dve(lambda e, lb=lb: e.reciprocal(out=rstd[:, lb:lb + 1], in_=ms[:, lb:lb + 1]),
                          reads=[("ms", lb)], writes=[("rstd", lb)])
                    y, yk = yb.next()
                    P.dve(lambda e, y=y, hn=hn, lb=lb: e.scalar_tensor_tensor(out=y[:], in0=hn[:], scalar=rstd[:, lb:lb + 1], in1=gfb[:],
                                                                             op0=ALU.mult, op1=ALU.mult),
                          reads=[hnk, ("rstd", lb), "gfb"], writes=[yk])
                    P.dma("sp", "sty", yv[lb], y[:], reads=[yk], writes=["D:y"])

                T["mix_keys"] = [g1k]
                T["pm"] = Rot(pS + pN + pD)
                epi_run(P, 4, None, (lambda bl, g1=g1: (lambda kc: g1[:, kc, bl * 128:(bl + 1) * 128])), T, W,
                        (lambda bl, a=a: hv2[4 * a + bl]), (lambda bl, a=a: pv[4 * a + bl]), None, post_d)
                T["pm"] = Rot(pS)
        P.flush()


def build_D_test():
    nc = bass.Bass("TRN2", target_bir_lowering=False)
    io = {}
    ei = lambda n, shp, dt: nc.dram_tensor(n, shp, dt, kind="ExternalInput").ap()
    eo = lambda n, shp, dt: nc.dram_tensor(n, shp, dt, kind="ExternalOutput").ap()
    it = lambda n, shp, dt: nc.dram_tensor(n, shp, dt).ap()
    io["k1Tg"] = ei("k1Tg", [4, 3072, NTL], BF16)
    io["v1g"] = ei("v1g", [4, 3, 128, NLB, D], BF16)
    io["k1T"] = ei("k1T", [3072, NTL], BF16)
    io["v1"] = ei("v1", [3, 128, NLB, D], BF16)
    io["q1T"] = ei("q1T", [3072, NTL], BF16)
    io["zs1T"] = ei("zs1T", [D, NTL], BF16)
    io["h2"] = ei("h2", [NTL, D], F32)
    io["p1"] = ei("p1", [NTL, 256], F32)
    io["dil_w_out"] = ei("dil_w_out", [D, D], F32)
    io["ple_w_up1"] = ei("ple_w_up1", [256, D], F32)
    io["ple_w_gate1"] = ei("ple_w_gate1", [D, D], F32)
    io["final_norm"] = ei("final_norm", [D], F32)
    io["Et"] = ei("Et", [128, 8, 6, 128], F32)
    io["flag"] = ei("flag", [128, 4], F32)
    io["hkT"] = it("hkT", [4, 2048, 512], BF16)
    io["hv"] = it("hv", [4, 2, 128, 4096], BF16)
    io["y"] = eo("y", [NTL, D], F32)
    P = Prog(nc)
    phase_D(nc, P, io)
    return nc


def _io_common(nc, ei):
    io = {}
    io["x"] = ei("x", [NTL, D], F32)
    io["p0"] = ei("p0", [NTL, 256], F32)
    io["p1"] = ei("p1", [NTL, 256], F32)
    io["fox_norm"] = ei("fox_norm", [D], F32)
    io["fox_w_in"] = ei("fox_w_in", [D, 4112], F32)
    io["fox_b_f"] = ei("fox_b_f", [16], F32)
    io["fox_w_out"] = ei("fox_w_out", [D, D], F32)
    io["dil_norm"] = ei("dil_norm", [D], F32)
    io["dil_w_in"] = ei("dil_w_in", [D, 10240], F32)
    io["dil_w_out"] = ei("dil_w_out", [D, D], F32)
    io["ple_w_up0"] = ei("ple_w_up0", [256, D], F32)
    io["ple_w_up1"] = ei("ple_w_up1", [256, D], F32)
    io["ple_w_gate0"] = ei("ple_w_gate0", [D, D], F32)
    io["ple_w_gate1"] = ei("ple_w_gate1", [D, D], F32)
    io["final_norm"] = ei("final_norm", [D], F32)
    io["maskT"] = ei("maskT", [128, 16, 512], BF16)
    io["tri"] = ei("tri", [128, 128], F32)
    io["ustrip"] = ei("ustrip", [128, 127], F32)
    io["Et"] = ei("Et", [128, 8, 6, 128], F32)
    io["flag"] = ei("flag", [128, 4], F32)
    return io


LOCAL0 = [("qT0", [1024, NTL], BF16), ("kT0", [1024, NTL], BF16), ("v0", [4, 128, 4, D], BF16),
          ("zs0", [128, NLB, D], BF16), ("lf0", [128, NLB, 16], F32)]
LOCAL1 = [("h2", [NTL, D], F32), ("q1T", [3072, NTL], BF16), ("k1T", [3, 4, D, 512], BF16),
          ("v1", [3, 4, 128, 4, D], BF16), ("zs1T", [D, NTL], BF16),
          ("k0tail", [D, 4, 128], BF16), ("v0tail", [128, 4, D], BF16)]
GATH0 = [("kT0g", [4, 1024, NTL], BF16), ("v0g", [4, 4, 128, 4, D], BF16), ("lf0g", [4, 128, NLB, 16], F32)]
GATH1 = [("k1Tg", [4, 3, 4, D, 512], BF16), ("v1g", [4, 3, 4, 128, 4, D], BF16),
         ("k0tailg", [4, D, 4, 128], BF16), ("v0tailg", [4, 128, 4, D], BF16)]
SCR = [("caug", [16, 6, S], BF16), ("caug_own", [16, 3, NTL], BF16), ("hkT", [4, 2048, 512], BF16),
       ("hv", [4, 2, 128, 4096], BF16)]


def build_stage(stage):
    nc = bass.Bass("TRN2", target_bir_lowering=False)
    used = {}

    def ei(n, shp, dt):
        return nc.dram_tensor(n, shp, dt, kind="ExternalInput").ap()

    eo = lambda n, shp, dt: nc.dram_tensor(n, shp, dt, kind="ExternalOutput").ap()
    it = lambda n, shp, dt: nc.dram_tensor(n, shp, dt).ap()
    need = {1: ["x", "fox_norm", "fox_w_in", "fox_b_f"],
            2: ["x", "p0", "fox_w_out", "ple_w_up0", "ple_w_gate0", "dil_norm", "dil_w_in", "maskT", "tri", "ustrip"],
            3: ["p1", "dil_w_out", "ple_w_up1", "ple_w_gate1", "final_norm", "Et", "flag"]}[stage]
    shapes = {}
    _io_common(None, lambda n, shp, dt: shapes.setdefault(n, (shp, dt)))
    io = {n: ei(n, *shapes[n]) for n in need}
    P = Prog(nc)
    if stage == 1:
        for n, shp, dt in LOCAL0:
            io[n] = eo(n, shp, dt)
        phase_A(nc, P, io)
    elif stage == 2:
        for n, shp, dt in GATH0:
            io[n] = ei(n, shp, dt)
        for n in ("qT0", "zs0"):
            io[n] = ei(n, *[(s_, d_) for (m_, s_, d_) in LOCAL0 if m_ == n][0])
        for n, shp, dt in SCR[:2]:
            io[n] = it(n, shp, dt)
        for n, shp, dt in LOCAL1:
            io[n] = eo(n, shp, dt)
        with ExitStack() as es2:
            gz = es2.enter_context(nc.sbuf_tensor("gz_sb", [128, NLB, D], BF16))
            Wpre = dict(wout=es2.enter_context(nc.sbuf_tensor("W0_out", [128, 8, D], BF16)),
                        wg=es2.enter_context(nc.sbuf_tensor("W0_g", [128, 8, D], BF16)),
                        wup=es2.enter_context(nc.sbuf_tensor("W0_up", [128, 2, D], BF16)))
            phase_B12(nc, P, io, gz, Wpre)
            phase_B3C(nc, P, io, gz, Wpre)
    else:
        for n, shp, dt in GATH1:
            io[n] = ei(n, shp, dt)
        for n in ("k1T", "v1", "q1T", "zs1T", "h2"):
            io[n] = ei(n, *[(s_, d_) for (m_, s_, d_) in LOCAL1 if m_ == n][0])
        for n, shp, dt in SCR[2:]:
            io[n] = it(n, shp, dt)
        io["y"] = eo("y", [NTL, D], F32)
        phase_D(nc, P, io)
    return nc, need


def own_tokens(j):
    return np.concatenate([np.arange(512 * (4 * a + j), 512 * (4 * a + j) + 512) for a in range(4)])


def host_inputs(inputs):
    x = np.asarray(inputs["x"], np.float32)
    p = np.asarray(inputs["p"], np.float32)
    maps = []
    for c in range(NCORES):
        b, j = c // 4, c % 4
        own = own_tokens(j)
        maskT, tri, us = consts_B(j)
        Et, flag = consts_D(j)
        m = {
            "x": np.ascontiguousarray(x[b][own]), "p0": np.ascontiguousarray(p[0, b][own]),
            "p1": np.ascontiguousarray(p[1, b][own]),
            "fox_norm": np.asarray(inputs["fox_norm"], np.float32)[0], "fox_w_in": np.asarray(inputs["fox_w_in"], np.float32)[0],
            "fox_b_f": np.asarray(inputs["fox_b_f"], np.float32)[0], "fox_w_out": np.asarray(inputs["fox_w_out"], np.float32)[0],
            "dil_norm": np.asarray(inputs["dil_norm"], np.float32)[0], "dil_w_in": np.asarray(inputs["dil_w_in"], np.float32)[0],
            "dil_w_out": np.asarray(inputs["dil_w_out"], np.float32)[0],
            "ple_w_up0": np.asarray(inputs["ple_w_up"], np.float32)[0], "ple_w_up1": np.asarray(inputs["ple_w_up"], np.float32)[1],
            "ple_w_gate0": np.asarray(inputs["ple_w_gate"], np.float32)[0], "ple_w_gate1": np.asarray(inputs["ple_w_gate"], np.float32)[1],
            "final_norm": np.asarray(inputs["final_norm"], np.float32),
            "maskT": maskT, "tri": tri, "ustrip": us, "Et": Et, "flag": flag,
        }
        maps.append(m)
    return maps


def assemble_output(ys):
    out = np.zeros((2, S, D), np.float32)
    for c in range(NCORES):
        b, j = c // 4, c % 4
        out[b, own_tokens(j)] = ys[c]
    return out


def kernel_unfused(**inputs):
    maps = host_inputs(inputs)
    cores = list(range(NCORES))
    nc1, need1 = build_stage(1)
    r1 = run_bass_kernel_spmd(nc1, [{k: m[k] for k in need1} for m in maps], core_ids=cores).results
    nc2, need2 = build_stage(2)
    in2 = []
    for c in range(NCORES):
        b = c // 4
        d2 = {k: maps[c][k] for k in need2}
        d2["kT0g"] = np.stack([r1[4 * b + r]["kT0"] for r in range(4)])
        d2["v0g"] = np.stack([r1[4 * b + r]["v0"] for r in range(4)])
        d2["lf0g"] = np.stack([r1[4 * b + r]["lf0"] for r in range(4)])
        d2["qT0"] = r1[c]["qT0"]
        d2["zs0"] = r1[c]["zs0"]
        in2.append(d2)
    r2 = run_bass_kernel_spmd(nc2, in2, core_ids=cores).results
    nc3, need3 = build_stage(3)
    in3 = []
    for c in range(NCORES):
        b = c // 4
        d3 = {k: maps[c][k] for k in need3}
        d3["k1Tg"] = np.stack([r2[4 * b + r]["k1T"] for r in range(4)])
        d3["v1g"] = np.stack([r2[4 * b + r]["v1"] for r in range(4)])
        d3["k0tailg"] = np.stack([r2[4 * b + r]["k0tail"] for r in range(4)])
        d3["v0tailg"] = np.stack([r2[4 * b + r]["v0tail"] for r in range(4)])
        for k in ("k1T", "v1", "q1T", "zs1T", "h2"):
            d3[k] = r2[c][k]
        in3.append(d3)
    r3 = run_bass_kernel_spmd(nc3, in3, core_ids=cores).results
    return assemble_output([r3[c]["y"] for c in range(NCORES)])


RG = [[0, 1, 2, 3], [4, 5, 6, 7]]


def _flat2(ap):
    n = len(ap.shape)
    if n == 2:
        return ap
    return ap.rearrange({3: "a b c -> (a b) c", 4: "a b c d -> (a b) (c d)", 5: "a b c d e -> (a b c) (d e)",
                         6: "a b c d e f -> (a b c d) (e f)"}[n])


CC_CHUNK_BYTES = 1 << 20
CC_INFLIGHT = 4


def gather_tensor(P, nc, loc, gat, tag, dep_keys, q_scatter, rng=None, cm_out=None):
    g2 = _flat2(gat)
    l2 = _flat2(loc) if len(loc.shape) != 3 else loc.rearrange("a b c -> a (b c)")
    if l2.dtype == BF16:
        l2, g2 = l2.bitcast(F32), g2.bitcast(F32)
    g3 = g2.rearrange("(r a) c -> r a c", r=4)
    if rng is not None:
        l2 = l2[rng[0]:rng[1], :]
        g3 = g3[:, rng[0]:rng[1], :]
    rows, cols = l2.shape
    n = max(1, min(rows, CC_CHUNK_BYTES // (cols * 4)))
    assert rows % n == 0
    nch = rows // n
    keys = []
    if nch == 1 and rng is None:
        P.op("pool", (lambda e: e.collective_compute("AllGather", ALU.bypass, replica_groups=RG,
                                                     ins=[l2.opt()], outs=[g2.opt()])),
             reads=list(dep_keys), writes=[("D:gath_" + tag, 0)], dma="cc", inc=1)
        P.pool(lambda e: e.engine_nop(), reads=[("D:gath_" + tag, 0)], writes=["cc_nop"])
        return None
    tmp = nc.dram_tensor("cc_tmp_" + tag, [nch, 4 * n, cols], F32).ap()
    if cm_out is not None:
        cm_out[tag] = tmp
        for c in range(nch):
            key = ("D:gath_" + tag, c)
            src = l2[c * n:(c + 1) * n, :]
            dst = tmp[c]
            P.op("pool", (lambda e, src=src, dst=dst: e.collective_compute("AllGather", ALU.bypass, replica_groups=RG,
                                                                           ins=[src.opt()], outs=[dst.opt()])),
                 reads=list(dep_keys), writes=[key], dma="cc", inc=1)
            keys.append(key)
            if c - CC_INFLIGHT + 1 >= 0:
                P.pool(lambda e: e.engine_nop(), reads=[keys[c - CC_INFLIGHT + 1]], writes=["cc_nop"])
        for c in range(max(0, nch - CC_INFLIGHT + 1), nch):
            P.pool(lambda e: e.engine_nop(), reads=[keys[c]], writes=["cc_nop"])
        return tmp

    def scatter(c):
        P.dma(q_scatter, "ccs_" + tag, g3[:, c * n:(c + 1) * n, :], tmp[c].rearrange("(r a) c -> r a c", r=4),
              reads=[keys[c]], writes=["D:gath_" + tag])
        if q_scatter != "pool":
            P.pool(lambda e: e.engine_nop(), reads=[keys[c]], writes=["cc_nop"])

    for c in range(nch):
        key = "D:cc_%s_%d" % (tag, c)
        src = l2[c * n:(c + 1) * n, :]
        dst = tmp[c]
        P.op("pool", (lambda e, src=src, dst=dst: e.collective_compute("AllGather", ALU.bypass, replica_groups=RG,
                                                                       ins=[src.opt()], outs=[dst.opt()])),
             reads=list(dep_keys), writes=[key], dma="cc", inc=1)
        keys.append(key)
        if c - CC_INFLIGHT + 1 >= 0:
            scatter(c - CC_INFLIGHT + 1)
    for c in range(max(0, nch - CC_INFLIGHT + 1), nch):
        scatter(c)


def all_gather(P, nc, pairs, dummy, mode="ag", tag=""):
    for i, (loc, gat, stg) in enumerate(pairs):
        gather_tensor(P, nc, loc, gat, "%s%d" % (tag, i), (), "sp")
    P.flush()


def build_fused(mode="ag"):
    nc = bass.Bass("TRN2", target_bir_lowering=False)
    ei = lambda n, shp, dt: nc.dram_tensor(n, shp, dt, kind="ExternalInput").ap()
    it = lambda n, shp, dt: nc.dram_tensor(n, shp, dt).ap()
    io = _io_common(nc, ei)
    for n, shp, dt in LOCAL0 + LOCAL1 + GATH0 + GATH1 + SCR:
        io[n] = it(n, shp, dt)
    io["y"] = nc.dram_tensor("y", [NTL, D], F32, kind="ExternalOutput").ap()
    P = Prog(nc)
    g0 = [("kT0", "kT0g"), ("v0", "v0g"), ("lf0", "lf0g")]
    g1 = [("k1T", "k1Tg"), ("v1", "v1g")]
    stg = {}
    with ExitStack() as es:
        dummy = es.enter_context(nc.sbuf_tensor("cc_dummy", [128, 8], F32))
        if mode == "ar":
            pid = nc.partition_id()
            jj = pid % 4
            zt = es.enter_context(nc.sbuf_tensor("cc_zero", [128, 8192], BF16))
            P.pool(lambda e: e.memset(zt[:], 0.0), writes=["zt"])
            for (ln, gn) in g0 + g1:
                shp = list(io[gn].shape)
                dt = F32 if ln == "lf0" else BF16
                stg[ln] = it("stg_" + ln, shp, dt)
                flat = _flat2(stg[ln])
                rows, cols = flat.shape
                if dt == F32:
                    zsrc = zt[:].bitcast(F32)[:, 0:cols]
                    for r0 in range(0, rows, 128):
                        P.dma("pool", "zf", flat[r0:r0 + 128, :], zsrc, reads=["zt"], writes=["D:stg" + ln])
                elif cols > 8192:
                    for r0 in range(0, rows, 128):
                        for c0 in range(0, cols, 8192):
                            P.dma("pool", "zf", flat[r0:r0 + 128, c0:c0 + 8192], zt[:], reads=["zt"], writes=["D:stg" + ln])
                else:
                    per = max(1, 8192 // cols)
                    for r0 in range(0, rows, 128 * per):
                        nb = min(per, (rows - r0) // 128)
                        P.dma("pool", "zf", flat[r0:r0 + 128 * nb, :].rearrange("(n p) c -> p n c", p=128),
                              zt[:, 0:nb * cols].rearrange("p (n c) -> p n c", c=cols), reads=["zt"], writes=["D:stg" + ln])

        def exchange(names):
            if mode == "ar":
                for (ln, gn) in names:
                    loc = io[ln]
                    n_el = 1
                    for d_ in loc.shape:
                        n_el *= d_
                    l2 = _flat2(loc) if len(loc.shape) != 3 else loc.rearrange("a b c -> a (b c)")
                    dst = bass.AP(tensor=stg[ln].tensor, offset=jj * n_el, ap=[[l2.shape[1], l2.shape[0]], [1, l2.shape[1]]])
                    P.dma("sp", "sx", dst, l2, writes=["D:stgw" + ln])
                P.flush()
            all_gather(P, nc, [(io[ln], io[gn], stg.get(ln)) for (ln, gn) in names], dummy, mode, tag=names[0][0])

        OVERLAP = (mode == "ag")
        gmap = dict(g0 + g1 + [("k0tail", "k0tailg"), ("v0tail", "v0tailg")])

        cm = {}
        io["cm"] = cm

        def gather_cb(ln, dep_keys, rng=None):
            loc_, gat_ = io[ln], io[gmap[ln]]
            if ln == "k1T":
                loc_ = loc_.rearrange("g a x t -> (g a x) t")
                gat_ = gat_.rearrange("r g a x t -> (r g a x) t")
            gather_tensor(P, nc, loc_, gat_, ln, dep_keys, "pool", rng,
                          cm_out=(cm if ln in ("kT0", "k1T", "v1", "v0") else None))

        phase_A(nc, P, io, gather_cb if OVERLAP else None)
        if not OVERLAP:
            exchange(g0)
        with ExitStack() as es2:
            gz = es2.enter_context(nc.sbuf_tensor("gz_sb", [128, NLB, D], BF16))
            Wpre = dict(wout=es2.enter_context(nc.sbuf_tensor("W0_out", [128, 8, D], BF16)),
                        wg=es2.enter_context(nc.sbuf_tensor("W0_g", [128, 8, D], BF16)),
                        wup=es2.enter_context(nc.sbuf_tensor("W0_up", [128, 2, D], BF16)))
            phase_B12(nc, P, io, gz, Wpre)
            phase_B3C(nc, P, io, gz, Wpre, gather_cb if OVERLAP else None)
        if not OVERLAP:
            exchange(g1)
        phase_D(nc, P, io)
    return nc, list(_io_common(None, lambda n, shp, dt: None).keys())


FUSED_MODE = "ag"
USE_FUSED = True


def kernel_fused(**inputs):
    maps = host_inputs(inputs)
    nc, need = build_fused(FUSED_MODE)
    res = run_bass_kernel_spmd(nc, [{k: m[k] for k in need} for m in maps], core_ids=list(range(NCORES))).results
    return assemble_output([res[c]["y"] for c in range(NCORES)])


def kernel(**inputs):
    return (kernel_fused if USE_FUSED else kernel_unfused)(**inputs)
```

```python
from contextlib import ExitStack
import numpy as np
import ml_dtypes
import concourse.bass as bass
import concourse.mybir as mybir
from concourse.bass_utils import run_bass_kernel_spmd

F32 = mybir.dt.float32
BF16 = mybir.dt.bfloat16
AF = mybir.ActivationFunctionType
ALU = mybir.AluOpType
AX = mybir.AxisListType

NCORES = 8
S = 8192
D = 1024
NTL = 2048
NLB = 16
EPS = 1e-6
NEG = -30000.0


class _Op:
    __slots__ = ("eng", "fn", "deps", "inc", "is_dma", "sem", "val", "idx", "grp")


class Prog:
    CE = ("pe", "act", "dve", "pool")

    def __init__(self, nc):
        self.nc = nc
        self.esem = {e: nc.alloc_semaphore("es_" + e) for e in self.CE}
        self.ecnt = {e: 0 for e in self.CE}
        self.waited = {}
        self.dsem = {}
        self.ops = []
        self.tw = {}
        self.tr = {}
        self.alias = {}
        self.marks = {}
        self.no_pool_cast = False
        self.n_total = 0

    @staticmethod
    def _need(x_eng, x_dma, w):
        return w.is_dma or x_dma or w.eng != x_eng

    def op(self, eng, fn, reads=(), writes=(), dma=None, inc=16):
        if self.alias:
            ex = lambda ks: [kk for k in ks for kk in self.alias.get(k, [k])]
            reads, writes = ex(reads), ex(writes)
        x = _Op()
        x.eng = eng
        x.fn = fn
        x.is_dma = dma is not None
        x.grp = dma
        x.inc = False
        x.idx = len(self.ops)
        deps = set()
        for r in reads:
            w = self.tw.get(r)
            if w is not None and (self._need(eng, x.is_dma, w) or eng != "pe"):
                deps.add(w)
        for k in writes:
            w = self.tw.get(k)
            if w is not None and self._need(eng, x.is_dma, w):
                if not (x.is_dma and w.is_dma and w.eng == eng and getattr(w, "grp", None) == dma):
                    deps.add(w)
            for r in self.tr.get(k, ()):
                if self._need(eng, x.is_dma, r):
                    deps.add(r)
        x.deps = sorted(deps, key=lambda o: o.idx)
        for r in reads:
            self.tr.setdefault(r, []).append(x)
        for k in writes:
            self.tw[k] = x
            self.tr[k] = []
        if x.is_dma:
            if dma not in self.dsem:
                self.dsem[dma] = [self.nc.alloc_semaphore("ds_" + str(dma)), 0]
            ent = self.dsem[dma]
            ent[1] += inc
            x.sem = ent[0]
            x.val = ent[1]
            x.inc = inc
        self.ops.append(x)
        return x

    def mark(self, group):
        k = (group, len(self.marks.setdefault(group, [])))
        self.marks[group].append(k)
        return k

    def pe(self, fn, reads=(), writes=()):
        return self.op("pe", fn, reads, writes)

    def act(self, fn, reads=(), writes=()):
        return self.op("act", fn, reads, writes)

    def dve(self, fn, reads=(), writes=()):
        return self.op("dve", fn, reads, writes)

    def pool(self, fn, reads=(), writes=()):
        return self.op("pool", fn, reads, writes)

    def dma(self, q, sem, out, in_, reads=(), writes=(), **kw):
        return self.op(q, lambda e: e.dma_start(out=out, in_=in_, **kw), reads, writes, dma=sem)

    def flush(self):
        nc = self.nc
        ops = self.ops
        for x in ops:
            for d in x.deps:
                if not d.is_dma:
                    d.inc = True
        for x in ops:
            if not x.is_dma:
                if x.inc:
                    self.ecnt[x.eng] += 1
                x.sem = self.esem[x.eng]
                x.val = self.ecnt[x.eng]
        per = {}
        for x in ops:
            per.setdefault(x.eng, []).append(x)
        tail = [(ent[0], ent[1]) for ent in self.dsem.values() if ent[1] > 0]
        waited = self.waited

        def emit(ename, eng, lst):
            for x in lst:
                for d in x.deps:
                    key = (ename, id(d.sem))
                    if waited.get(key, 0) < d.val:
                        eng.wait_ge(d.sem, d.val)
                        waited[key] = d.val
                ins = x.fn(eng)
                if x.inc:
                    ins.then_inc(x.sem, int(x.inc) if x.is_dma else 1)
            if ename == "sp":
                for s, v in tail:
                    key = (ename, id(s))
                    if waited.get(key, 0) < v:
                        eng.wait_ge(s, v)
                        waited[key] = v

        with nc.Block() as block:
            per.setdefault("sp", [])
            for ename, lst in per.items():
                f = (lambda en, l: (lambda eng: emit(en, eng, l)))(ename, lst)
                {"pe": block.tensor, "act": block.scalar, "dve": block.vector,
                 "pool": block.gpsimd, "sp": block.sync}[ename](f)
        self.n_total += len(ops)
        self.ops = []
        self.tw = {}
        self.tr = {}
        self.alias = {}


class Rot:
    def __init__(self, items):
        self.items = items
        self.i = 0

    def next(self):
        it = self.items[self.i % len(self.items)]
        self.i += 1
        return it


def make_ident(P, identf, ident):
    P.pool(lambda e: e.memset(identf[:], 1.0), writes=["identf"])
    P.pool(lambda e: e.affine_select(out=identf[:], in_=identf[:], pattern=[[-1, 128]],
                                     compare_op=ALU.is_equal, fill=0.0, base=0, channel_multiplier=1),
           reads=["identf"], writes=["identf"])
    P.dve(lambda e: e.tensor_copy(out=ident[:], in_=identf[:]), reads=["identf"], writes=["ident"])


def load_weight_bf(P, q, wst_rot, dst, dst_key, src_ap, nk, ncols, split=True):
    st, skey = wst_rot.next()
    P.dma(q, skey, st[:, 0:nk, 0:ncols], src_ap.rearrange("(kc kp) c -> kp kc c", kp=128), writes=[skey])
    if split and nk >= 4:
        k1 = (nk * 5) // 8 if not P.no_pool_cast else nk // 2
        if P.no_pool_cast:
            P.act(lambda e: e.copy(out=dst[:, 0:k1, 0:ncols], in_=st[:, 0:k1, 0:ncols]),
                  reads=[skey], writes=[(dst_key, "a")])
        else:
            P.pool(lambda e: e.tensor_copy(out=dst[:, 0:k1, 0:ncols], in_=st[:, 0:k1, 0:ncols]),
                   reads=[skey], writes=[(dst_key, "a")])
        P.dve(lambda e: e.tensor_copy(out=dst[:, k1:nk, 0:ncols], in_=st[:, k1:nk, 0:ncols]),
              reads=[skey], writes=[(dst_key, "b")])
        P.alias[dst_key] = [(dst_key, "a"), (dst_key, "b")]
    else:
        P.pool(lambda e: e.tensor_copy(out=dst[:, 0:nk, 0:ncols], in_=st[:, 0:nk, 0:ncols]),
               reads=[skey], writes=[dst_key])
        P.alias.pop(dst_key, None)


def rmsnorm_to_xnT(P, nc, xs, xs_key, gb, xnT, lb, T, ident):
    sq, ss, ms, rstd, xnb, ptr = T["sq"], T["ss"], T["ms"], T["rstd"], T["xnb"], T["ptr"]
    P.act(lambda e: e.activation(out=sq[:], in_=xs[:], func=AF.Square, accum_out=ss[:, lb:lb + 1]),
          reads=[xs_key], writes=["sq", ("ss", lb)])
    P.dve(lambda e: e.tensor_scalar(out=ms[:, lb:lb + 1], in0=ss[:, lb:lb + 1], scalar1=1.0 / D, scalar2=EPS,
                                    op0=ALU.mult, op1=ALU.add), reads=[("ss", lb)], writes=[("ms", lb)])
    P.act(lambda e: e.activation(out=ms[:, lb:lb + 1], in_=ms[:, lb:lb + 1], func=AF.Sqrt),
          reads=[("ms", lb)], writes=[("ms", lb)])
    P.dve(lambda e: e.reciprocal(out=rstd[:, lb:lb + 1], in_=ms[:, lb:lb + 1]),
          reads=[("ms", lb)], writes=[("rstd", lb)])
    xb, xbk = xnb.next()
    P.dve(lambda e: e.scalar_tensor_tensor(out=xb[:], in0=xs[:], scalar=rstd[:, lb:lb + 1], in1=gb[:],
                                           op0=ALU.mult, op1=ALU.mult),
          reads=[xs_key, ("rstd", lb), "gb"], writes=[xbk])
    pt, ptk = ptr.next()
    for kc in range(8):
        P.pe(lambda e, kc=kc: e.transpose(out=pt[:, kc, :], in_=xb[:, kc * 128:(kc + 1) * 128], identity=ident[:]),
             reads=[xbk, "ident"], writes=[ptk])
    P.act(lambda e: e.copy(out=xnT[:, :, lb * 128:(lb + 1) * 128], in_=pt[:]),
          reads=[ptk], writes=[("xnT", lb)])


def phase_A(nc, P, io, gather=None):
    x, gnorm, w_in, b_f = io["x"], io["fox_norm"], io["fox_w_in"], io["fox_b_f"]
    qT0, kT0, v0, zs0, lf0 = io["qT0"], io["kT0"], io["v0"], io["zs0"], io["lf0"]
    with ExitStack() as es:
        identf = es.enter_context(nc.sbuf_tensor("A_identf", [128, 128], F32))
        ident = es.enter_context(nc.sbuf_tensor("A_ident", [128, 128], BF16))
        gb = es.enter_context(nc.sbuf_tensor("A_gb", [128, D], F32))
        bfb = es.enter_context(nc.sbuf_tensor("A_bfb", [128, 16], F32))
        xs0 = es.enter_context(nc.sbuf_tensor("A_xs0", [128, D], F32))
        xs1 = es.enter_context(nc.sbuf_tensor("A_xs1", [128, D], F32))
        sq = es.enter_context(nc.sbuf_tensor("A_sq", [128, D], F32))
        ss = es.enter_context(nc.sbuf_tensor("A_ss", [128, NLB], F32))
        ms = es.enter_context(nc.sbuf_tensor("A_ms", [128, NLB], F32))
        rstd = es.enter_context(nc.sbuf_tensor("A_rstd", [128, NLB], F32))
        xnb0 = es.enter_context(nc.sbuf_tensor("A_xnb0", [128, D], BF16))
        xnb1 = es.enter_context(nc.sbuf_tensor("A_xnb1", [128, D], BF16))
        xnT = es.enter_context(nc.sbuf_tensor("A_xnT", [128, 8, NTL], BF16))
        wst0 = es.enter_context(nc.sbuf_tensor("A_wst0", [128, 8, 512], F32))
        wst1 = es.enter_context(nc.sbuf_tensor("A_wst1", [128, 8, 512], F32))
        wbf0 = es.enter_context(nc.sbuf_tensor("A_wbf0", [128, 8, 512], BF16))
        wbf1 = es.enter_context(nc.sbuf_tensor("A_wbf1", [128, 8, 512], BF16))
        wf = es.enter_context(nc.sbuf_tensor("A_wf", [128, 8, 16], BF16))
        vz0 = es.enter_context(nc.sbuf_tensor("A_vz0", [128, NLB, 512], BF16))
        vz1 = es.enter_context(nc.sbuf_tensor("A_vz1", [128, NLB, 512], BF16))
        qk0 = es.enter_context(nc.sbuf_tensor("A_qk0", [128, 512], BF16))
        qk1 = es.enter_context(nc.sbuf_tensor("A_qk1", [128, 512], BF16))
        qk2 = es.enter_context(nc.sbuf_tensor("A_qk2", [128, 512], BF16))
        qk3 = es.enter_context(nc.sbuf_tensor("A_qk3", [128, 512], BF16))
        ft = es.enter_context(nc.sbuf_tensor("A_ft", [128, NLB, 16], F32))
        lf = es.enter_context(nc.sbuf_tensor("A_lf", [128, NLB, 16], F32))
        ptr0 = es.enter_context(nc.psum_tensor("A_ptr0", [128, 8, 128], BF16))
        ptr1 = es.enter_context(nc.psum_tensor("A_ptr1", [128, 8, 128], BF16))
        pm0 = es.enter_context(nc.psum_tensor("A_pm0", [128, 512], F32))
        pm1 = es.enter_context(nc.psum_tensor("A_pm1", [128, 512], F32))
        pm2 = es.enter_context(nc.psum_tensor("A_pm2", [128, 512], F32))
        pm3 = es.enter_context(nc.psum_tensor("A_pm3", [128, 512], F32))
        pf = es.enter_context(nc.psum_tensor("A_pf", [128, NLB, 16], F32))
        make_ident(P, identf, ident)
        P.dma("sp", "c0", gb[:], gnorm.partition_broadcast(128), writes=["gb"])
        P.dma("sp", "c0", bfb[:], b_f.partition_broadcast(128), writes=["bfb"])
        T = dict(sq=sq, ss=ss, ms=ms, rstd=rstd,
                 xnb=Rot([(xnb0, "xnb0"), (xnb1, "xnb1")]),
                 ptr=Rot([(ptr0, "ptr0"), (ptr1, "ptr1")]))
        xs_rot = Rot([(xs0, "xs0"), (xs1, "xs1")])
        wst = Rot([(wst0, "wst0"), (wst1, "wst1")])
        wbf = Rot([(wbf0, "wbf0"), (wbf1, "wbf1")])
        pm = Rot([(pm0, "pm0"), (pm1, "pm1"), (pm2, "pm2"), (pm3, "pm3")])
        qk = Rot([(qk0, "qk0"), (qk1, "qk1"), (qk2, "qk2"), (qk3, "qk3")])
        vz = Rot([(vz0, "vz0"), (vz1, "vz1")])
        xv = x.rearrange("(lb p) d -> lb p d", p=128)
        load_weight_bf(P, "sp", wst, wf, "wf", w_in[:, 4096:4112], 8, 16)
        chunks = [("k", 0), ("k", 1), ("v", 0), ("v", 1), ("q", 0), ("q", 1), ("z", 0), ("z", 1)]
        P.no_pool_cast = gather is not None
        col0 = {"q": 0, "k": 1024, "v": 2048, "z": 3072}
        wcur = []

        def issue_w(ci):
            kind, hh = chunks[ci]
            wb, wbk = wbf.next()
            c0 = col0[kind] + hh * 512
            load_weight_bf(P, "sp", wst, wb, wbk, w_in[:, c0:c0 + 512], 8, 512)
            wcur.append((wb, wbk))

        issue_w(0)
        for lb in range(NLB):
            xs, xsk = xs_rot.next()
            P.dma("sp", xsk, xs[:], xv[lb], writes=[xsk])
            rmsnorm_to_xnT(P, nc, xs, xsk, gb, xnT, lb, T, ident)
        allx = [("xnT", lb) for lb in range(NLB)]
        for lb in range(NLB):
            for kc in range(8):
                P.pe(lambda e, lb=lb, kc=kc: e.matmul(pf[:, lb, :], lhsT=xnT[:, kc, lb * 128:(lb + 1) * 128],
                                                      rhs=wf[:, kc, :], start=(kc == 0), stop=(kc == 7)),
                     reads=[("xnT", lb), "wf"], writes=["pf"])
        P.dve(lambda e: e.tensor_tensor(out=ft[:], in0=pf[:], in1=bfb[:].unsqueeze(1).broadcast_to([128, NLB, 16]),
                                        op=ALU.add), reads=["pf", "bfb"], writes=["ft"])
        P.act(lambda e: e.activation(out=ft[:], in_=ft[:], func=AF.Exp, scale=-1.0), reads=["ft"], writes=["ft"])
        P.act(lambda e: e.activation(out=ft[:], in_=ft[:], func=AF.Ln, bias=1.0), reads=["ft"], writes=["ft"])
        P.dve(lambda e: e.tensor_scalar(out=lf[:], in0=ft[:], scalar1=-1.0, scalar2=None, op0=ALU.mult),
              reads=["ft"], writes=["lf"])
        P.dma("sp", "stl", lf0, lf[:], reads=["lf"], writes=[P.mark("A:lf")])
        if gather is not None:
            gather("lf0", P.marks["A:lf"])
        ev = 0
        for ci, (kind, hh) in enumerate(chunks):
            if ci + 1 < len(chunks):
                issue_w(ci + 1)
            wb, wbk = wcur[ci]
            if kind in ("q", "k"):
                dstT = qT0 if kind == "q" else kT0
                for sl in range(4):
                    for tt in range(4):
                        ps, psk = pm.next()
                        for kc in range(8):
                            P.pe(lambda e, ps=ps, kc=kc, sl=sl, tt=tt, wb=wb: e.matmul(
                                ps[:], lhsT=wb[:, kc, sl * 128:(sl + 1) * 128], rhs=xnT[:, kc, tt * 512:(tt + 1) * 512],
                                start=(kc == 0), stop=(kc == 7)),
                                reads=[wbk] + allx[tt * 4:tt * 4 + 4], writes=[psk])
                        sb, sbk = qk.next()
                        sc = 0.125 if kind == "q" else 1.0
                        if ev % 2 == 0:
                            P.act(lambda e, sb=sb, ps=ps, sc=sc: e.activation(out=sb[:], in_=ps[:], func=AF.Copy, scale=sc),
                                  reads=[psk], writes=[sbk])
                        else:
                            P.dve(lambda e, sb=sb, ps=ps, sc=sc: e.tensor_scalar(out=sb[:], in0=ps[:], scalar1=sc, scalar2=None,
                                                                               op0=ALU.mult), reads=[psk], writes=[sbk])
                        ev += 1
                        r0 = hh * 512 + sl * 128
                        P.dma("sp", "st" + sbk, dstT[r0:r0 + 128, tt * 512:(tt + 1) * 512], sb[:], reads=[sbk],
                              writes=[P.mark("A:" + kind)])
            else:
                vs, vsk = vz.next()
                for lb in range(NLB):
                    ps, psk = pm.next()
                    for kc in range(8):
                        P.pe(lambda e, ps=ps, kc=kc, lb=lb, wb=wb: e.matmul(
                            ps[:], lhsT=xnT[:, kc, lb * 128:(lb + 1) * 128], rhs=wb[:, kc, :],
                            start=(kc == 0), stop=(kc == 7)), reads=[wbk, ("xnT", lb)], writes=[psk])
                    if kind == "v":
                        P.dve(lambda e, ps=ps, lb=lb, vs=vs: e.tensor_copy(out=vs[:, lb, :], in_=ps[:]),
                              reads=[psk], writes=[vsk])
                    else:
                        P.act(lambda e, ps=ps, lb=lb, vs=vs: e.activation(out=vs[:, lb, :], in_=ps[:], func=AF.Silu),
                              reads=[psk], writes=[vsk])
                if kind == "v":
                    for a4 in range(4):
                        P.dma("sp", "st" + vsk, v0[a4][:, :, hh * 512:(hh + 1) * 512], vs[:, 4 * a4:4 * a4 + 4, :], reads=[vsk],
                              writes=[P.mark("A:v")])
                else:
                    P.dma("sp", "st" + vsk, zs0[:, :, hh * 512:(hh + 1) * 512], vs[:], reads=[vsk], writes=[P.mark("A:" + kind)])
            if gather is not None and (kind, hh) == ("k", 1):
                gather("kT0", P.marks["A:k"])
            if gather is not None and (kind, hh) == ("v", 1):
                gather("v0", P.marks["A:v"])
        P.flush()


def own_tokens(j):
    return np.concatenate([np.arange(512 * (4 * a + j), 512 * (4 * a + j) + 512) for a in range(4)])


def build_A():
    nc = bass.Bass("TRN2", target_bir_lowering=False)
    io = {}
    io["x"] = nc.dram_tensor("x", [NTL, D], F32, kind="ExternalInput").ap()
    io["fox_norm"] = nc.dram_tensor("fox_norm", [D], F32, kind="ExternalInput").ap()
    io["fox_w_in"] = nc.dram_tensor("fox_w_in", [D, 4112], F32, kind="ExternalInput").ap()
    io["fox_b_f"] = nc.dram_tensor("fox_b_f", [16], F32, kind="ExternalInput").ap()
    io["qT0"] = nc.dram_tensor("qT0", [1024, NTL], BF16, kind="ExternalOutput").ap()
    io["kT0"] = nc.dram_tensor("kT0", [1024, NTL], BF16, kind="ExternalOutput").ap()
    io["v0"] = nc.dram_tensor("v0", [128, NLB, D], BF16, kind="ExternalOutput").ap()
    io["zs0"] = nc.dram_tensor("zs0", [128, NLB, D], BF16, kind="ExternalOutput").ap()
    io["lf0"] = nc.dram_tensor("lf0", [128, NLB, 16], F32, kind="ExternalOutput").ap()
    P = Prog(nc)
    phase_A(nc, P, io)
    return nc


def phase_B12(nc, P, io, gz, Wpre=None):
    kT0g, v0g, lf0g, qT0, zs0 = io["kT0g"], io["v0g"], io["lf0g"], io["qT0"], io["zs0"]
    caug, maskT_d, tri_d, ustrip_d = io["caug"], io["maskT"], io["tri"], io["ustrip"]
    caug_own = io["caug_own"]
    v0cm = io.get("cm", {}).get("v0")
    pid = nc.partition_id()
    jj = pid % 4
    with ExitStack() as es:
        lfg = es.enter_context(nc.sbuf_tensor("B1_lfg", [128, 64, 16], F32))
        tri = es.enter_context(nc.sbuf_tensor("B1_tri", [128, 128], F32))
        us = es.enter_context(nc.sbuf_tensor("B1_us", [128, 127], F32))
        carry = es.enter_context(nc.sbuf_tensor("B1_carry", [16, 64], F32))
        cc = es.enter_context(nc.sbuf_tensor("B1_cc", [16, 2048], F32))
        t1 = es.enter_context(nc.sbuf_tensor("B1_t1", [16, 2048], F32))
        t2 = es.enter_context(nc.sbuf_tensor("B1_t2", [16, 2048], F32))
        aug0 = es.enter_context(nc.sbuf_tensor("B1_aug0", [16, 6, 2048], BF16))
        aug1 = es.enter_context(nc.sbuf_tensor("B1_aug1", [16, 6, 2048], BF16))
        pc = es.enter_context(nc.psum_tensor("B1_pc", [16, 64], F32))
        pcs0 = es.enter_context(nc.psum_tensor("B1_pcs0", [16, 512], F32))
        pcs1 = es.enter_context(nc.psum_tensor("B1_pcs1", [16, 512], F32))
        P.dma("sp", "b1c", tri[:], tri_d, writes=["tri"])
        P.dma("sp", "b1c", us[:], ustrip_d, writes=["us"])
        P.dma("sp", "b1z", gz[:], zs0, writes=["gz"])
        lfv = lfg[:].rearrange("p (a r bl) h -> p a r bl h", a=4, r=4, bl=4)
        for r in range(4):
            P.dma("sp", "b1l", lfv[:, :, r], lf0g[r].rearrange("p (a bl) h -> p a bl h", bl=4), writes=["lfg"])
        for G in range(64):
            P.pe(lambda e, G=G: e.matmul(pc[:], lhsT=lfg[:, G, :], rhs=us[:, 63 - G:127 - G],
                                         start=(G == 0), stop=(G == 63)), reads=["lfg", "us"], writes=["pc"])
        P.dve(lambda e: e.tensor_copy(out=carry[:], in_=pc[:]), reads=["pc"], writes=["carry"])
        pcs = Rot([(pcs0, "pcs0"), (pcs1, "pcs1")])
        augr = Rot([(aug0, "aug0"), (aug1, "aug1")])
        for ch in range(4):
            for g4 in range(4):
                ps, psk = pcs.next()
                for bq in range(4):
                    G = 16 * ch + 4 * g4 + bq
                    P.pe(lambda e, ps=ps, bq=bq, G=G: e.matmul(ps[:, bq * 128:(bq + 1) * 128], lhsT=lfg[:, G, :], rhs=tri[:],
                                                              start=True, stop=True), reads=["lfg", "tri"], writes=[psk])
                for bq in range(4):
                    G = 16 * ch + 4 * g4 + bq
                    c0 = (4 * g4 + bq) * 128
                    P.dve(lambda e, ps=ps, bq=bq, G=G, c0=c0: e.tensor_scalar(
                        out=cc[:, c0:c0 + 128], in0=ps[:, bq * 128:(bq + 1) * 128], scalar1=carry[:, G:G + 1], scalar2=None,
                        op0=ALU.add), reads=[psk, "carry"], writes=["cc"])
            ag, agk = augr.next()
            P.dve(lambda e, ag=ag: e.tensor_copy(out=ag[:, 0, :], in_=cc[:]), reads=["cc"], writes=[agk])
            P.dve(lambda e, ag=ag: e.tensor_tensor(out=t1[:], in0=cc[:], in1=ag[:, 0, :], op=ALU.subtract),
                  reads=["cc", agk], writes=["t1"])
            P.dve(lambda e, ag=ag: e.tensor_copy(out=ag[:, 1, :], in_=t1[:]), reads=["t1"], writes=[agk])
            P.dve(lambda e, ag=ag: e.tensor_tensor(out=t2[:], in0=t1[:], in1=ag[:, 1, :], op=ALU.subtract),
                  reads=["t1", agk], writes=["t2"])
            P.dve(lambda e, ag=ag: e.tensor_copy(out=ag[:, 2, :], in_=t2[:]), reads=["t2"], writes=[agk])
            P.dve(lambda e, ag=ag: e.tensor_scalar(out=ag[:, 3:6, :], in0=ag[:, 0:3, :], scalar1=-1.0, scalar2=None,
                                                   op0=ALU.mult), reads=[agk], writes=[agk])
            P.dma("sp", "b1s", caug[:, :, ch * 2048:(ch + 1) * 2048], ag[:], reads=[agk], writes=["D:caug"])
        for a in range(4):
            P.dma("pool", "b1o", caug_own[:, :, a * 512:(a + 1) * 512], caug[:, 0:3, bass.ds(jj * 512 + 2048 * a, 512)],
                  reads=["D:caug"], writes=["D:caug_own"])
        P.flush()
    with ExitStack() as es:
        identf = es.enter_context(nc.sbuf_tensor("B2_identf", [128, 128], F32))
        ident = es.enter_context(nc.sbuf_tensor("B2_ident", [128, 128], BF16))
        maskT = es.enter_context(nc.sbuf_tensor("B2_mask", [128, 16, 512], BF16))
        kTa = es.enter_context(nc.sbuf_tensor("B2_kT0", [70, S], BF16))
        kTb = es.enter_context(nc.sbuf_tensor("B2_kT1", [70, S], BF16))
        qTa = es.enter_context(nc.sbuf_tensor("B2_qT0", [70, NTL], BF16))
        qTb = es.enter_context(nc.sbuf_tensor("B2_qT1", [70, NTL], BF16))
        vraw = es.enter_context(nc.sbuf_tensor("B2_vraw", [128, 64, 128], BF16))
        vA0 = es.enter_context(nc.sbuf_tensor("B2_vA0", [128, 64, 2, 65], BF16))
        vA1 = es.enter_context(nc.sbuf_tensor("B2_vA1", [128, 64, 2, 65], BF16))
        pT0 = es.enter_context(nc.sbuf_tensor("B2_pT0", [128, 2, 512], BF16))
        pT1 = es.enter_context(nc.sbuf_tensor("B2_pT1", [128, 2, 512], BF16))
        pT2 = es.enter_context(nc.sbuf_tensor("B2_pT2", [128, 2, 512], BF16))
        rc = es.enter_context(nc.sbuf_tensor("B2_rc", [128, 8], F32))
        pS0 = es.enter_context(nc.psum_tensor("B2_pS0", [128, 2, 512], F32))
        pS1 = es.enter_context(nc.psum_tensor("B2_pS1", [128, 2, 512], F32))
        pS2 = es.enter_context(nc.psum_tensor("B2_pS2", [128, 2, 512], F32))
        pO0 = es.enter_context(nc.psum_tensor("B2_pO0", [128, 4, 128], F32))
        pO1 = es.enter_context(nc.psum_tensor("B2_pO1", [128, 4, 128], F32))
        make_ident(P, identf, ident)
        P.dma("sp", "b2c", maskT[:], maskT_d, writes=["mask"])
        for kt, ktk in ((kTa, "kT0"), (kTb, "kT1")):
            P.pool(lambda e, kt=kt: e.memset(kt[64:67, :], 1.0), writes=[ktk])
        for qt, qtk in ((qTa, "qT0"), (qTb, "qT1")):
            P.pool(lambda e, qt=qt: e.memset(qt[64:70, :], 1.0), writes=[qtk])
        for va, vak in ((vA0, "vA0"), (vA1, "vA1")):
            P.pool(lambda e, va=va: e.memset(va[:, :, :, 64:65], 1.0), writes=[vak])
        kTr = Rot([(kTa, "kT0"), (kTb, "kT1")])
        qTr = Rot([(qTa, "qT0"), (qTb, "qT1")])
        pSr = Rot([(pS0, "pS0"), (pS1, "pS1"), (pS2, "pS2")])
        pTr = Rot([(pT0, "pT0"), (pT1, "pT1"), (pT2, "pT2")])
        pOr = Rot([(pO0, "pO0"), (pO1, "pO1")])
        rci = [0]

        def load_head(h):
            kt, ktk = kTr.next()
            qt, qtk = qTr.next()
            ktv = kt[0:64, :].rearrange("d (a r t) -> d a r t", a=4, r=4)
            kcm = io.get("cm", {}).get("kT0")
            for r in range(4):
                if kcm is None:
                    ksrc_ = kT0g[r, h * 64:(h + 1) * 64, :]
                else:
                    ksrc_ = kcm.bitcast(BF16)[h // 4, r * 256 + (h % 4) * 64:r * 256 + (h % 4) * 64 + 64, :]
                P.dma("sp", "ld" + ktk, ktv[:, :, r], ksrc_.rearrange("d (a t) -> d a t", a=4), writes=[ktk])
            P.dma("sp", "ld" + ktk, kt[67:70, :], caug[h, 3:6, :], reads=["D:caug"], writes=[ktk])
            P.dma("sp", "ld" + qtk, qt[0:64, :], qT0[h * 64:(h + 1) * 64, :], writes=[qtk])
            P.dma("sp", "ld" + qtk, qt[64:67, :], caug_own[h], reads=["D:caug_own"], writes=[qtk])
            return kt, ktk, qt, qtk

        vbufs = [(vA0, "vA0"), (vA1, "vA1")]

        def load_vpair(hp):
            va, vak = vbufs[hp % 2]
            vrv = vraw[:].rearrange("p (a r bl) c -> p a r bl c", a=4, r=4, bl=4)
            for r in range(4):
                for a in range(4):
                    vsrc_ = v0cm.bitcast(BF16)[a, r * 128:(r + 1) * 128, :].rearrange("p (bl c) -> p bl c", bl=4)[
                        :, :, hp * 128:(hp + 1) * 128]
                    P.dma("sp", "ldvr", vrv[:, a, r], vsrc_, writes=["vraw"])
            P.pool(lambda e, va=va: e.tensor_copy(out=va[:, :, :, 0:64], in_=vraw[:].rearrange("p g (hh c) -> p g hh c", hh=2)),
                   reads=["vraw"], writes=[vak])

        def prefetch_epilogue_weights():
            wst_ = es.enter_context(nc.sbuf_tensor("B2_wst", [128, 8, 512], F32))
            for (wsrc, wdst, key, nk) in ((io["fox_w_out"], Wpre["wout"], "wout", 8), (io["ple_w_gate0"], Wpre["wg"], "wg", 8),
                                          (io["ple_w_up0"], Wpre["wup"], "wup", 2)):
                for n in range(2):
                    P.dma("sp", "b2wst", wst_[:, 0:nk, :], wsrc[:, n * 512:(n + 1) * 512].rearrange("(kc kp) c -> kp kc c", kp=128),
                          writes=["b2wst"])
                    P.pool(lambda e, wdst=wdst, n=n, nk=nk: e.tensor_copy(out=wdst[:, :, n * 512:(n + 1) * 512], in_=wst_[:, 0:nk, :]),
                           reads=["b2wst"], writes=[key])

        DEPTH = 2
        state = {}

        def gen():
            nxt_head = load_head(0)
            load_vpair(0)
            if Wpre is not None:
                prefetch_epilogue_weights()
            for h in range(16):
                hp, hh = h // 2, h % 2
                kt, ktk, qt, qtk = nxt_head
                va, vak = vbufs[hp % 2]
                if h + 1 < 16:
                    nxt_head = load_head(h + 1)
                for a in range(4):
                    nJ = 16 * a + 16
                    po, pok = pOr.next()
                    for J2 in range(nJ // 2):
                        yield dict(h=h, hh=hh, a=a, J2=J2, nJ=nJ, kt=kt, ktk=ktk, qt=qt, qtk=qtk, va=va, vak=vak, po=po, pok=pok)

        def emit_S(st):
            ps, psk = pSr.next()
            st["ps"], st["psk"] = ps, psk
            a, kt, qt = st["a"], st["kt"], st["qt"]
            for u in range(2):
                J = 2 * st["J2"] + u
                masked = J >= 16 * a
                P.pe(lambda e, ps=ps, kt=kt, qt=qt, J=J, a=a, masked=masked, u=u: e.matmul(
                    ps[:, u, :], lhsT=kt[0:70, J * 128:(J + 1) * 128], rhs=qt[0:70, a * 512:(a + 1) * 512],
                    start=True, stop=(not masked)), reads=[st["ktk"], st["qtk"]], writes=[psk])
                if masked:
                    P.pe(lambda e, ps=ps, J=J, a=a, u=u: e.matmul(ps[:, u, :], lhsT=ident[:], rhs=maskT[:, J - 16 * a, :],
                                                                start=False, stop=True), reads=["ident", "mask"], writes=[psk])

        def emit_rest(st):
            ps, psk, po, pok, va, vak = st["ps"], st["psk"], st["po"], st["pok"], st["va"], st["vak"]
            h, hh, a, nJ = st["h"], st["hh"], st["a"], st["nJ"]
            if hh == 0 and a == 0 and st["J2"] == 0 and h // 2 + 1 < 8:
                load_vpair(h // 2 + 1)
            pt, ptk = pTr.next()
            P.act(lambda e, pt=pt, ps=ps: e.activation(out=pt[:], in_=ps[:], func=AF.Exp), reads=[psk], writes=[ptk])
            for u in range(2):
                J = 2 * st["J2"] + u
                for qb in range(4):
                    P.pe(lambda e, po=po, pt=pt, va=va, qb=qb, J=J, hh=hh, nJ=nJ, u=u: e.matmul(
                        po[:, qb, 0:65], lhsT=pt[:, u, qb * 128:(qb + 1) * 128], rhs=va[:, J, hh, :],
                        start=(J == 0 and qb == 0), stop=(J == nJ - 1), skip_group_check=True), reads=[ptk, vak], writes=[pok])
            if st["J2"] == nJ // 2 - 1:
                for qb in range(4):
                    lb = 4 * a + qb
                    ri = rci[0] % 8
                    rci[0] += 1
                    P.dve(lambda e, po=po, qb=qb, ri=ri: e.reciprocal(out=rc[:, ri:ri + 1], in_=po[:, qb, 64:65]),
                          reads=[pok], writes=[("rc", ri)])
                    P.dve(lambda e, po=po, qb=qb, ri=ri, lb=lb, h=h: e.scalar_tensor_tensor(
                        out=gz[:, lb, h * 64:(h + 1) * 64], in0=po[:, qb, 0:64], scalar=rc[:, ri:ri + 1],
                        in1=gz[:, lb, h * 64:(h + 1) * 64], op0=ALU.mult, op1=ALU.mult),
                        reads=[pok, ("rc", ri), "gz"], writes=["gz"])

        pend = []
        for st in gen():
            emit_S(st)
            pend.append(st)
            if len(pend) > DEPTH:
                emit_rest(pend.pop(0))
        while pend:
            emit_rest(pend.pop(0))
        P.flush()


def consts_B(j):
    k = np.arange(128)[:, None, None]
    Jr = np.arange(16)[None, :, None]
    q = np.arange(512)[None, None, :]
    maskT = np.where(128 * Jr + k <= 512 * j + q, 0.0, NEG).astype(ml_dtypes.bfloat16)
    s = np.arange(128)[:, None]
    t = np.arange(128)[None, :]
    tri = (s <= t).astype(np.float32)
    us = np.broadcast_to((np.arange(127) > 63).astype(np.float32)[None, :], (128, 127)).copy()
    return maskT, tri, us


def build_B12_test(dbg=False):
    nc = bass.Bass("TRN2", target_bir_lowering=False)
    io = {}
    if dbg:
        io["dbg_pt"] = nc.dram_tensor("dbg_pt", [2, 128, 512], BF16, kind="ExternalOutput").ap()
        io["dbg_ps"] = nc.dram_tensor("dbg_ps", [2, 128, 512], F32, kind="ExternalOutput").ap()
        io["dbg_po"] = nc.dram_tensor("dbg_po", [128, 512], F32, kind="ExternalOutput").ap()
    io["kT0g"] = nc.dram_tensor("kT0g", [4, 1024, NTL], BF16, kind="ExternalInput").ap()
    io["v0g"] = nc.dram_tensor("v0g", [4, 128, NLB, D], BF16, kind="ExternalInput").ap()
    io["lf0g"] = nc.dram_tensor("lf0g", [4, 128, NLB, 16], F32, kind="ExternalInput").ap()
    io["qT0"] = nc.dram_tensor("qT0", [1024, NTL], BF16, kind="ExternalInput").ap()
    io["zs0"] = nc.dram_tensor("zs0", [128, NLB, D], BF16, kind="ExternalInput").ap()
    io["maskT"] = nc.dram_tensor("maskT", [128, 16, 512], BF16, kind="ExternalInput").ap()
    io["tri"] = nc.dram_tensor("tri", [128, 128], F32, kind="ExternalInput").ap()
    io["ustrip"] = nc.dram_tensor("ustrip", [128, 127], F32, kind="ExternalInput").ap()
    io["caug"] = nc.dram_tensor("caug", [16, 6, S], BF16, kind="ExternalOutput").ap()
    io["caug_own"] = nc.dram_tensor("caug_own", [16, 3, NTL], BF16, kind="ExternalOutput").ap()
    gz_d = nc.dram_tensor("gz", [128, NLB, D], BF16, kind="ExternalOutput").ap()
    P = Prog(nc)
    with nc.sbuf_tensor("gz_sb", [128, NLB, D], BF16) as gz:
        phase_B12(nc, P, io, gz)
        P.dma("sp", "gzst", gz_d, gz[:], reads=["gz"])
        P.flush()
    return nc


def epilogue_block(P, nc, lb, mixT, mixT_keys, T, W, hres_src, p_src, h_out_dst, nk_mix):
    pm, ptr, ident = T["pm"], T["ptr"], T["ident"]
    xs, xsk = T["xs"].next()
    P.dma("sp", "ld" + xsk, xs[:], hres_src, writes=[xsk])
    pb, pbk = T["pb"].next()
    P.dma("sp", "ld" + pbk, pb[:], p_src, writes=[pbk])
    h1, h1k = T["h1"].next()
    for n in range(2):
        ps, psk = pm.next()
        for kc in range(nk_mix):
            P.pe(lambda e, ps=ps, kc=kc, n=n: e.matmul(ps[:], lhsT=mixT(kc), rhs=W["wout"][:, kc, n * 512:(n + 1) * 512],
                                                     start=(kc == 0), stop=(kc == nk_mix - 1)),
                 reads=list(mixT_keys) + ["wout"], writes=[psk])
        P.dve(lambda e, ps=ps, n=n, h1=h1, xs=xs: e.tensor_tensor(out=h1[:, n * 512:(n + 1) * 512], in0=ps[:],
                                                                in1=xs[:, n * 512:(n + 1) * 512], op=ALU.add),
              reads=[psk, xsk], writes=[h1k])
    hb, hbk = T["hb"].next()
    P.act(lambda e, hb=hb, h1=h1: e.copy(out=hb[:], in_=h1[:]), reads=[h1k], writes=[hbk])
    pt, ptk = ptr.next()
    for kc in range(8):
        P.pe(lambda e, kc=kc, pt=pt, hb=hb: e.transpose(out=pt[:, kc, :], in_=hb[:, kc * 128:(kc + 1) * 128], identity=ident[:]),
             reads=[hbk, "ident"], writes=[ptk])
    hT, hTk = T["hT"].next()
    P.dve(lambda e, hT=hT, pt=pt: e.tensor_copy(out=hT[:], in_=pt[:]), reads=[ptk], writes=[hTk])
    pbb, pbbk = T["pbb"].next()
    P.dve(lambda e, pbb=pbb, pb=pb: e.tensor_copy(out=pbb[:], in_=pb[:]), reads=[pbk], writes=[pbbk])
    pt2, pt2k = ptr.next()
    for k2 in range(2):
        P.pe(lambda e, k2=k2, pt2=pt2, pbb=pbb: e.transpose(out=pt2[:, k2, :], in_=pbb[:, k2 * 128:(k2 + 1) * 128], identity=ident[:]),
             reads=[pbbk, "ident"], writes=[pt2k])
    pT, pTk = T["pT"].next()
    P.dve(lambda e, pT=pT, pt2=pt2: e.tensor_copy(out=pT[:], in_=pt2[:, 0:2, :]), reads=[pt2k], writes=[pTk])
    gate, gk = T["gate"].next()
    for n in range(2):
        ps, psk = pm.next()
        for kc in range(8):
            P.pe(lambda e, ps=ps, kc=kc, n=n, hT=hT: e.matmul(ps[:], lhsT=hT[:, kc, :], rhs=W["wg"][:, kc, n * 512:(n + 1) * 512],
                                                            start=(kc == 0), stop=(kc == 7)), reads=[hTk, "wg"], writes=[psk])
        P.act(lambda e, ps=ps, n=n, gate=gate: e.activation(out=gate[:, n * 512:(n + 1) * 512], in_=ps[:], func=AF.Sigmoid),
              reads=[psk], writes=[gk])
    hn, hnk = T["hn"].next()
    for n in range(2):
        ps, psk = pm.next()
        for k2 in range(2):
            P.pe(lambda e, ps=ps, k2=k2, n=n, pT=pT: e.matmul(ps[:], lhsT=pT[:, k2, :], rhs=W["wup"][:, k2, n * 512:(n + 1) * 512],
                                                            start=(k2 == 0), stop=(k2 == 1)), reads=[pTk, "wup"], writes=[psk])
        P.dve(lambda e, ps=ps, n=n, gate=gate: e.tensor_tensor(out=gate[:, n * 512:(n + 1) * 512], in0=ps[:],
                                                             in1=gate[:, n * 512:(n + 1) * 512], op=ALU.mult),
              reads=[psk, gk], writes=[gk])
    P.dve(lambda e, hn=hn, gate=gate, h1=h1: e.tensor_tensor(out=hn[:], in0=gate[:], in1=h1[:], op=ALU.add),
          reads=[gk, h1k], writes=[hnk])
    if h_out_dst is not None:
        P.dma("sp", "st" + hnk, h_out_dst, hn[:], reads=[hnk], writes=["D:hout"])
    return hn, hnk


def epi_run(P, nblocks, pre, mixT_of, T, W, hres_of, p_of, hout_of, post, nk_mix=8):
    pm, ptr, ident = T["pm"], T["ptr"], T["ident"]
    ctx = {}

    def s_pre(i):
        c = ctx[i] = {}
        c["mk"] = pre(i) if pre is not None else list(T.get("mix_keys", []))
        c["xs"], c["xsk"] = T["xs"].next()
        P.dma("sp", "ld" + c["xsk"], c["xs"][:], hres_of(i), writes=[c["xsk"]])
        c["pb"], c["pbk"] = T["pb"].next()
        P.dma("sp", "ld" + c["pbk"], c["pb"][:], p_of(i), writes=[c["pbk"]])
        c["pbb"], c["pbbk"] = T["pbb"].next()
        P.dve(lambda e, c=c: e.tensor_copy(out=c["pbb"][:], in_=c["pb"][:]), reads=[c["pbk"]], writes=[c["pbbk"]])

    def s1(i):
        c = ctx[i]
        mixT = mixT_of(i)
        c["h1"], c["h1k"] = T["h1"].next()
        for n in range(2):
            ps, psk = pm.next()
            for kc in range(nk_mix):
                P.pe(lambda e, ps=ps, kc=kc, n=n: e.matmul(ps[:], lhsT=mixT(kc), rhs=W["wout"][:, kc, n * 512:(n + 1) * 512],
                                                         start=(kc == 0), stop=(kc == nk_mix - 1)),
                     reads=list(c["mk"]) + ["wout"], writes=[psk])
            P.dve(lambda e, ps=ps, n=n, c=c: e.tensor_tensor(out=c["h1"][:, n * 512:(n + 1) * 512], in0=ps[:],
                                                            in1=c["xs"][:, n * 512:(n + 1) * 512], op=ALU.add),
                  reads=[psk, c["xsk"]], writes=[c["h1k"]])
        c["hb"], c["hbk"] = T["hb"].next()
        P.act(lambda e, c=c: e.copy(out=c["hb"][:], in_=c["h1"][:]), reads=[c["h1k"]], writes=[c["hbk"]])

    def s2a(i):
        c = ctx[i]
        pt, ptk = ptr.next()
        for kc in range(8):
            P.pe(lambda e, kc=kc, pt=pt, c=c: e.transpose(out=pt[:, kc, :], in_=c["hb"][:, kc * 128:(kc + 1) * 128], identity=ident[:]),
                 reads=[c["hbk"], "ident"], writes=[ptk])
        c["hT"], c["hTk"] = T["hT"].next()
        P.dve(lambda e, c=c, pt=pt: e.tensor_copy(out=c["hT"][:], in_=pt[:]), reads=[ptk], writes=[c["hTk"]])
        pt2, pt2k = ptr.next()
        for k2 in range(2):
            P.pe(lambda e, k2=k2, pt2=pt2, c=c: e.transpose(out=pt2[:, k2, :], in_=c["pbb"][:, k2 * 128:(k2 + 1) * 128], identity=ident[:]),
                 reads=[c["pbbk"], "ident"], writes=[pt2k])
        c["pT"], c["pTk"] = T["pT"].next()
        P.act(lambda e, c=c, pt2=pt2: e.copy(out=c["pT"][:], in_=pt2[:, 0:2, :]), reads=[pt2k], writes=[c["pTk"]])

    def s2b(i):
        c = ctx[i]
        gate, gk = T["gate"].next()
        for n in range(2):
            ps, psk = pm.next()
            for kc in range(8):
                P.pe(lambda e, ps=ps, kc=kc, n=n, c=c: e.matmul(ps[:], lhsT=c["hT"][:, kc, :], rhs=W["wg"][:, kc, n * 512:(n + 1) * 512],
                                                              start=(kc == 0), stop=(kc == 7)), reads=[c["hTk"], "wg"], writes=[psk])
            P.act(lambda e, ps=ps, n=n, gate=gate: e.activation(out=gate[:, n * 512:(n + 1) * 512], in_=ps[:], func=AF.Sigmoid),
                  reads=[psk], writes=[gk])
        hn, hnk = T["hn"].next()
        for n in range(2):
            ps, psk = pm.next()
            for k2 in range(2):
                P.pe(lambda e, ps=ps, k2=k2, n=n, c=c: e.matmul(ps[:], lhsT=c["pT"][:, k2, :], rhs=W["wup"][:, k2, n * 512:(n + 1) * 512],
                                                              start=(k2 == 0), stop=(k2 == 1)), reads=[c["pTk"], "wup"], writes=[psk])
            P.dve(lambda e, ps=ps, n=n, gate=gate: e.tensor_tensor(out=gate[:, n * 512:(n + 1) * 512], in0=ps[:],
                                                                 in1=gate[:, n * 512:(n + 1) * 512], op=ALU.mult),
                  reads=[psk, gk], writes=[gk])
        P.dve(lambda e, hn=hn, gate=gate, c=c: e.tensor_tensor(out=hn[:], in0=gate[:], in1=c["h1"][:], op=ALU.add),
              reads=[gk, c["h1k"]], writes=[hnk])
        dst = hout_of(i) if hout_of is not None else None
        if dst is not None:
            P.dma("sp", "st" + hnk, dst, hn[:], reads=[hnk], writes=["D:hout"])
        c["hn"], c["hnk"] = hn, hnk

    for i in range(nblocks + 2):
        if i < nblocks:
            s_pre(i)
        if 0 <= i - 1 < nblocks:
            s2a(i - 1)
        if i < nblocks:
            s1(i)
        if 0 <= i - 1 < nblocks:
            s2b(i - 1)
        if 0 <= i - 2 < nblocks:
            c = ctx.pop(i - 2)
            post(i - 2, c["hn"], c["hnk"])


def alloc_epilogue(nc, es, pfx):
    sb = lambda n, shp, dt: es.enter_context(nc.sbuf_tensor(pfx + n, shp, dt))
    ps = lambda n, shp, dt: es.enter_context(nc.psum_tensor(pfx + n, shp, dt))
    T = {}
    T["identf"] = sb("identf", [128, 128], F32)
    T["ident_t"] = sb("ident", [128, 128], BF16)
    T["ident"] = T["ident_t"]
    mk = lambda n, shp, dt, k: Rot([(sb(f"{n}{i}", shp, dt), f"{pfx}{n}{i}") for i in range(k)])
    T["xs"] = mk("xs", [128, D], F32, 2)
    T["pb"] = mk("pb", [128, 256], F32, 2)
    T["pbb"] = mk("pbb", [128, 256], BF16, 2)
    T["h1"] = mk("h1", [128, D], F32, 2)
    T["hb"] = mk("hb", [128, D], BF16, 2)
    T["hT"] = mk("hT", [128, 8, 128], BF16, 1)
    T["pT"] = mk("pT", [128, 2, 128], BF16, 1)
    T["gate"] = mk("gate", [128, D], F32, 1)
    T["hn"] = mk("hn", [128, D], F32, 3)
    T["ptr"] = Rot([(ps(f"ptr{i}", [128, 8, 128], BF16), f"{pfx}ptr{i}") for i in range(3)])
    T["pm"] = Rot([(ps(f"pm{i}", [128, 512], F32), f"{pfx}pm{i}") for i in range(5)])
    T["sq"] = sb("sq", [128, D], BF16)
    T["ss"] = sb("ss", [128, NLB], F32)
    T["ms"] = sb("ms", [128, NLB], F32)
    T["rstd"] = sb("rstd", [128, NLB], F32)
    T["xnb"] = mk("xnb", [128, D], BF16, 2)
    return T


def phase_B3C(nc, P, io, gz, Wpre=None, gather=None):
    x, p0 = io["x"], io["p0"]
    w_out, w_up, w_gate, g1n, w_in1 = io["fox_w_out"], io["ple_w_up0"], io["ple_w_gate0"], io["dil_norm"], io["dil_w_in"]
    h2_d, q1T, k1T, v1, zs1T = io["h2"], io["q1T"], io["k1T"], io["v1"], io["zs1T"]
    k0tail, v0tail = io["k0tail"], io["v0tail"]
    with ExitStack() as es:
        sb = lambda n, shp, dt: es.enter_context(nc.sbuf_tensor("C_" + n, shp, dt))
        T = alloc_epilogue(nc, es, "C_")
        gb = sb("gb", [128, D], F32)
        if Wpre is None:
            wout = sb("wout", [128, 8, D], BF16)
            wg = sb("wg", [128, 8, D], BF16)
            wup = sb("wup", [128, 2, D], BF16)
        else:
            wout, wg, wup = Wpre["wout"], Wpre["wg"], Wpre["wup"]
        wst = Rot([(sb(f"wst{i}", [128, 8, 512], F32), f"C_wst{i}") for i in range(1)])
        wbf = Rot([(sb(f"wbf{i}", [128, 8, 512], BF16), f"C_wbf{i}") for i in range(2)])
        gT = Rot([(sb(f"gT{i}", [128, 8, 128], BF16), f"C_gT{i}") for i in range(2)])
        xn1T = sb("xn1T", [128, 8, NTL], BF16)
        qk = Rot([(sb(f"qk{i}", [128, 512], BF16), f"C_qk{i}") for i in range(4)])
        W = dict(wout=wout, wg=wg, wup=wup)
        make_ident(P, T["identf"], T["ident_t"])
        P.dma("sp", "c1", gb[:], g1n.partition_broadcast(128), writes=["gb"])
        for n in range(2 if Wpre is None else 0):
            st, stk = wst.next()
            P.dma("sp", stk, st[:], w_out[:, n * 512:(n + 1) * 512].rearrange("(kc kp) c -> kp kc c", kp=128), writes=[stk])
            P.pool(lambda e, st=st, n=n: e.tensor_copy(out=wout[:, :, n * 512:(n + 1) * 512], in_=st[:]), reads=[stk], writes=["wout"])
        for n in range(2 if Wpre is None else 0):
            st, stk = wst.next()
            P.dma("sp", stk, st[:], w_gate[:, n * 512:(n + 1) * 512].rearrange("(kc kp) c -> kp kc c", kp=128), writes=[stk])
            P.pool(lambda e, st=st, n=n: e.tensor_copy(out=wg[:, :, n * 512:(n + 1) * 512], in_=st[:]), reads=[stk], writes=["wg"])
        for n in range(2 if Wpre is None else 0):
            st, stk = wst.next()
            P.dma("sp", stk, st[:, 0:2, :], w_up[:, n * 512:(n + 1) * 512].rearrange("(kc kp) c -> kp kc c", kp=128), writes=[stk])
            P.pool(lambda e, st=st, n=n: e.tensor_copy(out=wup[:, :, n * 512:(n + 1) * 512], in_=st[:, 0:2, :]), reads=[stk], writes=["wup"])
        wcur = []
        chunks = [("k", i) for i in range(6)] + [("v", i) for i in range(6)] + [("q", i) for i in range(6)] + [("z", i) for i in range(2)]
        P.no_pool_cast = gather is not None
        col0 = {"q": 0, "k": 3072, "v": 6144, "z": 9216}

        def issue_w(ci):
            kind, i = chunks[ci]
            wb, wbk = wbf.next()
            c0 = col0[kind] + i * 512
            load_weight_bf(P, "sp", wst, wb, wbk, w_in1[:, c0:c0 + 512], 8, 512)
            wcur.append((wb, wbk))

        xv = x.rearrange("(lb p) d -> lb p d", p=128)
        pv = p0.rearrange("(lb p) d -> lb p d", p=128)
        hv = h2_d.rearrange("(lb p) d -> lb p d", p=128)
        gcur = {}

        def pre_c(lb):
            pt, ptk = T["ptr"].next()
            for kc in range(8):
                P.pe(lambda e, kc=kc, pt=pt, lb=lb: e.transpose(out=pt[:, kc, :], in_=gz[:, lb, kc * 128:(kc + 1) * 128],
                                                              identity=T["ident"][:]), reads=["gz", "ident"], writes=[ptk])
            g, gk = gT.next()
            P.act(lambda e, g=g, pt=pt: e.copy(out=g[:], in_=pt[:]), reads=[ptk], writes=[gk])
            gcur[lb] = g
            return [gk]

        def post_c(lb, hn, hnk):
            rmsnorm_to_xnT(P, nc, hn, hnk, gb, xn1T, lb, T, T["ident"])
            if lb == 8:
                issue_w(0)

        epi_run(P, NLB, pre_c, (lambda lb: (lambda kc: gcur[lb][:, kc, :])), T, W,
                (lambda lb: xv[lb]), (lambda lb: pv[lb]), (lambda lb: hv[lb]), post_c)
        allx = [("xnT", lb) for lb in range(NLB)]
        pm = T["pm"]
        ev = 0
        for ci, (kind, i) in enumerate(chunks):
            if ci + 1 < len(chunks):
                issue_w(ci + 1)
            wb, wbk = wcur[ci]
            if kind in ("q", "k", "z"):
                g = i // 2 if kind != "z" else 0
                dstT = {"q": q1T, "k": k1T, "z": zs1T}[kind]
                R = 1
                for sl in range(4):
                    for a in range(4):
                        ps, psk = pm.next()
                        for kc in range(8):
                            P.pe(lambda e, ps=ps, kc=kc, sl=sl, a=a, wb=wb: e.matmul(
                                ps[:], lhsT=wb[:, kc, sl * 128:(sl + 1) * 128], rhs=xn1T[:, kc, a * 512:(a + 1) * 512],
                                start=(kc == 0), stop=(kc == 7)), reads=[wbk] + allx[a * 4:a * 4 + 4], writes=[psk])
                        sbt, sbk = qk.next()
                        if R == 1:
                            o_ap, i_ap = sbt[:], ps[:]
                        else:
                            o_ap = sbt[:].rearrange("d (r i) -> d r i", r=R)
                            i_ap = ps[:].rearrange("d (i r) -> d r i", r=R)
                        if kind == "z":
                            P.act(lambda e, o_ap=o_ap, i_ap=i_ap: e.activation(out=o_ap, in_=i_ap, func=AF.Silu), reads=[psk], writes=[sbk])
                        elif ev % 2 == 0:
                            P.act(lambda e, o_ap=o_ap, i_ap=i_ap: e.copy(out=o_ap, in_=i_ap), reads=[psk], writes=[sbk])
                        else:
                            P.dve(lambda e, o_ap=o_ap, i_ap=i_ap: e.tensor_copy(out=o_ap, in_=i_ap), reads=[psk], writes=[sbk])
                        ev += 1
                        r0 = i * 512 + sl * 128
                        if kind == "k":
                            rk = (i % 2) * 512 + sl * 128
                            P.dma("sp", "st" + sbk, k1T[g, a][rk:rk + 128, :], sbt[:], reads=[sbk], writes=[P.mark("C:k")])
                        else:
                            P.dma("sp", "st" + sbk, dstT[r0:r0 + 128, a * 512:(a + 1) * 512], sbt[:], reads=[sbk], writes=[P.mark("C:" + kind)])
                        if kind == "k" and g == 0:
                            P.dma("sp", "st" + sbk, k0tail[r0:r0 + 128, a, :], sbt[:, 384:512], reads=[sbk], writes=[P.mark("C:kt")])
            else:
                g = i // 2
                hh = i % 2
                for a in range(4):
                    for b4 in range(4):
                        ps, psk = pm.next()
                        for kc in range(8):
                            if g == 0:
                                lt = xn1T[:, kc, a * 512 + b4 * 128:a * 512 + b4 * 128 + 128]
                            else:
                                lt = xn1T[:, kc, a * 512:(a + 1) * 512].rearrange("k (i r) -> k r i", r=4)[:, b4, :]
                            P.pe(lambda e, ps=ps, kc=kc, lt=lt, wb=wb: e.matmul(ps[:], lhsT=lt, rhs=wb[:, kc, :],
                                                                             start=(kc == 0), stop=(kc == 7)),
                                 reads=[wbk] + allx[a * 4:a * 4 + 4], writes=[psk])
                        sbt, sbk = qk.next()
                        if ev % 2 == 0:
                            P.act(lambda e, ps=ps, sbt=sbt: e.copy(out=sbt[:], in_=ps[:]), reads=[psk], writes=[sbk])
                        else:
                            P.dve(lambda e, ps=ps, sbt=sbt: e.tensor_copy(out=sbt[:], in_=ps[:]), reads=[psk], writes=[sbk])
                        ev += 1
                        P.dma("sp", "st" + sbk, v1[g, a][:, b4, hh * 512:(hh + 1) * 512], sbt[:], reads=[sbk], writes=[P.mark("C:v")])
                        if g == 0 and b4 == 3:
                            P.dma("sp", "st" + sbk, v0tail[:, a, hh * 512:(hh + 1) * 512], sbt[:], reads=[sbk], writes=[P.mark("C:vt")])
            if gather is not None and (kind, i) == ("k", 5):
                gather("k1T", P.marks["C:k"], (4096, 12288))
                gather("k0tail", P.marks["C:kt"])
            if gather is not None and (kind, i) == ("v", 5):
                gather("v1", P.marks["C:v"], (512, 1536))
                gather("v0tail", P.marks["C:vt"])
        P.flush()


def build_B3C_test():
    nc = bass.Bass("TRN2", target_bir_lowering=False)
    io = {}
    ei = lambda n, shp, dt: nc.dram_tensor(n, shp, dt, kind="ExternalInput").ap()
    eo = lambda n, shp, dt: nc.dram_tensor(n, shp, dt, kind="ExternalOutput").ap()
    io["x"] = ei("x", [NTL, D], F32)
    io["p0"] = ei("p0", [NTL, 256], F32)
    io["fox_w_out"] = ei("fox_w_out", [D, D], F32)
    io["ple_w_up0"] = ei("ple_w_up0", [256, D], F32)
    io["ple_w_gate0"] = ei("ple_w_gate0", [D, D], F32)
    io["dil_norm"] = ei("dil_norm", [D], F32)
    io["dil_w_in"] = ei("dil_w_in", [D, 10240], F32)
    gz_d = ei("gz", [128, NLB, D], BF16)
    io["h2"] = eo("h2", [NTL, D], F32)
    io["q1T"] = eo("q1T", [3072, NTL], BF16)
    io["k1T"] = eo("k1T", [3072, NTL], BF16)
    io["v1"] = eo("v1", [3, 128, NLB, D], BF16)
    io["zs1T"] = eo("zs1T", [D, NTL], BF16)
    P = Prog(nc)
    with nc.sbuf_tensor("gz_sb", [128, NLB, D], BF16) as gz:
        P.dma("sp", "gzld", gz[:], gz_d, writes=["gz"])
        phase_B3C(nc, P, io, gz)
    return nc


def consts_D(j):
    i = np.arange(24, dtype=np.float64)
    slopes = (2.0 ** (-8.0 * (i + 1) / 24)).reshape(3, 8)
    k = np.arange(128, dtype=np.float64)[:, None]
    q = np.arange(128, dtype=np.float64)[None, :]
    Et = np.zeros((128, 8, 6, 128), np.float64)
    for h in range(8):
        for g, dil in ((0, 1.0), (1, 4.0)):
            s = slopes[g, h] * dil
            Et[:, h, 2 * g, :] = np.where(k <= q, np.exp(-s * (q - k)), 0.0)
            Et[:, h, 2 * g + 1, :] = np.where(k >= q, np.exp(-s * (128 + q - k)), 0.0)
        s = slopes[2, h] * 16.0
        iq = 32 * j + np.arange(32, dtype=np.float64)[None, :]
        Et[:, h, 4, 0:32] = np.where(k <= iq, np.exp(-s * (iq - k)), 0.0)
        Et[:, h, 5, 0:32] = np.where(k >= iq, np.exp(-s * (128 + iq - k)), 0.0)
    flag = np.ones((128, 4), np.float32)
    if j == 0:
        flag[:, 0] = 0.0
    return Et.astype(np.float32), flag


def phase_D(nc, P, io):
    k1Tg, v1g, k1T, v1, q1T, zs1T = io["k1Tg"], io["v1g"], io["k1T"], io["v1"], io["q1T"], io["zs1T"]
    h2_d, p1, y_d = io["h2"], io["p1"], io["y"]
    w_out, w_up, w_gate, gfin = io["dil_w_out"], io["ple_w_up1"], io["ple_w_gate1"], io["final_norm"]
    Et_d, flag_d, hkT, hv = io["Et"], io["flag"], io["hkT"], io["hv"]
    k0tg, v0tg = io["k0tailg"], io["v0tailg"]
    k1cm = io.get("cm", {}).get("k1T")
    v1cm = io.get("cm", {}).get("v1")
    SC = float(128 ** -0.5)
    pid = nc.partition_id()
    jj = pid % 4
    with ExitStack() as es:
        sb = lambda n, shp, dt: es.enter_context(nc.sbuf_tensor("D_" + n, shp, dt))
        psm = lambda n, shp, dt: es.enter_context(nc.psum_tensor("D_" + n, shp, dt))
        rr = (jj + 3) % 4

        def halo_copy(a):
            ap_ = ((a - 1) + (jj + 3) // 4) if a >= 1 else 0
            if True:
                kb = k1cm.bitcast(BF16)
                koff = ap_ * (4 * D * 512) + rr * (D * 512)
                ksrc = bass.AP(tensor=kb.tensor, offset=koff, ap=[[512, 1024], [1, 512]])
                P.dma("sp", "dhk", hkT[a, 1024:2048, :], ksrc, writes=[("D:hkT", a)])
            else:
                kb = k1cm.bitcast(BF16)
                for c4 in range(4):
                    koff = (c4 * 1024 + rr * 256) * NTL + ap_ * 512
                    ksrc = bass.AP(tensor=kb.tensor, offset=koff, ap=[[NTL, 256], [1, 512]])
                    P.dma("sp", "dhk", hkT[a, 1024 + 256 * c4:1024 + 256 * (c4 + 1), :], ksrc, writes=[("D:hkT", a)])
            ktoff = rr * (D * 512) + ap_ * 128
            ktsrc = bass.AP(tensor=k0tg.tensor, offset=ktoff, ap=[[512, 1024], [1, 128]])
            P.dma("act", "dhkt", hkT[a, 0:1024, 384:512], ktsrc, writes=[("D:hkTt", a)])
            vb = v1cm.bitcast(BF16)
            voff = ap_ * (512 * 4 * D) + rr * (128 * 4 * D)
            vsrc = bass.AP(tensor=vb.tensor, offset=voff, ap=[[4 * D, 128], [1, 4 * D]])
            P.dma("pool", "dhv", hv[a, 1], vsrc, writes=[("D:hv", a)])
            vtoff = rr * (128 * 4 * D) + ap_ * D
            vtsrc = bass.AP(tensor=v0tg.tensor, offset=vtoff, ap=[[4 * D, 128], [1, D]])
            P.dma("pool", "dhv", hv[a, 0][:, 3 * D:4 * D], vtsrc, writes=[("D:hv", a)])

        halo_copy(0)
        T = {}
        T["identf"] = sb("identf", [128, 128], F32)
        T["ident"] = sb("ident", [128, 128], BF16)
        mk = lambda n, shp, dt, k: Rot([(sb(f"{n}{i}", shp, dt), f"D_{n}{i}") for i in range(k)])
        T["xs"] = mk("xs", [128, D], F32, 2)
        T["pb"] = mk("pb", [128, 256], F32, 2)
        T["pbb"] = mk("pbb", [128, 256], BF16, 2)
        T["h1"] = mk("h1", [128, D], F32, 2)
        T["hb"] = mk("hb", [128, D], BF16, 2)
        T["hT"] = mk("hT", [128, 8, 128], BF16, 1)
        T["pT"] = mk("pT", [128, 2, 128], BF16, 1)
        T["gate"] = mk("gate", [128, D], F32, 1)
        T["hn"] = mk("hn", [128, D], F32, 3)
        T["ptr"] = Rot([(psm("ptr0", [128, 8, 128], BF16), "D_ptr0")])
        pS = [(psm(f"pS{i}", [128, 512], F32), f"D_pS{i}") for i in range(3)]
        pN = [(psm(f"pN{i}", [128, 512], F32), f"D_pN{i}") for i in range(2)]
        pD = [(psm(f"pD{i}", [128, 512], F32), f"D_pD{i}") for i in range(2)]
        T["pm"] = Rot(pS)
        sq = sb("sq", [128, D], BF16)
        ss = sb("ss", [128, NLB], F32)
        ms = sb("ms", [128, NLB], F32)
        rstd = sb("rstd", [128, NLB], F32)
        yb = mk("yb", [128, D], F32, 2)
        gfb = sb("gfb", [128, D], F32)
        wout = sb("wout", [128, 8, D], BF16)
        wg = sb("wg", [128, 8, D], BF16)
        wup = sb("wup", [128, 2, D], BF16)
        wst = sb("wst", [128, 8, 256], F32)
        Et = sb("Et", [128, 8, 6, 128], F32)
        flag = sb("flag", [128, 4], F32)
        ones = sb("ones", [128, 128], BF16)
        g1T = Rot([(sb(f"g1T{i}", [128, 8, 512], BF16), f"D_g1T{i}") for i in range(1)])
        exr = Rot([(sb(f"ex{i}", [128, 512], F32), f"D_ex{i}") for i in range(3)])
        ptr_ = Rot([(sb(f"pt{i}", [128, 512], BF16), f"D_pt{i}") for i in range(4)])
        rDr = Rot([(sb(f"rD{i}", [128, 512], F32), f"D_rD{i}") for i in range(1)])
        tNr = Rot([(sb(f"tN{i}", [128, 512], F32), f"D_tN{i}") for i in range(1)])
        bund = []
        for i in range(2):
            bund.append(dict(
                q=(sb(f"bq{i}", [128, 3, 512], BF16), f"D_bq{i}"),
                k01=(sb(f"bk{i}", [128, 2, 512], BF16), f"D_bk{i}"),
                hk=(sb(f"bhk{i}", [128, 2, 512], BF16), f"D_bhk{i}"),
                k2=(sb(f"bk2{i}", [128, 2, 2048], BF16), f"D_bk2{i}"),
                z=(sb(f"bz{i}", [128, 512], BF16), f"D_bz{i}"),
                v01=(sb(f"bv{i}", [128, 2, 4, 128], BF16), f"D_bv{i}"),
                hvh=(sb(f"bhv{i}", [128, 2, 4, 128], BF16), f"D_bhv{i}"),
                v2=(sb(f"bv2{i}", [128, 2, 4, 4, 128], BF16), f"D_bv2{i}"),
            ))
        W = dict(wout=wout, wg=wg, wup=wup)
        make_ident(P, T["identf"], T["ident"])
        P.pool(lambda e: e.memset(ones[:], 1.0), writes=["ones"])
        P.dma("sp", "dc", Et[:], Et_d, writes=["Et"])
        P.dma("sp", "dc", flag[:], flag_d, writes=["flag"])
        P.dma("sp", "dc", gfb[:], gfin.partition_broadcast(128), writes=["gfb"])
        def load_epi_weights():
            for (wsrc, wdst, key, nk) in ((w_out, wout, "wout", 8), (w_gate, wg, "wg", 8), (w_up, wup, "wup", 2)):
                for n in range(4):
                    P.dma("sp", "dwst", wst[:, 0:nk, :], wsrc[:, n * 256:(n + 1) * 256].rearrange("(kc kp) c -> kp kc c", kp=128),
                          writes=["wst"])
                    P.pool(lambda e, wdst=wdst, n=n, nk=nk: e.tensor_copy(out=wdst[:, :, n * 256:(n + 1) * 256], in_=wst[:, 0:nk, :]),
                           reads=["wst"], writes=[key])

        def load_bundle(bi, a, h):
            B = bund[bi]
            qt, qk_ = B["q"]
            hs = slice(h * 128, (h + 1) * 128)
            cs = slice(a * 512, (a + 1) * 512)
            P.dma("sp", "l" + qk_, qt[:], q1T.rearrange("(g r) t -> r g t", g=3)[hs, :, cs], writes=[qk_])
            kt, kk = B["k01"]
            hk_, hkk = B["hk"]
            P.dma("sp", "l" + kk, kt[:], k1T[0:2, a, hs, :].rearrange("g d t -> d g t"), writes=[kk])
            P.dma("sp", "l" + hkk, hk_[:], hkT[a].rearrange("(g r) t -> r g t", g=2)[hs, :, :],
                  reads=[("D:hkT", a), ("D:hkTt", a)], writes=[hkk])
            k2, k2k = B["k2"]
            r0 = 2048 + h * 128
            for sp_, aa in ((0, a - 1), (1, a)):
                if aa < 0:
                    continue
                k2src = k1cm.bitcast(BF16)[4 + aa].rearrange("(r x) t -> x r t", r=4)[h * 128:(h + 1) * 128, :, :]
                P.dma("sp", "l" + k2k, k2[:, sp_, :].rearrange("d (r t) -> d r t", r=4), k2src, writes=[k2k])
            zt, zk = B["z"]
            P.dma("sp", "l" + zk, zt[:], zs1T[h * 128:(h + 1) * 128, a * 512:(a + 1) * 512], writes=[zk])
            vt, vk = B["v01"]
            hvt, hvk = B["hvh"]
            for g in range(2):
                P.dma("sp", "l" + vk, vt[:, g], v1[g, a][:, :, h * 128:(h + 1) * 128], writes=[vk])
                P.dma("sp", "l" + hvk, hvt[:, g], hv[a, g].rearrange("p (b c) -> p b c", b=4)[:, :, h * 128:(h + 1) * 128],
                      reads=[("D:hv", a)], writes=[hvk])
            v2, v2k = B["v2"]
            for sp_, aa in ((0, a - 1), (1, a)):
                if aa < 0:
                    continue
                for r in range(4):
                    for u in range(4):
                        src = v1cm.bitcast(BF16)[4 + aa, r * 128:(r + 1) * 128, :].rearrange(
                            "(i u) (b c) -> u i b c", u=4, b=4)[u][:, :, h * 128:(h + 1) * 128]
                        qq = "act" if (r + u) % 2 == 0 else "sp"
                        P.dma(qq, "l" + v2k + qq, v2[32 * r:32 * r + 32, sp_, u], src, writes=[v2k + qq])
            return B

        s4 = lambda t3, r1: t3.rearrange("p (i r) -> p r i", r=4)[:, r1, :]
        s16 = lambda t3, r2: t3.rearrange("p (i r) -> p r i", r=16)[:, r2, :]

        def pairs_of(B, a, h, ni, g1, g1k):
            qt, qk_ = B["q"]; kt, kk = B["k01"]; hk_, hkk = B["hk"]; k2, k2k = B["k2"]
            vt, vk = B["v01"]; hvt, hvk = B["hvh"]; v2, v2k = B["v2"]
            lst = []
            lst.append(dict(kind=0, nblk=4, w=128, lhs=lambda b: kt[:, 0, b * 128:(b + 1) * 128],
                            rhs=lambda b: qt[:, 0, b * 128:(b + 1) * 128], v=lambda b: vt[:, 0, b, :],
                            out=lambda t, b: t[:, b * 128:(b + 1) * 128], nf=0, sk=[kk, qk_], vkeys=[vk]))
            lst.append(dict(kind=1, nblk=4, w=128,
                            lhs=lambda b: (hk_[:, 0, 384:512] if b == 0 else kt[:, 0, (b - 1) * 128:b * 128]),
                            rhs=lambda b: qt[:, 0, b * 128:(b + 1) * 128],
                            v=lambda b: (hvt[:, 0, 3, :] if b == 0 else vt[:, 0, b - 1, :]),
                            out=lambda t, b: t[:, b * 128:(b + 1) * 128], nf=1, sk=[kk, hkk, qk_], vkeys=[vk, hvk]))
            lst.append(dict(kind=2, nblk=4, w=128, lhs=lambda b: s4(kt[:, 1, :], b), rhs=lambda b: s4(qt[:, 1, :], b),
                            v=lambda b: vt[:, 1, b, :], out=lambda t, b: s4(t[:], b), nf=0, sk=[kk, qk_], vkeys=[vk]))
            lst.append(dict(kind=3, nblk=4, w=128, lhs=lambda b: s4(hk_[:, 1, :], b), rhs=lambda b: s4(qt[:, 1, :], b),
                            v=lambda b: hvt[:, 1, b, :], out=lambda t, b: s4(t[:], b), nf=4, sk=[hkk, qk_], vkeys=[hvk]))
            for sp_ in ((1, 0) if a >= 1 else (1,)):
                lst.append(dict(kind=(4 if sp_ == 1 else 5), nblk=16, w=32, lhs=lambda b, sp_=sp_: s16(k2[:, sp_, :], b),
                                rhs=lambda b: s16(qt[:, 2, :], b), v=lambda b, sp_=sp_: v2[:, sp_, b // 4, b % 4, :],
                                out=lambda t, b: s16(t[:], b), nf=0, sk=[k2k, qk_], vkeys=[v2k + "act", v2k + "sp"]))
            for i, pr in enumerate(lst):
                pr.update(a=a, h=h, ni=ni, g1=g1, g1k=g1k, B=B, first=(i == 0), last=(i == len(lst) - 1))
                yield pr

        def emit_S(pr):
            ps, psk = T["pm"].next()
            pr["ps"], pr["psk"] = ps, psk
            w = pr["w"]
            for b in range(pr["nblk"]):
                P.pe(lambda e, ps=ps, b=b, pr=pr, w=w: e.matmul(ps[:, b * w:(b + 1) * w], lhsT=pr["lhs"](b), rhs=pr["rhs"](b),
                                                             start=True, stop=True), reads=pr["sk"], writes=[psk])

        def emit_rest(pr):
            ps, psk, w, nblk, a, h, ni = pr["ps"], pr["psk"], pr["w"], pr["nblk"], pr["a"], pr["h"], pr["ni"]
            if pr["first"] and ni + 1 < len(order) and ni >= 1:
                load_bundle((ni + 1) % 2, *order[ni + 1])
            pn, pnk = pN[ni % 2]
            pd, pdk = pD[ni % 2]
            ex, exk = exr.next()
            P.act(lambda e, ex=ex, ps=ps: e.activation(out=ex[:], in_=ps[:], func=AF.Exp, scale=SC), reads=[psk], writes=[exk])
            pt, ptk = ptr_.next()
            Eb = Et[:, h, pr["kind"], 0:w]
            nf = pr["nf"]
            if nf == 0:
                P.dve(lambda e, pt=pt, ex=ex: e.tensor_tensor(
                    out=pt[:].rearrange("p (b w) -> p b w", w=w), in0=ex[:].rearrange("p (b w) -> p b w", w=w),
                    in1=Eb.unsqueeze(1).broadcast_to([128, nblk, w]), op=ALU.mult), reads=[exk, "Et"], writes=[ptk])
            else:
                P.dve(lambda e, pt=pt, ex=ex: e.scalar_tensor_tensor(
                    out=pt[:, 0:nf * w].rearrange("p (b w) -> p b w", w=w),
                    in0=ex[:, 0:nf * w].rearrange("p (b w) -> p b w", w=w), scalar=flag[:, a:a + 1],
                    in1=Eb.unsqueeze(1).broadcast_to([128, nf, w]), op0=ALU.mult, op1=ALU.mult),
                    reads=[exk, "Et", "flag"], writes=[ptk])
                if nf < nblk:
                    P.dve(lambda e, pt=pt, ex=ex: e.tensor_tensor(
                        out=pt[:, nf * w:].rearrange("p (b w) -> p b w", w=w),
                        in0=ex[:, nf * w:].rearrange("p (b w) -> p b w", w=w),
                        in1=Eb.unsqueeze(1).broadcast_to([128, nblk - nf, w]), op=ALU.mult),
                        reads=[exk, "Et"], writes=[ptk])
            for b in range(nblk):
                st = pr["first"] and b == 0
                P.pe(lambda e, b=b, pt=pt, st=st, pr=pr: e.matmul(pr["out"](pn, b), lhsT=pr["v"](b), rhs=pt[:, b * w:(b + 1) * w],
                                                                 start=st, stop=False, skip_group_check=True),
                     reads=[ptk] + pr["vkeys"], writes=[pnk])
                P.pe(lambda e, b=b, pt=pt, st=st, pr=pr: e.matmul(pr["out"](pd, b), lhsT=ones[:], rhs=pt[:, b * w:(b + 1) * w],
                                                                 start=st, stop=False, skip_group_check=True),
                     reads=[ptk, "ones"], writes=[pdk])
            if pr["last"]:
                zt, zk = pr["B"]["z"]
                g1, g1k = pr["g1"], pr["g1k"]
                rD, rDk = rDr.next()
                P.dve(lambda e, rD=rD: e.reciprocal(out=rD[:], in_=pd[:]), reads=[pdk], writes=[rDk])
                tN, tNk = tNr.next()
                P.dve(lambda e, tN=tN, rD=rD: e.tensor_tensor(out=tN[:], in0=pn[:], in1=rD[:], op=ALU.mult),
                      reads=[pnk, rDk], writes=[tNk])
                P.pool(lambda e, tN=tN, g1=g1, zt=zt: e.tensor_tensor(out=g1[:, h, :], in0=tN[:], in1=zt[:], op=ALU.mult),
                       reads=[tNk, zk], writes=[g1k])

        hv2 = h2_d.rearrange("(lb p) d -> lb p d", p=128)
        pv = p1.rearrange("(lb p) d -> lb p d", p=128)
        yv = y_d.rearrange("(lb p) d -> lb p d", p=128)
        order = [(a, h) for a in range(4) for h in range(8)]
        DEPTH = 2
        bundles = {0: load_bundle(0, *order[0]), 1: load_bundle(1, *order[1])}
        load_epi_weights()
        pend = []
        for ni, (a, h) in enumerate(order):
            B = bund[ni % 2]
            if h == 0:
                g1, g1k = g1T.next()
                if a + 1 < 4:
                    halo_copy(a + 1)
            for pr in pairs_of(B, a, h, ni, g1, g1k):
                emit_S(pr)
                pend.append(pr)
                if len(pend) > DEPTH:
                    emit_rest(pend.pop(0))
            if h == 7:
                while pend:
                    emit_rest(pend.pop(0))
                def post_d(bl, hn, hnk, a=a):
                    lb = 4 * a + bl
                    P.act(lambda e, hn=hn, lb=lb: e.activation(out=sq[:], in_=hn[:], func=AF.Square, accum_out=ss[:, lb:lb + 1]),
                          reads=[hnk], writes=["sq", ("ss", lb)])
                    P.dve(lambda e, lb=lb: e.tensor_scalar(out=ms[:, lb:lb + 1], in0=ss[:, lb:lb + 1], scalar1=1.0 / D, scalar2=EPS,
                                                          op0=ALU.mult, op1=ALU.add), reads=[("ss", lb)], writes=[("ms", lb)])
                    P.act(lambda e, lb=lb: e.activation(out=ms[:, lb:lb + 1], in_=ms[:, lb:lb + 1], func=AF.Sqrt),
                          reads=[("ms", lb)], writes=[("ms", lb)])
                    P.dve(lambda e, lb=lb: e.reciprocal(out=rstd[:, lb:lb + 1], in_=ms[:, lb:lb + 1]),
                          reads=[("ms", lb)], writes=[("rstd", lb)])
                    y, yk = yb.next()
                    P.dve(lambda e, y=y, hn=hn, lb=lb: e.scalar_tensor_tensor(out=y[:], in0=hn[:], scalar=rstd[:, lb:lb + 1], in1=gfb[:],
                                                                             op0=ALU.mult, op1=ALU.mult),
                          reads=[hnk, ("rstd", lb), "gfb"], writes=[yk])
                    P.dma("sp", "sty", yv[lb], y[:], reads=[yk], writes=["D:y"])

                T["mix_keys"] = [g1k]
                T["pm"] = Rot(pS + pN + pD)
                epi_run(P, 4, None, (lambda bl, g1=g1: (lambda kc: g1[:, kc, bl * 128:(bl + 1) * 128])), T, W,
                        (lambda bl, a=a: hv2[4 * a + bl]), (lambda bl, a=a: pv[4 * a + bl]), None, post_d)
                T["pm"] = Rot(pS)
        P.flush()


def build_D_test():
    nc = bass.Bass("TRN2", target_bir_lowering=False)
    io = {}
    ei = lambda n, shp, dt: nc.dram_tensor(n, shp, dt, kind="ExternalInput").ap()
    eo = lambda n, shp, dt: nc.dram_tensor(n, shp, dt, kind="ExternalOutput").ap()
    it = lambda n, shp, dt: nc.dram_tensor(n, shp, dt).ap()
    io["k1Tg"] = ei("k1Tg", [4, 3072, NTL], BF16)
    io["v1g"] = ei("v1g", [4, 3, 128, NLB, D], BF16)
    io["k1T"] = ei("k1T", [3072, NTL], BF16)
    io["v1"] = ei("v1", [3, 128, NLB, D], BF16)
    io["q1T"] = ei("q1T", [3072, NTL], BF16)
    io["zs1T"] = ei("zs1T", [D, NTL], BF16)
    io["h2"] = ei("h2", [NTL, D], F32)
    io["p1"] = ei("p1", [NTL, 256], F32)
    io["dil_w_out"] = ei("dil_w_out", [D, D], F32)
    io["ple_w_up1"] = ei("ple_w_up1", [256, D], F32)
    io["ple_w_gate1"] = ei("ple_w_gate1", [D, D], F32)
    io["final_norm"] = ei("final_norm", [D], F32)
    io["Et"] = ei("Et", [128, 8, 6, 128], F32)
    io["flag"] = ei("flag", [128, 4], F32)
    io["hkT"] = it("hkT", [4, 2048, 512], BF16)
    io["hv"] = it("hv", [4, 2, 128, 4096], BF16)
    io["y"] = eo("y", [NTL, D], F32)
    P = Prog(nc)
    phase_D(nc, P, io)
    return nc


def _io_common(nc, ei):
    io = {}
    io["x"] = ei("x", [NTL, D], F32)
    io["p0"] = ei("p0", [NTL, 256], F32)
    io["p1"] = ei("p1", [NTL, 256], F32)
    io["fox_norm"] = ei("fox_norm", [D], F32)
    io["fox_w_in"] = ei("fox_w_in", [D, 4112], F32)
    io["fox_b_f"] = ei("fox_b_f", [16], F32)
    io["fox_w_out"] = ei("fox_w_out", [D, D], F32)
    io["dil_norm"] = ei("dil_norm", [D], F32)
    io["dil_w_in"] = ei("dil_w_in", [D, 10240], F32)
    io["dil_w_out"] = ei("dil_w_out", [D, D], F32)
    io["ple_w_up0"] = ei("ple_w_up0", [256, D], F32)
    io["ple_w_up1"] = ei("ple_w_up1", [256, D], F32)
    io["ple_w_gate0"] = ei("ple_w_gate0", [D, D], F32)
    io["ple_w_gate1"] = ei("ple_w_gate1", [D, D], F32)
    io["final_norm"] = ei("final_norm", [D], F32)
    io["maskT"] = ei("maskT", [128, 16, 512], BF16)
    io["tri"] = ei("tri", [128, 128], F32)
    io["ustrip"] = ei("ustrip", [128, 127], F32)
    io["Et"] = ei("Et", [128, 8, 6, 128], F32)
    io["flag"] = ei("flag", [128, 4], F32)
    return io


LOCAL0 = [("qT0", [1024, NTL], BF16), ("kT0", [1024, NTL], BF16), ("v0", [4, 128, 4, D], BF16),
          ("zs0", [128, NLB, D], BF16), ("lf0", [128, NLB, 16], F32)]
LOCAL1 = [("h2", [NTL, D], F32), ("q1T", [3072, NTL], BF16), ("k1T", [3, 4, D, 512], BF16),
          ("v1", [3, 4, 128, 4, D], BF16), ("zs1T", [D, NTL], BF16),
          ("k0tail", [D, 4, 128], BF16), ("v0tail", [128, 4, D], BF16)]
GATH0 = [("kT0g", [4, 1024, NTL], BF16), ("v0g", [4, 4, 128, 4, D], BF16), ("lf0g", [4, 128, NLB, 16], F32)]
GATH1 = [("k1Tg", [4, 3, 4, D, 512], BF16), ("v1g", [4, 3, 4, 128, 4, D], BF16),
         ("k0tailg", [4, D, 4, 128], BF16), ("v0tailg", [4, 128, 4, D], BF16)]
SCR = [("caug", [16, 6, S], BF16), ("caug_own", [16, 3, NTL], BF16), ("hkT", [4, 2048, 512], BF16),
       ("hv", [4, 2, 128, 4096], BF16)]


def build_stage(stage):
    nc = bass.Bass("TRN2", target_bir_lowering=False)
    used = {}

    def ei(n, shp, dt):
        return nc.dram_tensor(n, shp, dt, kind="ExternalInput").ap()

    eo = lambda n, shp, dt: nc.dram_tensor(n, shp, dt, kind="ExternalOutput").ap()
    it = lambda n, shp, dt: nc.dram_tensor(n, shp, dt).ap()
    need = {1: ["x", "fox_norm", "fox_w_in", "fox_b_f"],
            2: ["x", "p0", "fox_w_out", "ple_w_up0", "ple_w_gate0", "dil_norm", "dil_w_in", "maskT", "tri", "ustrip"],
            3: ["p1", "dil_w_out", "ple_w_up1", "ple_w_gate1", "final_norm", "Et", "flag"]}[stage]
    shapes = {}
    _io_common(None, lambda n, shp, dt: shapes.setdefault(n, (shp, dt)))
    io = {n: ei(n, *shapes[n]) for n in need}
    P = Prog(nc)
    if stage == 1:
        for n, shp, dt in LOCAL0:
            io[n] = eo(n, shp, dt)
        phase_A(nc, P, io)
    elif stage == 2:
        for n, shp, dt in GATH0:
            io[n] = ei(n, shp, dt)
        for n in ("qT0", "zs0"):
            io[n] = ei(n, *[(s_, d_) for (m_, s_, d_) in LOCAL0 if m_ == n][0])
        for n, shp, dt in SCR[:2]:
            io[n] = it(n, shp, dt)
        for n, shp, dt in LOCAL1:
            io[n] = eo(n, shp, dt)
        with ExitStack() as es2:
            gz = es2.enter_context(nc.sbuf_tensor("gz_sb", [128, NLB, D], BF16))
            Wpre = dict(wout=es2.enter_context(nc.sbuf_tensor("W0_out", [128, 8, D], BF16)),
                        wg=es2.enter_context(nc.sbuf_tensor("W0_g", [128, 8, D], BF16)),
                        wup=es2.enter_context(nc.sbuf_tensor("W0_up", [128, 2, D], BF16)))
            phase_B12(nc, P, io, gz, Wpre)
            phase_B3C(nc, P, io, gz, Wpre)
    else:
        for n, shp, dt in GATH1:
            io[n] = ei(n, shp, dt)
        for n in ("k1T", "v1", "q1T", "zs1T", "h2"):
            io[n] = ei(n, *[(s_, d_) for (m_, s_, d_) in LOCAL1 if m_ == n][0])
        for n, shp, dt in SCR[2:]:
            io[n] = it(n, shp, dt)
        io["y"] = eo("y", [NTL, D], F32)
        phase_D(nc, P, io)
    return nc, need


def own_tokens(j):
    return np.concatenate([np.arange(512 * (4 * a + j), 512 * (4 * a + j) + 512) for a in range(4)])


def host_inputs(inputs):
    x = np.asarray(inputs["x"], np.float32)
    p = np.asarray(inputs["p"], np.float32)
    maps = []
    for c in range(NCORES):
        b, j = c // 4, c % 4
        own = own_tokens(j)
        maskT, tri, us = consts_B(j)
        Et, flag = consts_D(j)
        m = {
            "x": np.ascontiguousarray(x[b][own]), "p0": np.ascontiguousarray(p[0, b][own]),
            "p1": np.ascontiguousarray(p[1, b][own]),
            "fox_norm": np.asarray(inputs["fox_norm"], np.float32)[0], "fox_w_in": np.asarray(inputs["fox_w_in"], np.float32)[0],
            "fox_b_f": np.asarray(inputs["fox_b_f"], np.float32)[0], "fox_w_out": np.asarray(inputs["fox_w_out"], np.float32)[0],
            "dil_norm": np.asarray(inputs["dil_norm"], np.float32)[0], "dil_w_in": np.asarray(inputs["dil_w_in"], np.float32)[0],
            "dil_w_out": np.asarray(inputs["dil_w_out"], np.float32)[0],
            "ple_w_up0": np.asarray(inputs["ple_w_up"], np.float32)[0], "ple_w_up1": np.asarray(inputs["ple_w_up"], np.float32)[1],
            "ple_w_gate0": np.asarray(inputs["ple_w_gate"], np.float32)[0], "ple_w_gate1": np.asarray(inputs["ple_w_gate"], np.float32)[1],
            "final_norm": np.asarray(inputs["final_norm"], np.float32),
            "maskT": maskT, "tri": tri, "ustrip": us, "Et": Et, "flag": flag,
        }
        maps.append(m)
    return maps


def assemble_output(ys):
    out = np.zeros((2, S, D), np.float32)
    for c in range(NCORES):
        b, j = c // 4, c % 4
        out[b, own_tokens(j)] = ys[c]
    return out


def kernel_unfused(**inputs):
    maps = host_inputs(inputs)
    cores = list(range(NCORES))
    nc1, need1 = build_stage(1)
    r1 = run_bass_kernel_spmd(nc1, [{k: m[k] for k in need1} for m in maps], core_ids=cores).results
    nc2, need2 = build_stage(2)
    in2 = []
    for c in range(NCORES):
        b = c // 4
        d2 = {k: maps[c][k] for k in need2}
        d2["kT0g"] = np.stack([r1[4 * b + r]["kT0"] for r in range(4)])
        d2["v0g"] = np.stack([r1[4 * b + r]["v0"] for r in range(4)])
        d2["lf0g"] = np.stack([r1[4 * b + r]["lf0"] for r in range(4)])
        d2["qT0"] = r1[c]["qT0"]
        d2["zs0"] = r1[c]["zs0"]
        in2.append(d2)
    r2 = run_bass_kernel_spmd(nc2, in2, core_ids=cores).results
    nc3, need3 = build_stage(3)
    in3 = []
    for c in range(NCORES):
        b = c // 4
        d3 = {k: maps[c][k] for k in need3}
        d3["k1Tg"] = np.stack([r2[4 * b + r]["k1T"] for r in range(4)])
        d3["v1g"] = np.stack([r2[4 * b + r]["v1"] for r in range(4)])
        d3["k0tailg"] = np.stack([r2[4 * b + r]["k0tail"] for r in range(4)])
        d3["v0tailg"] = np.stack([r2[4 * b + r]["v0tail"] for r in range(4)])
        for k in ("k1T", "v1", "q1T", "zs1T", "h2"):
            d3[k] = r2[c][k]
        in3.append(d3)
    r3 = run_bass_kernel_spmd(nc3, in3, core_ids=cores).results
    return assemble_output([r3[c]["y"] for c in range(NCORES)])


RG = [[0, 1, 2, 3], [4, 5, 6, 7]]


def _flat2(ap):
    n = len(ap.shape)
    if n == 2:
        return ap
    return ap.rearrange({3: "a b c -> (a b) c", 4: "a b c d -> (a b) (c d)", 5: "a b c d e -> (a b c) (d e)",
                         6: "a b c d e f -> (a b c d) (e f)"}[n])


CC_CHUNK_BYTES = 1 << 20
CC_INFLIGHT = 4


def gather_tensor(P, nc, loc, gat, tag, dep_keys, q_scatter, rng=None, cm_out=None):
    g2 = _flat2(gat)
    l2 = _flat2(loc) if len(loc.shape) != 3 else loc.rearrange("a b c -> a (b c)")
    if l2.dtype == BF16:
        l2, g2 = l2.bitcast(F32), g2.bitcast(F32)
    g3 = g2.rearrange("(r a) c -> r a c", r=4)
    if rng is not None:
        l2 = l2[rng[0]:rng[1], :]
        g3 = g3[:, rng[0]:rng[1], :]
    rows, cols = l2.shape
    n = max(1, min(rows, CC_CHUNK_BYTES // (cols * 4)))
    assert rows % n == 0
    nch = rows // n
    keys = []
    if nch == 1 and rng is None:
        P.op("pool", (lambda e: e.collective_compute("AllGather", ALU.bypass, replica_groups=RG,
                                                     ins=[l2.opt()], outs=[g2.opt()])),
             reads=list(dep_keys), writes=[("D:gath_" + tag, 0)], dma="cc", inc=1)
        P.pool(lambda e: e.engine_nop(), reads=[("D:gath_" + tag, 0)], writes=["cc_nop"])
        return None
    tmp = nc.dram_tensor("cc_tmp_" + tag, [nch, 4 * n, cols], F32).ap()
    if cm_out is not None:
        cm_out[tag] = tmp
        for c in range(nch):
            key = ("D:gath_" + tag, c)
            src = l2[c * n:(c + 1) * n, :]
            dst = tmp[c]
            P.op("pool", (lambda e, src=src, dst=dst: e.collective_compute("AllGather", ALU.bypass, replica_groups=RG,
                                                                           ins=[src.opt()], outs=[dst.opt()])),
                 reads=list(dep_keys), writes=[key], dma="cc", inc=1)
            keys.append(key)
            if c - CC_INFLIGHT + 1 >= 0:
                P.pool(lambda e: e.engine_nop(), reads=[keys[c - CC_INFLIGHT + 1]], writes=["cc_nop"])
        for c in range(max(0, nch - CC_INFLIGHT + 1), nch):
            P.pool(lambda e: e.engine_nop(), reads=[keys[c]], writes=["cc_nop"])
        return tmp

    def scatter(c):
        P.dma(q_scatter, "ccs_" + tag, g3[:, c * n:(c + 1) * n, :], tmp[c].rearrange("(r a) c -> r a c", r=4),
              reads=[keys[c]], writes=["D:gath_" + tag])
        if q_scatter != "pool":
            P.pool(lambda e: e.engine_nop(), reads=[keys[c]], writes=["cc_nop"])

    for c in range(nch):
        key = "D:cc_%s_%d" % (tag, c)
        src = l2[c * n:(c + 1) * n, :]
        dst = tmp[c]
        P.op("pool", (lambda e, src=src, dst=dst: e.collective_compute("AllGather", ALU.bypass, replica_groups=RG,
                                                                       ins=[src.opt()], outs=[dst.opt()])),
             reads=list(dep_keys), writes=[key], dma="cc", inc=1)
        keys.append(key)
        if c - CC_INFLIGHT + 1 >= 0:
            scatter(c - CC_INFLIGHT + 1)
    for c in range(max(0, nch - CC_INFLIGHT + 1), nch):
        scatter(c)


def all_gather(P, nc, pairs, dummy, mode="ag", tag=""):
    for i, (loc, gat, stg) in enumerate(pairs):
        gather_tensor(P, nc, loc, gat, "%s%d" % (tag, i), (), "sp")
    P.flush()


def build_fused(mode="ag"):
    nc = bass.Bass("TRN2", target_bir_lowering=False)
    ei = lambda n, shp, dt: nc.dram_tensor(n, shp, dt, kind="ExternalInput").ap()
    it = lambda n, shp, dt: nc.dram_tensor(n, shp, dt).ap()
    io = _io_common(nc, ei)
    for n, shp, dt in LOCAL0 + LOCAL1 + GATH0 + GATH1 + SCR:
        io[n] = it(n, shp, dt)
    io["y"] = nc.dram_tensor("y", [NTL, D], F32, kind="ExternalOutput").ap()
    P = Prog(nc)
    g0 = [("kT0", "kT0g"), ("v0", "v0g"), ("lf0", "lf0g")]
    g1 = [("k1T", "k1Tg"), ("v1", "v1g")]
    stg = {}
    with ExitStack() as es:
        dummy = es.enter_context(nc.sbuf_tensor("cc_dummy", [128, 8], F32))
        if mode == "ar":
            pid = nc.partition_id()
            jj = pid % 4
            zt = es.enter_context(nc.sbuf_tensor("cc_zero", [128, 8192], BF16))
            P.pool(lambda e: e.memset(zt[:], 0.0), writes=["zt"])
            for (ln, gn) in g0 + g1:
                shp = list(io[gn].shape)
                dt = F32 if ln == "lf0" else BF16
                stg[ln] = it("stg_" + ln, shp, dt)
                flat = _flat2(stg[ln])
                rows, cols = flat.shape
                if dt == F32:
                    zsrc = zt[:].bitcast(F32)[:, 0:cols]
                    for r0 in range(0, rows, 128):
                        P.dma("pool", "zf", flat[r0:r0 + 128, :], zsrc, reads=["zt"], writes=["D:stg" + ln])
                elif cols > 8192:
                    for r0 in range(0, rows, 128):
                        for c0 in range(0, cols, 8192):
                            P.dma("pool", "zf", flat[r0:r0 + 128, c0:c0 + 8192], zt[:], reads=["zt"], writes=["D:stg" + ln])
                else:
                    per = max(1, 8192 // cols)
                    for r0 in range(0, rows, 128 * per):
                        nb = min(per, (rows - r0) // 128)
                        P.dma("pool", "zf", flat[r0:r0 + 128 * nb, :].rearrange("(n p) c -> p n c", p=128),
                              zt[:, 0:nb * cols].rearrange("p (n c) -> p n c", c=cols), reads=["zt"], writes=["D:stg" + ln])

        def exchange(names):
            if mode == "ar":
                for (ln, gn) in names:
                    loc = io[ln]
                    n_el = 1
                    for d_ in loc.shape:
                        n_el *= d_
                    l2 = _flat2(loc) if len(loc.shape) != 3 else loc.rearrange("a b c -> a (b c)")
                    dst = bass.AP(tensor=stg[ln].tensor, offset=jj * n_el, ap=[[l2.shape[1], l2.shape[0]], [1, l2.shape[1]]])
                    P.dma("sp", "sx", dst, l2, writes=["D:stgw" + ln])
                P.flush()
            all_gather(P, nc, [(io[ln], io[gn], stg.get(ln)) for (ln, gn) in names], dummy, mode, tag=names[0][0])

        OVERLAP = (mode == "ag")
        gmap = dict(g0 + g1 + [("k0tail", "k0tailg"), ("v0tail", "v0tailg")])

        cm = {}
        io["cm"] = cm

        def gather_cb(ln, dep_keys, rng=None):
            loc_, gat_ = io[ln], io[gmap[ln]]
            if ln == "k1T":
                loc_ = loc_.rearrange("g a x t -> (g a x) t")
                gat_ = gat_.rearrange("r g a x t -> (r g a x) t")
            gather_tensor(P, nc, loc_, gat_, ln, dep_keys, "pool", rng,
                          cm_out=(cm if ln in ("kT0", "k1T", "v1", "v0") else None))

        phase_A(nc, P, io, gather_cb if OVERLAP else None)
        if not OVERLAP:
            exchange(g0)
        with ExitStack() as es2:
            gz = es2.enter_context(nc.sbuf_tensor("gz_sb", [128, NLB, D], BF16))
            Wpre = dict(wout=es2.enter_context(nc.sbuf_tensor("W0_out", [128, 8, D], BF16)),
                        wg=es2.enter_context(nc.sbuf_tensor("W0_g", [128, 8, D], BF16)),
                        wup=es2.enter_context(nc.sbuf_tensor("W0_up", [128, 2, D], BF16)))
            phase_B12(nc, P, io, gz, Wpre)
            phase_B3C(nc, P, io, gz, Wpre, gather_cb if OVERLAP else None)
        if not OVERLAP:
            exchange(g1)
        phase_D(nc, P, io)
    return nc, list(_io_common(None, lambda n, shp, dt: None).keys())


FUSED_MODE = "ag"
USE_FUSED = True


def kernel_fused(**inputs):
    maps = host_inputs(inputs)
    nc, need = build_fused(FUSED_MODE)
    res = run_bass_kernel_spmd(nc, [{k: m[k] for k in need} for m in maps], core_ids=list(range(NCORES))).results
    return assemble_output([res[c]["y"] for c in range(NCORES)])


def kernel(**inputs):
    return (kernel_fused if USE_FUSED else kernel_unfused)(**inputs)
```
